# Optimizing a Trainium2 kernel written in Bass

```python
import math
import jax, jax.numpy as jnp
from jax import lax
import numpy as np

D_MODEL = 1024
BATCH = 8
SEQ = 8192
DEPTH = 1

D_MIX = D_MODEL
ATT_WIDTH = D_MIX // 2
SSM_WIDTH = D_MIX - ATT_WIDTH
ATT_HEADS = 4
ATT_QK_DIM = 64
ATT_V_DIM = ATT_WIDTH // ATT_HEADS
QK_COLS = ATT_HEADS * 2 * ATT_QK_DIM
SSM_GROUP = 16
SSM_GROUPS = SSM_WIDTH // SSM_GROUP
SSM_STATE = 64
D_FF = int(math.ceil(8 * D_MODEL / 3 / 256)) * 256
REL_BUCKETS = 32
REL_MAX_DIST = 128
Q_BLOCK = 128
EPS = 1e-6
IN_COLS = 2 * QK_COLS + ATT_WIDTH + SSM_WIDTH

kernel_name = "hybrid_diffattn_s5_parallel_heads"


def rms_norm(x, g):
    xf = x.astype(jnp.float32)
    y = xf * lax.rsqrt(jnp.mean(xf * xf, axis=-1, keepdims=True) + EPS)
    return (y * g.astype(jnp.float32)).astype(x.dtype)


def t5_bucket(n):
    max_exact = REL_BUCKETS // 2
    nf = jnp.maximum(n, 1).astype(jnp.float32)
    large = max_exact + (jnp.log(nf / max_exact) / math.log(REL_MAX_DIST / max_exact)
                         * (REL_BUCKETS - max_exact)).astype(jnp.int32)
    large = jnp.minimum(large, REL_BUCKETS - 1)
    return jnp.where(n < max_exact, n, large)


def diff_attention(q, k, v, lam, rel_bias):
    bsz, seq = q.shape[0], q.shape[1]
    nblk = seq // Q_BLOCK
    scale = ATT_QK_DIM ** -0.5
    k_pos = jnp.arange(seq)
    table = rel_bias.astype(jnp.float32)

    def block(i):
        q0 = i * Q_BLOCK
        qb = lax.dynamic_slice_in_dim(q, q0, Q_BLOCK, axis=1)
        logits = jnp.einsum('bqhcd,bkhcd->bhcqk', qb, k).astype(jnp.float32) * scale
        dist = (q0 + jnp.arange(Q_BLOCK))[:, None] - k_pos[None, :]
        bias = jnp.transpose(table[t5_bucket(jnp.maximum(dist, 0))], (2, 0, 1))
        logits = logits + bias[None, :, None]
        logits = jnp.where((dist >= 0)[None, None, None], logits, -jnp.inf)
        p = jax.nn.softmax(logits, axis=-1)
        w = p[:, :, 0] - lam * p[:, :, 1]
        return jnp.einsum('bhqk,bkhd->bqhd', w.astype(v.dtype), v)

    out = lax.map(block, jnp.arange(nblk))
    return jnp.moveaxis(out, 0, 1).reshape(bsz, seq, ATT_HEADS, ATT_V_DIM)


def s5_ssm(u, A_re, A_im, log_dt, B_re, B_im, C_re, C_im, D_skip):
    bsz, seq, _ = u.shape
    f32 = jnp.float32
    ug = u.astype(f32).reshape(bsz, seq, SSM_GROUPS, SSM_GROUP)
    lam = lax.complex(A_re.astype(f32), A_im.astype(f32))
    dt = jnp.exp(log_dt.astype(f32))[:, None]
    lam_bar = jnp.exp(lam * dt)
    b_bar = ((lam_bar - 1.0) / lam)[:, :, None] * lax.complex(B_re.astype(f32), B_im.astype(f32))
    bu = jnp.einsum('gnc,bsgc->bsgn', b_bar, ug.astype(jnp.complex64))
    a = jnp.broadcast_to(lam_bar, bu.shape)

    def combine(left, right):
        a1, b1 = left
        a2, b2 = right
        return a2 * a1, a2 * b1 + b2

    _, xs = lax.associative_scan(combine, (a, bu), axis=1)
    c = lax.complex(C_re.astype(f32), C_im.astype(f32))
    y = jnp.einsum('gcn,bsgn->bsgc', c, xs).real + D_skip.astype(f32) * ug
    return y.reshape(bsz, seq, SSM_WIDTH).astype(u.dtype)


def setup_inputs(seed: int = 0) -> dict:
    key = jax.random.key(seed)
    ks = jax.random.split(key, 32)
    f32 = jnp.float32
    L = DEPTH

    def nrm(k, shape, scale):
        return jax.random.normal(k, shape, f32) * scale

    def gain(k, shape):
        return 1.0 + 0.05 * jax.random.normal(k, shape, f32)

    n_idx = jnp.arange(SSM_STATE, dtype=f32)
    A_re = -0.5 + 0.01 * jax.random.normal(ks[9], (L, SSM_GROUPS, SSM_STATE), f32)
    A_im = math.pi * n_idx[None, None, :] + 0.01 * jax.random.normal(ks[10], (L, SSM_GROUPS, SSM_STATE), f32)
    log_dt = jax.random.uniform(ks[11], (L, SSM_GROUPS), f32, math.log(1e-3), math.log(1e-1))
    return {
        "x": jax.random.normal(ks[0], (BATCH, SEQ, D_MODEL), f32),
        "norm_mix_g": gain(ks[1], (L, D_MODEL)),
        "w_in": nrm(ks[2], (L, D_MODEL, IN_COLS), D_MODEL ** -0.5),
        "lambda_q1": nrm(ks[3], (L, ATT_QK_DIM), 0.1),
        "lambda_k1": nrm(ks[4], (L, ATT_QK_DIM), 0.1),
        "lambda_q2": nrm(ks[5], (L, ATT_QK_DIM), 0.1),
        "lambda_k2": nrm(ks[6], (L, ATT_QK_DIM), 0.1),
        "subln_g": gain(ks[7], (L, ATT_V_DIM)),
        "rel_bias": nrm(ks[8], (REL_BUCKETS, ATT_HEADS), 0.5),
        "A_re": A_re,
        "A_im": A_im,
        "log_dt": log_dt,
        "B_re": nrm(ks[12], (L, SSM_GROUPS, SSM_STATE, SSM_GROUP), (2 * SSM_GROUP) ** -0.5),
        "B_im": nrm(ks[13], (L, SSM_GROUPS, SSM_STATE, SSM_GROUP), (2 * SSM_GROUP) ** -0.5),
        "C_re": nrm(ks[14], (L, SSM_GROUPS, SSM_GROUP, SSM_STATE), (2 * SSM_STATE) ** -0.5),
        "C_im": nrm(ks[15], (L, SSM_GROUPS, SSM_GROUP, SSM_STATE), (2 * SSM_STATE) ** -0.5),
        "D_skip": nrm(ks[16], (L, SSM_GROUPS, SSM_GROUP), 1.0),
        "w_glu": nrm(ks[17], (L, SSM_WIDTH, SSM_WIDTH), SSM_WIDTH ** -0.5),
        "b_glu": nrm(ks[18], (L, SSM_WIDTH), 0.01),
        "ssm_norm_g": gain(ks[19], (L, SSM_WIDTH)),
        "w_out": nrm(ks[20], (L, D_MIX, D_MODEL), D_MIX ** -0.5),
        "norm_ffn_g": gain(ks[21], (L, D_MODEL)),
        "w_gate": nrm(ks[22], (L, D_MODEL, D_FF), D_MODEL ** -0.5),
        "w_up": nrm(ks[23], (L, D_MODEL, D_FF), D_MODEL ** -0.5),
        "w_down": nrm(ks[24], (L, D_FF, D_MODEL), D_FF ** -0.5),
        "norm_final_g": gain(ks[25], (D_MODEL,)),
    }


def reference(x, norm_mix_g, w_in, lambda_q1, lambda_k1, lambda_q2, lambda_k2, subln_g,
              rel_bias, A_re, A_im, log_dt, B_re, B_im, C_re, C_im, D_skip, w_glu, b_glu,
              ssm_norm_g, w_out, norm_ffn_g, w_gate, w_up, w_down, norm_final_g):
    bsz, seq, _ = x.shape
    f32 = jnp.float32
    for l in range(DEPTH):
        h = rms_norm(x, norm_mix_g[l])
        proj = h @ w_in[l]
        q, k, v, u = jnp.split(proj, [QK_COLS, 2 * QK_COLS, 2 * QK_COLS + ATT_WIDTH], axis=-1)
        q = q.reshape(bsz, seq, ATT_HEADS, 2, ATT_QK_DIM)
        k = k.reshape(bsz, seq, ATT_HEADS, 2, ATT_QK_DIM)
        v = v.reshape(bsz, seq, ATT_HEADS, ATT_V_DIM)

        lam_init = 0.8 - 0.6 * math.exp(-0.3 * l)
        lam = (jnp.exp(jnp.sum(lambda_q1[l].astype(f32) * lambda_k1[l].astype(f32)))
               - jnp.exp(jnp.sum(lambda_q2[l].astype(f32) * lambda_k2[l].astype(f32)))
               + lam_init)
        att = diff_attention(q, k, v, lam, rel_bias)
        att = (rms_norm(att, subln_g[l]) * (1.0 - lam_init)).reshape(bsz, seq, ATT_WIDTH)

        y = s5_ssm(u, A_re[l], A_im[l], log_dt[l], B_re[l], B_im[l], C_re[l], C_im[l], D_skip[l])
        g = jax.nn.gelu(y)
        ssm = g * jax.nn.sigmoid(g @ w_glu[l] + b_glu[l])
        ssm = rms_norm(ssm, ssm_norm_g[l])

        x = x + jnp.concatenate([att, ssm], axis=-1) @ w_out[l]

        h = rms_norm(x, norm_ffn_g[l])
        x = x + (jax.nn.silu(h @ w_gate[l]) * (h @ w_up[l])) @ w_down[l]
    return rms_norm(x, norm_final_g)
```

```python
import math
from contextlib import ExitStack

import numpy as np
import ml_dtypes
import concourse.bass as bass
import concourse.mybir as mybir
from concourse.bass_utils import run_bass_kernel_spmd

F32 = mybir.dt.float32
BF16 = mybir.dt.bfloat16
I32 = mybir.dt.int32
AF = mybir.ActivationFunctionType
ALU = mybir.AluOpType
AX = mybir.AxisListType

S_LEN = 8192
D = 1024
DFF = 2816
NFC = DFF // 128
EPS = 1e-6
TWO_PI_S = 6.2831850051879883
INV_2PI = 1.0 / (2.0 * math.pi)

ENGS = ("sync", "scalar", "vector", "gpsimd", "tensor")
N_DMA_SEMS = 40
N_HW_SEMS = 28


class Sched:
    def __init__(self, nc, stack):
        self.nc = nc
        self.esem = {e: stack.enter_context(nc.semaphore("s_" + e)) for e in ENGS}
        self.ecnt = {e: 0 for e in ENGS}
        self.dsem = [stack.enter_context(nc.semaphore("d%d" % i)) for i in range(N_DMA_SEMS)]
        self.dcnt = [0] * N_DMA_SEMS
        self.dnext = 0
        self.dnext_sw = 0
        self.dlast = [None] * N_DMA_SEMS
        self.waited = {e: {} for e in ENGS}
        self.ops = {e: [] for e in ENGS}
        self.lastw = {}
        self.readers = {}
        self.n_ops = 0

    def _need(self, eng, tok, waits):
        if tok is None:
            return
        semkey, sem, val = tok[0], tok[1], tok[2]
        w = self.waited[eng]
        if w.get(semkey, 0) >= val:
            return
        w[semkey] = val
        waits.append((sem, val))
        for k, v in tok[5].items():
            if w.get(k, 0) < v:
                w[k] = v

    def op(self, eng, fn, reads=(), writes=(), dma=False, accum=False):
        waits = []
        for k in reads:
            self._need(eng, self.lastw.get(k), waits)
        for k in writes:
            lw = self.lastw.get(k)
            if lw is not None and not (accum and eng == "tensor" and lw[3] == "tensor" and lw[4]):
                self._need(eng, lw, waits)
            for r in self.readers.get(k, ()):
                self._need(eng, r, waits)
        if dma:
            if eng == "gpsimd":
                i = N_HW_SEMS + self.dnext_sw
                self.dnext_sw = (self.dnext_sw + 1) % (N_DMA_SEMS - N_HW_SEMS)
            else:
                i = self.dnext
                self.dnext = (self.dnext + 1) % N_HW_SEMS
            self._need(eng, self.dlast[i], waits)
            self.dcnt[i] += 16
            tok = (("d", i), self.dsem[i], self.dcnt[i], "dma", False, dict(self.waited[eng]))
            self.dlast[i] = tok
            inc = (self.dsem[i], 16)
        else:
            self.ecnt[eng] += 1
            know = dict(self.waited[eng])
            know[("e", eng)] = self.ecnt[eng] - 1
            tok = (("e", eng), self.esem[eng], self.ecnt[eng], eng, accum, know)
            inc = (self.esem[eng], 1)
        self.ops[eng].append((waits, fn, inc))
        for k in reads:
            self.readers.setdefault(k, []).append(tok)
        for k in writes:
            self.lastw[k] = tok
            self.readers[k] = []
        self.n_ops += 1
        return tok

    def wait_all(self, eng, toks):
        waits = []
        for t in toks:
            self._need(eng, t, waits)
        if waits:
            self.ops[eng].append((waits, None, None))

    def flush(self):
        nc = self.nc
        self.wait_all("sync", [(("d", i), self.dsem[i], self.dcnt[i], "dma", False, {})
                               for i in range(N_DMA_SEMS) if self.dcnt[i]])
        ops = self.ops
        with nc.Block() as block:
            def mk(e):
                def body(engh):
                    for waits, fn, inc in ops[e]:
                        fuse = None
                        if fn is not None and waits and fn[0] != "dma_start" and "accum_out" not in fn[2]:
                            fuse = waits[-1]
                            waits = waits[:-1]
                        for sem, val in waits:
                            engh.wait_ge(sem, val)
                        if fn is not None:
                            name, a, kw = fn
                            ins = getattr(engh, name)(*a, **kw)
                            if fuse is not None:
                                ins._wait_ge(fuse[0], fuse[1])
                            ins.then_inc(inc[0], inc[1])
                return body
            for e in ENGS:
                if ops[e]:
                    getattr(block, e)(mk(e))
        self.ops = {e: [] for e in ENGS}
        self.lastw = {}
        self.readers = {}


def I(name, *a, **kw):
    return (name, a, kw)


class Ring:
    def __init__(self, n):
        self.n = n
        self.i = -1

    def next(self):
        self.i = (self.i + 1) % self.n
        return self.i


def _t5_bucket_np(n):
    n = np.asarray(n)
    nf = np.maximum(n, 1).astype(np.float32)
    large = 16 + (np.log(nf / np.float32(16)) / np.float32(math.log(8.0)) * np.float32(16)).astype(np.int32)
    large = np.minimum(large, 31)
    return np.where(n < 16, n, large)


def _consts():
    c = {}
    c["ident_bf"] = np.eye(128, dtype=np.float32).astype(ml_dtypes.bfloat16)
    c["ident_f"] = np.eye(128, dtype=np.float32)
    c["jrev"] = np.eye(128, dtype=np.float32)[::-1].copy()
    oh = np.zeros((32, 383), np.float32)
    mv = np.zeros((4, 383), np.float32)
    for i in range(383):
        n = i - 127
        if n >= 0:
            oh[int(_t5_bucket_np(n)), i] = 1.0
            mv[:, i] = 1.0
    c["oh"] = oh
    c["maskvec"] = mv
    sel = np.zeros((128, 64, 128), np.float32)
    for x_ in range(8):
        for y_ in range(8):
            for cc in range(16):
                sel[x_ * 16 + cc, x_ * 8 + y_, y_ * 16 + cc] = 1.0
    c["sel"] = sel.astype(ml_dtypes.bfloat16)
    jj = np.arange(128)[:, None] // 16
    ii = np.arange(128)[None, :] // 16
    c["tmask"] = (ii >= jj).astype(np.float32)
    c["kvec"] = np.broadcast_to(np.arange(1025, dtype=np.float32)[None, :], (128, 1025)).copy()
    sg = np.ones((128, 2), np.float32)
    sg[:64, 0] = -1.0
    sg[64:, 1] = -1.0
    c["sgn"] = sg
    mlist = [-j for j in range(8)] + [7 - j for j in range(8)] + list(range(9))
    c["mvals"] = np.broadcast_to(np.asarray(mlist, np.float32)[None, :], (128, len(mlist))).copy()
    return c


NM_ = 25
_CONST_SPECS = [
    ("ident_bf", [128, 128], BF16), ("ident_f", [128, 128], F32), ("jrev", [128, 128], F32),
    ("oh", [32, 383], F32), ("maskvec", [4, 383], F32), ("sel", [128, 64, 128], BF16),
    ("tmask", [128, 128], F32), ("kvec", [128, 1025], F32), ("sgn", [128, 2], F32), ("mvals", [128, NM_], F32),
]
_PARAM_SPECS = [
    ("x", [S_LEN, D], F32),
    ("w_in", [D, 2048], F32), ("w_glu", [512, 512], F32), ("w_out", [D, D], F32),
    ("w_gate", [D, DFF], F32), ("w_up", [D, DFF], F32), ("w_down", [DFF, D], F32),
    ("g_mix", [128, 8], F32), ("g_ffn", [128, 8], F32), ("g_sub", [128, 1], F32),
    ("g_ssm", [128, 4], F32), ("b_glu", [128, 4], F32), ("g_fin", [128, D], F32),
    ("lamv", [1, 256], F32), ("rel_bias", [32, 4], F32), ("rb31", [4, 1], F32),
    ("areT2", [128, 32], F32), ("aimT2", [128, 32], F32), ("logdt2", [128, 32], F32),
    ("bx2", [128, 512], F32), ("bsw2", [128, 512], F32), ("ca", [128, 512], F32), ("cb", [128, 512], F32),
    ("dvec", [128, 32], F32),
]

M_LIST = [-j for j in range(8)] + [7 - j for j in range(8)] + list(range(9))
NM = len(M_LIST)
IDX_L, IDX_B, IDX_P = 0, 8, 16


def build_nc(debug=False):
    nc = bass.Bass("TRN2", target_bir_lowering=False)
    IK = "ExternalOutput" if debug else "Internal"
    T = {}
    for name, shape, dt_ in _PARAM_SPECS + _CONST_SPECS:
        T[name] = nc.dram_tensor(name, shape, dt_, kind="ExternalInput").ap()
    out_d = nc.dram_tensor("out", [S_LEN, D], F32, kind="ExternalOutput").ap()
    qT_d = nc.dram_tensor("qT_d", [512, S_LEN], BF16, kind=IK).ap()
    kT_d = nc.dram_tensor("kT_d", [512, S_LEN], BF16, kind=IK).ap()
    v_d = nc.dram_tensor("v_d", [S_LEN, 512], BF16, kind=IK).ap()
    uT_d = nc.dram_tensor("uT_d", [512, 8, 1024], BF16, kind=IK).ap()
    gT_d = nc.dram_tensor("gT_d", [512, S_LEN], BF16, kind=IK).ap()
    catT_d = nc.dram_tensor("catT_d", [D, S_LEN], BF16, kind=IK).ap()
    gv_d = nc.dram_tensor("gv_d", [4, 383], F32, kind=IK).ap()
    tab_d = nc.dram_tensor("tab_d", [32, 2, 128, 1025], F32, kind="Internal").ap()
    wout_b = nc.dram_tensor("wout_b", [D, D], BF16, kind="Internal").ap()
    wg_b = nc.dram_tensor("wg_b", [D, DFF], BF16, kind="Internal").ap()
    wu_b = nc.dram_tensor("wu_b", [D, DFF], BF16, kind="Internal").ap()
    wd_b = nc.dram_tensor("wd_b", [DFF, D], BF16, kind="Internal").ap()

    with ExitStack() as top:
        S = Sched(nc, top)
        uq = [0]

        def sb(st, name, shape, dt_):
            uq[0] += 1
            return st.enter_context(nc.sbuf_tensor("sb%d_%s" % (uq[0], name), shape, dt_))

        def ps(st, name, shape, dt_):
            uq[0] += 1
            return st.enter_context(nc.psum_tensor("ps%d_%s" % (uq[0], name), shape, dt_))

        ident_bf = sb(top, "ident_bf", [128, 128], BF16)
        ident_f = sb(top, "ident_f", [128, 128], F32)
        ones_bf = sb(top, "ones_bf", [128, 128], BF16)
        epsb = sb(top, "epsb", [128, 1], F32)
        mid = ExitStack()
        neglam = sb(mid, "neglam", [128, 1], F32)
        E_all = sb(mid, "E_all", [128, 8, 128], BF16)
        W_BsT = sb(mid, "W_BsT", [128, 32, 128], BF16)
        W_BsTs = sb(mid, "W_BsTs", [128, 32, 128], BF16)
        W_CsA = sb(mid, "W_CsA", [128, 32, 128], BF16)
        W_CsB = sb(mid, "W_CsB", [128, 32, 128], BF16)
        W_T = sb(mid, "W_T", [128, 32, 128], BF16)
        rho2 = sb(mid, "rho2", [128, 32], F32)
        tau8 = sb(mid, "tau8", [128, 32], F32)
        mid2 = ExitStack()
        S1t = sb(mid2, "S1t", [128, 32, 33], F32)
        C1t = sb(mid2, "C1t", [128, 32, 33], F32)
        S0t = sb(mid2, "S0t", [128, 32, 32], F32)
        C0t = sb(mid2, "C0t", [128, 32, 32], F32)

        def load(eng, dst, src, key):
            return S.op(eng, I("dma_start", out=dst, in_=src), writes=[key], dma=True)

        with ExitStack() as ph:
            load("sync", ident_bf[:], T["ident_bf"], "ident_bf")
            load("sync", ident_f[:], T["ident_f"], "ident_f")
            S.op("vector", I("memset", ones_bf[:], 1.0), writes=["ones_bf"])
            S.op("vector", I("memset", epsb[:], EPS), writes=["epsb"])
            ones_f = sb(ph, "ones_f", [1, 128], F32)
            S.op("vector", I("memset", ones_f[:], 1.0), writes=["ones_f"])

            lamv = sb(ph, "lamv", [1, 256], F32)
            lprod = sb(ph, "lprod", [1, 128], F32)
            lsum = sb(ph, "lsum", [1, 2], F32)
            lexp = sb(ph, "lexp", [1, 2], F32)
            lval = sb(ph, "lval", [1, 1], F32)
            load("sync", lamv[:], T["lamv"], "lamv")
            lv = lamv[:].rearrange("p (a b c) -> p a b c", a=2, b=2)
            S.op("vector", I("tensor_tensor", out=lprod[:].rearrange("p (a c) -> p a c", a=2),
                                                     in0=lv[:, :, 0, :], in1=lv[:, :, 1, :], op=ALU.mult),
                 reads=["lamv"], writes=["lprod"])
            S.op("vector", I("tensor_reduce", out=lsum[:], in_=lprod[:].rearrange("p (a c) -> p a c", a=2),
                                                     axis=AX.X, op=ALU.add),
                 reads=["lprod"], writes=["lsum"])
            S.op("scalar", I("activation", out=lexp[:], in_=lsum[:], func=AF.Exp), reads=["lsum"], writes=["lexp"])
            S.op("vector", I("scalar_tensor_tensor", out=lval[:], in0=lexp[:, 1:2], scalar=-0.2, in1=lexp[:, 0:1],
                                                            op0=ALU.add, op1=ALU.subtract),
                 reads=["lexp"], writes=["lval"])
            pz = ps(ph, "pz0", [128, 512], F32)
            S.op("tensor", I("matmul", pz[:, 0:1], lhsT=ones_f[:], rhs=lval[:], start=True, stop=True),
                 reads=["ones_f", "lval"], writes=["pz0"])
            S.op("vector", I("tensor_copy", out=neglam[:], in_=pz[:, 0:1]), reads=["pz0"], writes=["neglam"])

            relb = sb(ph, "relb", [32, 4], F32)
            rb31 = sb(ph, "rb31", [4, 1], F32)
            nrb31 = sb(ph, "nrb31", [4, 1], F32)
            oh = sb(ph, "oh", [32, 383], F32)
            mvec = sb(ph, "mvec", [4, 383], F32)
            gvs = sb(ph, "gvs", [4, 383], F32)
            jrev = sb(ph, "jrev", [128, 128], F32)
            erev = sb(ph, "erev", [128, 8, 128], F32)
            load("sync", relb[:], T["rel_bias"], "relb")
            load("sync", rb31[:], T["rb31"], "rb31")
            load("sync", oh[:], T["oh"], "oh")
            load("sync", mvec[:], T["maskvec"], "mvec")
            load("sync", jrev[:], T["jrev"], "jrev")
            S.op("vector", I("tensor_scalar", out=nrb31[:], in0=rb31[:], scalar1=-1.0, scalar2=None, op0=ALU.mult),
                 reads=["rb31"], writes=["nrb31"])
            pz1 = ps(ph, "pz1", [128, 512], F32)
            S.op("tensor", I("matmul", pz1[0:4, 0:383], lhsT=relb[:], rhs=oh[:], start=True, stop=True),
                 reads=["relb", "oh"], writes=["pz1"])
            S.op("scalar", I("activation", out=gvs[:], in_=pz1[0:4, 0:383], func=AF.Exp, bias=nrb31[:, 0:1]),
                 reads=["pz1", "nrb31"], writes=["gvs"])
            S.op("vector", I("tensor_tensor", out=gvs[:], in0=gvs[:], in1=mvec[:], op=ALU.mult),
                 reads=["gvs", "mvec"], writes=["gvs"])
            S.op("gpsimd", I("dma_start", out=gv_d, in_=gvs[:]), reads=["gvs"], writes=["gv_d"], dma=True)
            for hh in range(4):
                for dd in range(2):
                    src = bass.AP(gv_d.tensor, hh * 383 + dd * 128, [[1, 128], [1, 128]])
                    S.op("sync", I("dma_start", out=erev[:, hh * 2 + dd, :], in_=src),
                         reads=["gv_d"], writes=["erev"], dma=True)
            pz2 = ps(ph, "pz2", [128, 512], F32)
            pz3 = ps(ph, "pz3", [128, 512], F32)
            erf = erev[:].rearrange("p a b -> p (a b)")
            eaf = E_all[:].rearrange("p a b -> p (a b)")
            for half, pzz in ((0, pz2), (1, pz3)):
                S.op("tensor", I("matmul", pzz[:, :], lhsT=jrev[:], rhs=erf[:, half * 512:(half + 1) * 512],
                                                                      start=True, stop=True),
                     reads=["jrev", "erev"], writes=["pzz%d" % half])
                S.op("vector", I("tensor_copy", out=eaf[:, half * 512:(half + 1) * 512], in_=pzz[:, :]),
                     reads=["pzz%d" % half], writes=["E_all"])

            areT = sb(ph, "areT", [128, 32], F32)
            aimT = sb(ph, "aimT", [128, 32], F32)
            ldt = sb(ph, "ldt", [128, 32], F32)
            sgn = sb(ph, "sgn", [128, 2], F32)
            tmask = sb(ph, "tmask", [128, 128], F32)
            dvec = sb(ph, "dvec", [128, 32], F32)
            bx2 = sb(ph, "bx2", [128, 32, 16], F32)
            bsw2 = sb(ph, "bsw2", [128, 32, 16], F32)
            ca = sb(ph, "ca", [128, 32, 16], F32)
            cbm = sb(ph, "cbm", [128, 32, 16], F32)
            load("sync", areT[:], T["areT2"], "areT")
            load("sync", aimT[:], T["aimT2"], "aimT")
            load("sync", ldt[:], T["logdt2"], "ldt")
            load("sync", sgn[:], T["sgn"], "sgn")
            load("sync", tmask[:], T["tmask"], "tmask")
            load("sync", dvec[:], T["dvec"], "dvec")
            load("sync", bx2[:].rearrange("p g c -> p (g c)"), T["bx2"], "bx2")
            load("sync", bsw2[:].rearrange("p g c -> p (g c)"), T["bsw2"], "bsw2")
            load("sync", ca[:].rearrange("p g c -> p (g c)"), T["ca"], "ca")
            load("sync", cbm[:].rearrange("p g c -> p (g c)"), T["cb"], "cbm")

            dtt = sb(ph, "dtt", [128, 32], F32)
            ar = sb(ph, "ar", [128, 32], F32)
            tau = sb(ph, "tau", [128, 32], F32)
            den = sb(ph, "den", [128, 32], F32)
            tmpa = sb(ph, "tmpa", [128, 32], F32)
            rden = sb(ph, "rden", [128, 32], F32)
            V = "vector"
            S.op("scalar", I("activation", out=dtt[:], in_=ldt[:], func=AF.Exp), reads=["ldt"], writes=["dtt"])
            S.op(V, I("tensor_tensor", out=ar[:], in0=areT[:], in1=dtt[:], op=ALU.mult), reads=["areT", "dtt"], writes=["ar"])
            S.op(V, I("scalar_tensor_tensor", out=tau[:], in0=aimT[:], scalar=INV_2PI, in1=dtt[:], op0=ALU.mult, op1=ALU.mult),
                 reads=["aimT", "dtt"], writes=["tau"])
            S.op(V, I("tensor_scalar", out=tau8[:], in0=tau[:], scalar1=8.0, scalar2=None, op0=ALU.mult),
                 reads=["tau"], writes=["tau8"])
            S.op("scalar", I("activation", out=rho2[:], in_=ar[:], func=AF.Exp, scale=8.0), reads=["ar"], writes=["rho2"])
            kv0 = sb(ph, "kv0", [128, 1025], F32)
            load("sync", kv0[:], T["kvec"], "kv0")
            RIs = sb(ph, "RIs", [128, 32, 33], I32)
            t8b33 = tau8[:].unsqueeze(2).broadcast_to([128, 32, 33])
            t8b32 = tau8[:].unsqueeze(2).broadcast_to([128, 32, 32])
            k1v = kv0[:, 0:1025:32].unsqueeze(1).broadcast_to([128, 32, 33])
            k0v = kv0[:, 0:32].unsqueeze(1).broadcast_to([128, 32, 32])
            for (dst_, kv_, tb_, n_, off_, key_) in ((S1t, k1v, t8b33, 33, 0.0, "S1t"), (C1t, k1v, t8b33, 33, 0.25, "C1t"),
                                                     (S0t, k0v, t8b32, 32, 0.0, "S0t"), (C0t, k0v, t8b32, 32, 0.25, "C0t")):
                S.op(V, I("tensor_tensor", out=dst_[:], in0=tb_, in1=kv_, op=ALU.mult), reads=["tau8", "kv0"], writes=[key_])
                if off_:
                    S.op(V, I("tensor_scalar", out=dst_[:], in0=dst_[:], scalar1=off_, scalar2=None, op0=ALU.add), reads=[key_], writes=[key_])
                S.op(V, I("tensor_copy", out=RIs[:, :, 0:n_], in_=dst_[:]), reads=[key_], writes=["RIs"])
                S.op(V, I("tensor_tensor", out=dst_[:], in0=dst_[:], in1=RIs[:, :, 0:n_], op=ALU.subtract), reads=[key_, "RIs"], writes=[key_])
                S.op("scalar", I("activation", out=dst_[:], in_=dst_[:], func=AF.Sin, scale=TWO_PI_S), reads=[key_], writes=[key_])
            S.op(V, I("tensor_tensor", out=den[:], in0=areT[:], in1=areT[:], op=ALU.mult), reads=["areT"], writes=["den"])
            S.op(V, I("tensor_tensor", out=tmpa[:], in0=aimT[:], in1=aimT[:], op=ALU.mult), reads=["aimT"], writes=["tmpa"])
            S.op(V, I("tensor_tensor", out=den[:], in0=den[:], in1=tmpa[:], op=ALU.add), reads=["den", "tmpa"], writes=["den"])
            S.op(V, I("reciprocal", out=rden[:], in_=den[:]), reads=["den"], writes=["rden"])

            MAG = sb(ph, "MAG", [128, NM, 32], F32)
            AS_ = sb(ph, "AS_", [128, NM, 32], F32)
            AC_ = sb(ph, "AC_", [128, NM, 32], F32)
            RI = sb(ph, "RI", [128, NM, 32], I32)
            SINT = sb(ph, "SINT", [128, NM, 32], F32)
            COST = sb(ph, "COST", [128, NM, 32], F32)
            PRE = sb(ph, "PRE", [128, NM, 32], F32)
            PIM = sb(ph, "PIM", [128, NM, 32], F32)
            mvt = sb(ph, "mvt", [128, NM], F32)
            load("sync", mvt[:], T["mvals"], "mvt")
            mb_ = mvt[:].unsqueeze(2).broadcast_to([128, NM, 32])
            S.op(V, I("tensor_tensor", out=AS_[:], in0=tau[:].unsqueeze(1).broadcast_to([128, NM, 32]), in1=mb_, op=ALU.mult),
                 reads=["tau", "mvt"], writes=["AS_"])
            S.op(V, I("tensor_scalar", out=AC_[:], in0=AS_[:], scalar1=0.25, scalar2=None, op0=ALU.add), reads=["AS_"], writes=["AC_"])
            S.op(V, I("tensor_tensor", out=MAG[:], in0=ar[:].unsqueeze(1).broadcast_to([128, NM, 32]), in1=mb_, op=ALU.mult),
                 reads=["ar", "mvt"], writes=["MAG"])
            S.op("scalar", I("activation", out=MAG[:], in_=MAG[:], func=AF.Exp), reads=["MAG"], writes=["MAG"])
            fl = lambda t: t[:].rearrange("p a b -> p (a b)")
            for A_, OUT, key in ((AS_, SINT, "SINT"), (AC_, COST, "COST")):
                akey = "AS_" if A_ is AS_ else "AC_"
                S.op(V, I("tensor_copy", out=fl(RI), in_=fl(A_)), reads=[akey], writes=["RI"])
                S.op(V, I("tensor_tensor", out=fl(A_), in0=fl(A_), in1=fl(RI), op=ALU.subtract),
                     reads=[akey, "RI"], writes=[akey])
                S.op("scalar", I("activation", out=fl(OUT), in_=fl(A_), func=AF.Sin, scale=TWO_PI_S),
                     reads=[akey], writes=[key])
            S.op(V, I("tensor_tensor", out=fl(PRE), in0=fl(MAG), in1=fl(COST), op=ALU.mult), reads=["MAG", "COST"], writes=["PRE"])
            S.op(V, I("tensor_tensor", out=fl(PIM), in0=fl(MAG), in1=fl(SINT), op=ALU.mult), reads=["MAG", "SINT"], writes=["PIM"])
            i1 = IDX_P + 1
            t0 = sb(ph, "c_t0", [128, 32], F32)
            t1 = sb(ph, "c_t1", [128, 32], F32)
            t2 = sb(ph, "c_t2", [128, 32], F32)
            cre = sb(ph, "cre", [128, 32], F32)
            cim = sb(ph, "cim", [128, 32], F32)
            S.op(V, I("tensor_scalar", out=t0[:], in0=PRE[:, i1, :], scalar1=-1.0, scalar2=None, op0=ALU.add), reads=["PRE"], writes=["c_t0"])
            S.op(V, I("tensor_tensor", out=t1[:], in0=t0[:], in1=areT[:], op=ALU.mult), reads=["c_t0", "areT"], writes=["c_t1"])
            S.op(V, I("tensor_tensor", out=t2[:], in0=PIM[:, i1, :], in1=aimT[:], op=ALU.mult), reads=["PIM", "aimT"], writes=["c_t2"])
            S.op(V, I("tensor_tensor", out=t1[:], in0=t1[:], in1=t2[:], op=ALU.add), reads=["c_t1", "c_t2"], writes=["c_t1"])
            S.op(V, I("tensor_tensor", out=cre[:], in0=t1[:], in1=rden[:], op=ALU.mult), reads=["c_t1", "rden"], writes=["cre"])
            S.op(V, I("tensor_tensor", out=t1[:], in0=PIM[:, i1, :], in1=areT[:], op=ALU.mult), reads=["PIM", "areT", "cre"], writes=["c_t1"])
            S.op(V, I("tensor_tensor", out=t2[:], in0=t0[:], in1=aimT[:], op=ALU.mult), reads=["c_t0", "aimT"], writes=["c_t2"])
            S.op(V, I("tensor_tensor", out=t1[:], in0=t1[:], in1=t2[:], op=ALU.subtract), reads=["c_t1", "c_t2"], writes=["c_t1"])
            S.op(V, I("tensor_tensor", out=cim[:], in0=t1[:], in1=rden[:], op=ALU.mult), reads=["c_t1", "rden"], writes=["cim"])
            QRE = sb(ph, "QRE", [128, 16, 32], F32)
            QIM = sb(ph, "QIM", [128, 16, 32], F32)
            QT1 = sb(ph, "QT1", [128, 16, 32], F32)
            cre_b = cre[:].unsqueeze(1).broadcast_to([128, 16, 32])
            cim_b = cim[:].unsqueeze(1).broadcast_to([128, 16, 32])
            S.op(V, I("tensor_tensor", out=QRE[:], in0=PRE[:, 0:16, :], in1=cre_b, op=ALU.mult), reads=["PRE", "cre"], writes=["QRE"])
            S.op(V, I("tensor_tensor", out=QT1[:], in0=PIM[:, 0:16, :], in1=cim_b, op=ALU.mult), reads=["PIM", "cim"], writes=["QT1"])
            S.op(V, I("tensor_tensor", out=QRE[:], in0=QRE[:], in1=QT1[:], op=ALU.subtract), reads=["QRE", "QT1"], writes=["QRE"])
            S.op(V, I("tensor_tensor", out=QIM[:], in0=PRE[:, 0:16, :], in1=cim_b, op=ALU.mult), reads=["PRE", "cim"], writes=["QIM"])
            S.op(V, I("tensor_tensor", out=QT1[:], in0=PIM[:, 0:16, :], in1=cre_b, op=ALU.mult), reads=["PIM", "cre", "QRE"], writes=["QT1"])
            S.op(V, I("tensor_tensor", out=QIM[:], in0=QIM[:], in1=QT1[:], op=ALU.add), reads=["QIM", "QT1"], writes=["QIM"])
            sT = sgn[:, 0:1]
            sB = sgn[:, 1:2]
            QIMsT = sb(ph, "QIMsT", [128, 16, 32], F32)
            QREsB = sb(ph, "QREsB", [128, 16, 32], F32)
            PREsB = sb(ph, "PREsB", [128, NM, 32], F32)
            PIMsT = sb(ph, "PIMsT", [128, NM, 32], F32)
            PREn = sb(ph, "PREn", [128, NM, 32], F32)
            PIMn = sb(ph, "PIMn", [128, NM, 32], F32)
            for (o_, i_, sc_, k_o, k_i) in ((QIMsT, QIM, sT, "QIMsT", "QIM"), (QREsB, QRE, sB, "QREsB", "QRE"),
                                            (PREsB, PRE, sB, "PREsB", "PRE"), (PIMsT, PIM, sT, "PIMsT", "PIM"),
                                            (PREn, PRE, -1.0, "PREn", "PRE"), (PIMn, PIM, -1.0, "PIMn", "PIM")):
                S.op(V, I("tensor_scalar", out=fl(o_), in0=fl(i_), scalar1=sc_, scalar2=None, op0=ALU.mult),
                     reads=[k_i, "sgn"], writes=[k_o])

            BIGA = sb(ph, "BIGA", [128, 32, 8, 16], F32)
            BIGB = sb(ph, "BIGB", [128, 32, 8, 16], F32)
            BIGT = sb(ph, "BIGT", [128, 32, 8, 16], F32)

            def big(out_t, okey, A, akey, a0, X, xkey, Bq, bkey, Y, ykey, eng):
                a_b = A[:, a0:a0 + 8, :].rearrange("p j g -> p g j").unsqueeze(3).broadcast_to([128, 32, 8, 16])
                b_b = Bq[:, a0:a0 + 8, :].rearrange("p j g -> p g j").unsqueeze(3).broadcast_to([128, 32, 8, 16])
                x_b = X[:].unsqueeze(2).broadcast_to([128, 32, 8, 16])
                y_b = Y[:].unsqueeze(2).broadcast_to([128, 32, 8, 16])
                for g0 in range(0, 32, 8):
                    gs = slice(g0, g0 + 8)
                    qk = "%s_q%d" % (okey, g0)
                    S.op(eng, I("tensor_tensor", out=out_t[:, gs], in0=a_b[:, gs], in1=x_b[:, gs], op=ALU.mult),
                         reads=[akey, xkey], writes=[qk, okey])
                    S.op(eng, I("tensor_tensor", out=BIGT[:, gs], in0=b_b[:, gs], in1=y_b[:, gs], op=ALU.mult),
                         reads=[bkey, ykey], writes=["BIGT_q%d" % g0])
                    S.op(eng, I("tensor_tensor", out=out_t[:, gs], in0=out_t[:, gs], in1=BIGT[:, gs], op=ALU.add),
                         reads=[qk, "BIGT_q%d" % g0], writes=[qk, okey])

            pT = [ps(ph, "pT%d" % i, [128, 512], F32) for i in range(4)]
            ptr = Ring(4)

            big(BIGA, "BIGA", PREsB, "PREsB", IDX_P + 1, ca, "ca", PIMn, "PIMn", cbm, "cbm", V)
            S.op("scalar", I("activation", out=W_CsA[:].rearrange("p g m -> p (g m)"), in_=BIGA[:].rearrange("p g j c -> p (g j c)"),
                                                  func=AF.Copy), reads=["BIGA"], writes=["W_CsA"])
            big(BIGB, "BIGB", PIMsT, "PIMsT", IDX_P + 1, ca, "ca", PREn, "PREn", cbm, "cbm", V)
            S.op("scalar", I("activation", out=W_CsB[:].rearrange("p g m -> p (g m)"), in_=BIGB[:].rearrange("p g j c -> p (g j c)"),
                                                  func=AF.Copy), reads=["BIGB"], writes=["W_CsB"])
            big(BIGA, "BIGA", QRE, "QRE", IDX_B, bx2, "bx2", QIMsT, "QIMsT", bsw2, "bsw2", V)
            for (src_t, skey, dst) in ((BIGA, "BIGA", W_BsT),):
                for g in range(32):
                    pi_ = ptr.next()
                    S.op("tensor", I("transpose", out=pT[pi_][:, 0:128],
                                                                                   in_=src_t[:, g].rearrange("p j c -> p (j c)"),
                                                                                   identity=ident_f[:]),
                         reads=[skey, "ident_f"], writes=["pT%d" % pi_])
                    if g % 2:
                        S.op("scalar", I("activation", out=dst[:, g, :], in_=pT[pi_][:, 0:128], func=AF.Copy),
                             reads=["pT%d" % pi_], writes=["W_BsT"])
                    else:
                        S.op("vector", I("tensor_copy", out=dst[:, g, :], in_=pT[pi_][:, 0:128]),
                             reads=["pT%d" % pi_], writes=["W_BsT"])
            big(BIGB, "BIGB", QREsB, "QREsB", IDX_B, bsw2, "bsw2", QIM, "QIM", bx2, "bx2", V)
            for g in range(32):
                pi_ = ptr.next()
                S.op("tensor", I("transpose", out=pT[pi_][:, 0:128], in_=BIGB[:, g].rearrange("p j c -> p (j c)"),
                                                                  identity=ident_f[:]),
                     reads=["BIGB", "ident_f"], writes=["pT%d" % pi_])
                if g % 2:
                    S.op("scalar", I("activation", out=W_BsTs[:, g, :], in_=pT[pi_][:, 0:128], func=AF.Copy),
                         reads=["pT%d" % pi_], writes=["W_BsTs"])
                else:
                    S.op("vector", I("tensor_copy", out=W_BsTs[:, g, :], in_=pT[pi_][:, 0:128]),
                         reads=["pT%d" % pi_], writes=["W_BsTs"])
            big(BIGA, "BIGA", QRE, "QRE", IDX_L, bx2, "bx2", QIMsT, "QIMsT", bsw2, "bsw2", V)
            big(BIGB, "BIGB", PREsB, "PREsB", IDX_P, ca, "ca", PIMn, "PIMn", cbm, "cbm", V)
            ttmp = sb(ph, "ttmp", [128, 2, 128], F32)
            tr2 = Ring(2)
            for g in range(32):
                pi_ = ptr.next()
                ti = tr2.next()
                S.op("tensor", I("matmul", pT[pi_][:, 0:128], lhsT=BIGA[:, g].rearrange("p j c -> p (j c)"),
                                                               rhs=BIGB[:, g].rearrange("p j c -> p (j c)"), start=True, stop=True),
                     reads=["BIGA", "BIGB"], writes=["pT%d" % pi_])
                S.op(V, I("tensor_tensor", out=ttmp[:, ti, :], in0=pT[pi_][:, 0:128], in1=tmask[:], op=ALU.mult),
                     reads=["pT%d" % pi_, "tmask"], writes=["ttmp%d" % ti])
                S.op(V, I("scalar_tensor_tensor", out=W_T[:, g, :], in0=ident_f[:], scalar=dvec[:, g:g + 1],
                                                                     in1=ttmp[:, ti, :], op0=ALU.mult, op1=ALU.add),
                     reads=["ident_f", "dvec", "ttmp%d" % ti], writes=["W_T"])
            S.flush()

        with ExitStack() as ph:
            win = sb(ph, "win", [128, 8, 2048], BF16)
            gmix = sb(ph, "gmix", [128, 8], F32)
            stg = [sb(ph, "stg%d" % i, [128, 2048], F32) for i in range(2)]
            load("sync", gmix[:], T["g_mix"], "gmix")
            w_in_v = T["w_in"].rearrange("(c p) n -> c p n", p=128)
            for c in range(8):
                si = c % 2
                load("sync", stg[si][:], w_in_v[c], "stg%d" % si)
                S.op("gpsimd" if c % 2 else "vector",
                     I("tensor_scalar", out=win[:, c, :], in0=stg[si][:], scalar1=gmix[:, c:c + 1], scalar2=1.0,
                                                           op0=ALU.mult, op1=ALU.mult),
                     reads=["stg%d" % si, "gmix"], writes=["win"])
            NXS = 6
            xs = [sb(ph, "xs%d" % i, [128, D], F32) for i in range(NXS)]
            xring = Ring(NXS)
            hb = [sb(ph, "hb%d" % i, [128, D], BF16) for i in range(8)]
            hring = Ring(8)
            junk = sb(ph, "junk", [128, D], BF16)
            ssq = sb(ph, "ssq", [128, 8], F32)
            sring = Ring(8)
            sdv = sb(ph, "sdv", [128, 8], F32)
            rsd = sb(ph, "rsd", [128, 8], F32)
            hT = [sb(ph, "hT%d" % i, [128, 8, 512], BF16) for i in range(2)]
            ost = [sb(ph, "ost%d" % i, [128, 512], BF16) for i in range(6)]
            oring = Ring(6)
            ptp = [ps(ph, "ptp%d" % i, [128, 1024], BF16) for i in range(2)]
            tpr = Ring(2)
            pmm = [ps(ph, "pmm%d" % i, [128, 512], F32) for i in range(4)]
            mring = Ring(4)
            x_v = T["x"].rearrange("(t p) d -> t p d", p=128)
            ev = [0]

            def evac(dst, src, rkeys, wkeys):
                ev[0] += 1
                if ev[0] % 2:
                    S.op("vector", I("tensor_copy", out=dst, in_=src), reads=rkeys, writes=wkeys)
                    return "vector"
                S.op("scalar", I("activation", out=dst, in_=src, func=AF.Copy), reads=rkeys, writes=wkeys)
                return "scalar"


            hb_of = {}

            def prepA(blk):
                his = []
                for i in range(4):
                    tix = blk * 4 + i
                    xi = xring.next()
                    load("sync", xs[xi][:], x_v[tix], "xs%d" % xi)
                    si = sring.next()
                    S.op("scalar", I("activation", out=junk[:], in_=xs[xi][:], func=AF.Square, accum_out=ssq[:, si:si + 1]),
                         reads=["xs%d" % xi], writes=["junk", "ssq%d" % si])
                    S.op("scalar", I("activation", out=sdv[:, si:si + 1], in_=ssq[:, si:si + 1], func=AF.Sqrt, scale=1.0 / D, bias=epsb[:, 0:1]),
                         reads=["ssq%d" % si, "epsb"], writes=["sdv%d" % si])
                    S.op("vector", I("reciprocal", out=rsd[:, si:si + 1], in_=sdv[:, si:si + 1]),
                         reads=["sdv%d" % si], writes=["rsd%d" % si])
                    hi = hring.next()
                    his.append(hi)
                    S.op("gpsimd", I("tensor_scalar", out=hb[hi][:], in0=xs[xi][:], scalar1=rsd[:, si:si + 1], scalar2=1.0,
                                     op0=ALU.mult, op1=ALU.mult),
                         reads=["xs%d" % xi, "rsd%d" % si], writes=["hb%d" % hi])
                hb_of[blk] = his

            def prepB(blk):
                hTi = blk % 2
                for i in range(4):
                    hi = hb_of[blk][i]
                    ti = tpr.next()
                    for c in range(8):
                        S.op("tensor", I("transpose", out=ptp[ti][:, c * 128:(c + 1) * 128], in_=hb[hi][:, c * 128:(c + 1) * 128],
                                         identity=ident_bf[:]),
                             reads=["hb%d" % hi, "ident_bf"], writes=["ptp%d" % ti], accum=True)
                    evac(hT[hTi][:, :, i * 128:(i + 1) * 128], ptp[ti][:].rearrange("p (c t) -> p c t", c=8),
                         ["ptp%d" % ti], ["hT%d" % hTi])

            prepA(0)
            prepB(0)
            prepA(1)
            for blk in range(16):
                hTi = blk % 2
                if blk + 2 < 16:
                    prepA(blk + 2)
                tsl = slice(blk * 512, (blk + 1) * 512)
                for oc in range(12):
                    col0 = oc * 128 if oc < 8 else 1536 + (oc - 8) * 128
                    mi = mring.next()
                    for c in range(8):
                        S.op("tensor", I("matmul", pmm[mi][:, :], lhsT=win[:, c, col0:col0 + 128],
                                                                                 rhs=hT[hTi][:, c, :], start=(c == 0), stop=(c == 7)),
                             reads=["win", "hT%d" % hTi], writes=["pmm%d" % mi], accum=True)
                    oi = oring.next()
                    if oc < 8:
                        se = evac(ost[oi][:], pmm[mi][:, :], ["pmm%d" % mi], ["ost%d" % oi])
                    else:
                        se = evac(ost[oi][:].rearrange("p (j k) -> p j k", j=8), pmm[mi][:, :].rearrange("p (k j) -> p j k", j=8),
                                  ["pmm%d" % mi], ["ost%d" % oi])
                    if oc < 4:
                        dst = qT_d[oc * 128:(oc + 1) * 128, tsl]
                        dk = "qT_d"
                    elif oc < 8:
                        dst = kT_d[(oc - 4) * 128:(oc - 3) * 128, tsl]
                        dk = "kT_d"
                    else:
                        dst = uT_d[(oc - 8) * 128:(oc - 7) * 128, :, blk * 64:(blk + 1) * 64]
                        dk = "uT_d"
                    src_ = ost[oi][:] if oc < 8 else ost[oi][:].rearrange("p (j k) -> p j k", j=8)
                    S.op("sync" if se == "vector" else se, I("dma_start", out=dst, in_=src_), reads=["ost%d" % oi], writes=[dk], dma=True)
                for i in range(4):
                    mi = mring.next()
                    for c in range(8):
                        S.op("tensor", I("matmul", pmm[mi][:, :], lhsT=hT[hTi][:, c, i * 128:(i + 1) * 128],
                                                                           rhs=win[:, c, 1024:1536], start=(c == 0), stop=(c == 7)),
                             reads=["win", "hT%d" % hTi], writes=["pmm%d" % mi], accum=True)
                    oi = oring.next()
                    se = evac(ost[oi][:], pmm[mi][:, :], ["pmm%d" % mi], ["ost%d" % oi])
                    r0 = blk * 512 + i * 128
                    S.op("sync" if se == "vector" else se, I("dma_start", out=v_d[r0:r0 + 128, :], in_=ost[oi][:]),
                         reads=["ost%d" % oi], writes=["v_d"], dma=True)
                if blk + 1 < 16:
                    prepB(blk + 1)
            S.flush()

        with ExitStack() as ph:
            KT = [sb(ph, "KT%d" % i, [128, S_LEN], BF16) for i in range(2)]
            QT = [sb(ph, "QT%d" % i, [128, S_LEN], BF16) for i in range(2)]
            VA = [sb(ph, "VA%d" % i, [128, 64, 130], BF16) for i in range(2)]
            NPT = 3
            PT = [[sb(ph, "PT%d_%d" % (c, i), [128, 512], BF16) for i in range(NPT)] for c in range(2)]
            ptr_ = Ring(NPT)
            SP = [[ps(ph, "SP%d_%d" % (c, i), [128, 512], F32) for i in range(2)] for c in range(2)]
            OA = ps(ph, "OA", [128, 512], F32)
            OB = ps(ph, "OB", [128, 512], F32)
            OC = ps(ph, "OC", [128, 512], F32)
            PTR = ps(ph, "PTR", [128, 1024], BF16)
            rc = sb(ph, "rc", [128, 8], F32)
            o2 = sb(ph, "o2", [128, 128], F32)
            oo4 = sb(ph, "oo4", [128, 4, 128], F32)
            ojunk = sb(ph, "ojunk", [128, 128], F32)
            ass = sb(ph, "ass", [128, 8], F32)
            attb2 = [sb(ph, "attb%d" % i, [128, 128], BF16) for i in range(2)]
            aTs = [sb(ph, "aTs%d" % i, [128, 512], BF16) for i in range(2)]
            for i in range(2):
                S.op("vector", I("memset", VA[i][:, :, 128:130], 1.0), writes=["VA%d" % i])

            def acc_ap(c, r):
                if r < 3:
                    return (OA if c == 0 else OB)[:, r * 129:(r + 1) * 129], ("OA" if c == 0 else "OB")
                return OC[:, c * 129:(c + 1) * 129], "OC"

            v_v = v_d.rearrange("(t p) d -> p t d", p=128)
            gffn = sb(ph, "gffn", [128, 8], F32)
            gsub = sb(ph, "gsub", [128, 1], F32)
            gssm = sb(ph, "gssm", [128, 4], F32)
            gout = sb(ph, "gout", [128, 8], F32)
            load("sync", gffn[:], T["g_ffn"], "gffn")
            load("sync", gsub[:], T["g_sub"], "gsub")
            load("sync", gssm[:], T["g_ssm"], "gssm")
            for c in range(4):
                S.op("vector", I("tensor_scalar", out=gout[:, c:c + 1], in0=gsub[:], scalar1=0.8, scalar2=None, op0=ALU.mult),
                     reads=["gsub"], writes=["gout"])
            S.op("vector", I("tensor_copy", out=gout[:, 4:8], in_=gssm[:]), reads=["gssm"], writes=["gout"])
            cst = [sb(ph, "cst%d" % i, [128, DFF], F32) for i in range(1)]
            cob = [sb(ph, "cob%d" % i, [128, DFF], BF16) for i in range(1)]
            osin = sb(ph, "osin", [128, 1025], F32)
            ocos = sb(ph, "ocos", [128, 1025], F32)
            tmpA = sb(ph, "tmpA", [128, 32, 32], F32)

            def gen_tab(g):
                s1 = S1t[:, g, 0:32].unsqueeze(2).broadcast_to([128, 32, 32])
                c1 = C1t[:, g, 0:32].unsqueeze(2).broadcast_to([128, 32, 32])
                s0 = S0t[:, g, :].unsqueeze(1).broadcast_to([128, 32, 32])
                c0 = C0t[:, g, :].unsqueeze(1).broadcast_to([128, 32, 32])
                osv = osin[:, 0:1024].rearrange("p (a b) -> p a b", b=32)
                ocv = ocos[:, 0:1024].rearrange("p (a b) -> p a b", b=32)
                P_ = "gpsimd"
                rk = ["S1t", "C1t", "S0t", "C0t"]
                S.op(P_, I("tensor_tensor", out=osv, in0=s1, in1=c0, op=ALU.mult), reads=rk, writes=["osin"])
                S.op(P_, I("tensor_tensor", out=tmpA[:], in0=c1, in1=s0, op=ALU.mult), reads=rk, writes=["tmpA"])
                S.op(P_, I("tensor_tensor", out=osv, in0=osv, in1=tmpA[:], op=ALU.add), reads=["osin", "tmpA"], writes=["osin"])
                S.op(P_, I("tensor_copy", out=osin[:, 1024:1025], in_=S1t[:, g, 32:33]), reads=rk, writes=["osin"])
                S.op(P_, I("dma_start", out=tab_d[g, 0], in_=osin[:]), reads=["osin"], writes=["tab_d"], dma=True)
                S.op(P_, I("tensor_tensor", out=ocv, in0=c1, in1=c0, op=ALU.mult), reads=rk, writes=["ocos"])
                S.op(P_, I("tensor_tensor", out=tmpA[:], in0=s1, in1=s0, op=ALU.mult), reads=rk, writes=["tmpA"])
                S.op(P_, I("tensor_tensor", out=ocv, in0=ocv, in1=tmpA[:], op=ALU.subtract), reads=["ocos", "tmpA"], writes=["ocos"])
                S.op(P_, I("tensor_copy", out=ocos[:, 1024:1025], in_=C1t[:, g, 32:33]), reads=rk, writes=["ocos"])
                S.op(P_, I("dma_start", out=tab_d[g, 1], in_=ocos[:]), reads=["ocos"], writes=["tab_d"], dma=True)

            cjobs = []
            for (src, dst, ncols, gain, nch) in ((T["w_out"], wout_b, D, gout, 8), (T["w_gate"], wg_b, DFF, gffn, 8),
                                                 (T["w_up"], wu_b, DFF, gffn, 8), (T["w_down"], wd_b, D, None, NFC)):
                sv = src.rearrange("(c p) n -> c p n", p=128)
                dv = dst.rearrange("(c p) n -> c p n", p=128)
                for c in range(nch):
                    cjobs.append((sv[c], dv[c], ncols, gain, c))
            cj = [0]

            def conv_job():
                if cj[0] >= len(cjobs):
                    return
                src, dst, ncols, gain, c = cjobs[cj[0]]
                k = 0
                cj[0] += 1
                load("sync", cst[k][:, 0:ncols], src, "cst%d" % k)
                if gain is None:
                    S.op("gpsimd", I("tensor_copy", out=cob[k][:, 0:ncols], in_=cst[k][:, 0:ncols]), reads=["cst%d" % k], writes=["cob%d" % k])
                else:
                    S.op("gpsimd", I("tensor_scalar", out=cob[k][:, 0:ncols], in0=cst[k][:, 0:ncols], scalar1=gain[:, c:c + 1], scalar2=1.0,
                                     op0=ALU.mult, op1=ALU.mult), reads=["cst%d" % k, "gout", "gffn"], writes=["cob%d" % k])
                S.op("gpsimd", I("dma_start", out=dst, in_=cob[k][:, 0:ncols]), reads=["cob%d" % k], writes=["wconv_d"], dma=True)

            def head_loads(h):
                bi = h % 2
                load("sync", KT[bi][:], kT_d[h * 128:(h + 1) * 128, :], "KT%d" % bi)
                load("sync", QT[bi][:], qT_d[h * 128:(h + 1) * 128, :], "QT%d" % bi)
                for t0 in range(0, 64, 16):
                    S.op("sync", I("dma_start", out=VA[bi][:, t0:t0 + 16, 0:128],
                                                                          in_=v_v[:, t0:t0 + 16, h * 128:(h + 1) * 128]),
                         reads=["v_d"], writes=["VA%d" % bi], dma=True)

            head_loads(0)
            for h in range(4):
                bi = h % 2
                if h + 1 < 4:
                    head_loads(h + 1)
                kq = ["KT%d" % bi, "QT%d" % bi]
                iters = [(jb, kt) for jb in range(16) for kt in range(4 * jb + 4)]

                def emit_qk(jb, kt, slot):
                    m = kt - 4 * jb
                    c0 = 128 * max(m, 0)
                    for c in range(2):
                        S.op("tensor", I("matmul", SP[c][slot][:, c0:512], lhsT=KT[bi][c * 64:(c + 1) * 64, kt * 128:(kt + 1) * 128],
                            rhs=QT[bi][c * 64:(c + 1) * 64, jb * 512 + c0:(jb + 1) * 512], start=True, stop=True),
                             reads=kq, writes=["SP%d_%d" % (c, slot)])

                emit_qk(iters[0][0], iters[0][1], 0)
                started = set()
                for it, (jb, kt) in enumerate(iters):
                    slot = it % 2
                    if it % 40 == 20:
                        conv_job()
                    gi_ = h * len(iters) + it
                    if gi_ % 68 == 10:
                        gen_tab(gi_ // 68)
                    if it + 1 < len(iters):
                        emit_qk(iters[it + 1][0], iters[it + 1][1], (it + 1) % 2)
                    m = kt - 4 * jb
                    c0 = 128 * max(m, 0)
                    pi_ = ptr_.next()
                    for c in range(2):
                        S.op("scalar", I("activation", out=PT[c][pi_][:, c0:512], in_=SP[c][slot][:, c0:512], func=AF.Exp, scale=0.125),
                             reads=["SP%d_%d" % (c, slot)], writes=["PT%d_%d" % (c, pi_)])
                    for r in range(4):
                        dl = 4 * jb + r - kt
                        if dl in (0, 1):
                            for c in range(2):
                                S.op("vector", I("tensor_tensor", out=PT[c][pi_][:, r * 128:(r + 1) * 128], in0=PT[c][pi_][:, r * 128:(r + 1) * 128],
                                                 in1=E_all[:, h * 2 + dl, :], op=ALU.mult),
                                     reads=["PT%d_%d" % (c, pi_), "E_all"], writes=["PT%d_%d" % (c, pi_)])
                    if kt == 0:
                        started = set()
                    for c in range(2):
                        for r in range(max(m, 0), 4):
                            ap_, key = acc_ap(c, r)
                            first = key not in started
                            started.add(key)
                            S.op("tensor", I("matmul", ap_, lhsT=PT[c][pi_][:, r * 128:(r + 1) * 128], rhs=VA[bi][:, kt, 0:129],
                                start=first, stop=(kt == 4 * jb + r), skip_group_check=True),
                                 reads=["PT%d_%d" % (c, pi_), "VA%d" % bi], writes=[key], accum=True)
                    if kt == 4 * jb + 3:
                        asi = jb % 2
                        for r in range(4):
                            a1, k1 = acc_ap(0, r)
                            a2, k2 = acc_ap(1, r)
                            S.op("vector", I("reciprocal", out=rc[:, 0:1], in_=a1[:, 128:129]), reads=[k1], writes=["rc0"])
                            S.op("vector", I("reciprocal", out=rc[:, 1:2], in_=a2[:, 128:129]), reads=[k2], writes=["rc1"])
                            S.op("vector", I("tensor_tensor", out=rc[:, 2:3], in0=rc[:, 1:2], in1=neglam[:], op=ALU.mult),
                                 reads=["rc1", "neglam"], writes=["rc2"])
                            S.op("vector", I("tensor_scalar", out=o2[:], in0=a2[:, 0:128], scalar1=rc[:, 2:3], scalar2=None, op0=ALU.mult),
                                 reads=[k2, "rc2"], writes=["o2"])
                            S.op("vector", I("scalar_tensor_tensor", out=oo4[:, r, :], in0=a1[:, 0:128], scalar=rc[:, 0:1], in1=o2[:],
                                             op0=ALU.mult, op1=ALU.add),
                                 reads=[k1, "rc0", "o2"], writes=["oo%d" % r])
                            S.op("vector", I("scalar_tensor_tensor", out=ojunk[:], in0=oo4[:, r, :], scalar=1.0, in1=oo4[:, r, :],
                                             op0=ALU.mult, op1=ALU.mult, accum_out=ass[:, r:r + 1]),
                                 reads=["oo%d" % r], writes=["ojunk", "ass_s%d" % r])
                        S.op("scalar", I("activation", out=ass[:, 4:8], in_=ass[:, 0:4], func=AF.Ln, scale=1.0 / 128, bias=epsb[:, 0:1]),
                             reads=["ass_s0", "ass_s1", "ass_s2", "ass_s3", "epsb"], writes=["ass_l"])
                        S.op("scalar", I("activation", out=rc[:, 4:8], in_=ass[:, 4:8], func=AF.Exp, scale=-0.5), reads=["ass_l"], writes=["rc_r"])
                        for r in range(4):
                            S.op("vector", I("tensor_scalar", out=attb2[r % 2][:], in0=oo4[:, r, :], scalar1=rc[:, 4 + r:5 + r], scalar2=None, op0=ALU.mult),
                                 reads=["oo%d" % r, "rc_r"], writes=["attb%d" % (r % 2)])
                            S.op("tensor", I("transpose", out=PTR[:, r * 128:(r + 1) * 128], in_=attb2[r % 2][:], identity=ident_bf[:]),
                                 reads=["attb%d" % (r % 2), "ident_bf"], writes=["PTR"], accum=True)
                        S.op("vector", I("tensor_copy", out=aTs[asi][:], in_=PTR[:, 0:512]), reads=["PTR"], writes=["aTs%d" % asi])
                        S.op("gpsimd", I("dma_start", out=catT_d[h * 128:(h + 1) * 128, jb * 512:(jb + 1) * 512],
                                                                                  in_=aTs[asi][:]),
                             reads=["aTs%d" % asi], writes=["catT_d"], dma=True)
            while cj[0] < len(cjobs):
                conv_job()
            S.flush()

        mid2.close()
        with ExitStack() as ph:
            sel = sb(ph, "sel", [128, 64, 128], BF16)
            load("sync", sel[:].rearrange("p a b -> p (a b)"), T["sel"].rearrange("p a b -> p (a b)"), "sel")
            uT = sb(ph, "uT", [128, 8, 1024], BF16)
            Gs = [sb(ph, "Gs%d" % i, [128, 1024], BF16) for i in range(8)]
            gTn = sb(ph, "gTn", [128, S_LEN], BF16)
            NTB = 3
            COS = [sb(ph, "COS%d" % i, [128, 1025], F32) for i in range(NTB)]
            SIN = [sb(ph, "SIN%d" % i, [128, 1025], F32) for i in range(NTB)]
            Sb = [sb(ph, "Sb%d" % i, [128, 1025], F32) for i in range(NTB)]
            t1b = [sb(ph, "t1b%d" % i, [128, 512], F32) for i in range(2)]
            t2b = [sb(ph, "t2b%d" % i, [128, 512], F32) for i in range(2)]
            vmb = [sb(ph, "vmb%d" % i, [128, 512], F32) for i in range(3)]
            wcb = [sb(ph, "wcb%d" % i, [128, 512], BF16) for i in range(3)]
            wsb = [sb(ph, "wsb%d" % i, [128, 512], BF16) for i in range(3)]
            usb = [sb(ph, "usb%d" % i, [128, 512], BF16) for i in range(4)]
            pU = [ps(ph, "pU%d" % i, [128, 512], F32) for i in range(2)]
            pV = [ps(ph, "pV%d" % i, [128, 512], F32) for i in range(2)]
            pVs = [ps(ph, "pVs%d" % i, [128, 512], F32) for i in range(2)]
            pY = [ps(ph, "pY%d" % i, [128, 512], F32) for i in range(2)]
            for i in range(NTB):
                S.op("vector", I("memset", Sb[i][:, 0:1], 0.0), writes=["Sb%d_0" % i])
            NU = 64
            pur = Ring(2)

            def tables(g):
                tb = g % NTB
                load("sync", SIN[tb][:], tab_d[g, 0], "SIN%d" % tb)
                load("sync", COS[tb][:], tab_d[g, 1], "COS%d" % tb)

            def stA(u):
                g, hf = u // 2, u % 2
                cc, g8 = g // 8, g % 8
                if u % 16 == 0:
                    load("sync", uT[:], uT_d[cc * 128:(cc + 1) * 128, :, :], "uT")
                if hf == 0:
                    tables(g)
                pi_ = pur.next()
                for j in range(8):
                    S.op("tensor", I("matmul", pU[pi_][:, :], lhsT=sel[:, g8 * 8 + j, :],
                                     rhs=uT[:, j, hf * 512:(hf + 1) * 512],
                                     start=(j == 0), stop=(j == 7)),
                         reads=["sel", "uT"], writes=["pU%d" % pi_], accum=True)
                S.op("scalar", I("activation", out=usb[u % 4][:], in_=pU[pi_][:, :], func=AF.Copy), reads=["pU%d" % pi_], writes=["usb%d" % (u % 4)])

            def stB(u):
                g, hf = u // 2, u % 2
                tb, k0, p2, uu = g % NTB, hf * 512, u % 2, "usb%d" % (u % 4)
                S.op("tensor", I("matmul", pV[p2][:, :], lhsT=W_BsT[:, g, :], rhs=usb[u % 4][:], start=True, stop=True),
                     reads=["W_BsT", uu], writes=["pV%d" % p2])
                S.op("tensor", I("matmul", pVs[p2][:, :], lhsT=W_BsTs[:, g, :], rhs=usb[u % 4][:], start=True, stop=True),
                     reads=["W_BsTs", uu], writes=["pVs%d" % p2])
                S.op("vector", I("tensor_tensor", out=t1b[p2][:], in0=pV[p2][:, :], in1=COS[tb][:, 1 + k0:513 + k0], op=ALU.mult),
                     reads=["pV%d" % p2, "COS%d" % tb], writes=["t1b%d" % p2])
                S.op("vector", I("tensor_tensor", out=t2b[p2][:], in0=pVs[p2][:, :], in1=SIN[tb][:, 1 + k0:513 + k0], op=ALU.mult),
                     reads=["pVs%d" % p2, "SIN%d" % tb], writes=["t2b%d" % p2])
                S.op("gpsimd", I("tensor_tensor", out=vmb[u % 3][:], in0=t1b[p2][:], in1=t2b[p2][:], op=ALU.add),
                     reads=["t1b%d" % p2, "t2b%d" % p2], writes=["vmb%d" % (u % 3)])

            def stC(u):
                g, hf = u // 2, u % 2
                tb, k0, p3 = g % NTB, hf * 512, u % 3
                prevk = "Sb%d_%d" % (tb, hf)
                S.op("vector", I("tensor_tensor_scan", out=Sb[tb][:, 1 + k0:513 + k0], data0=rho2[:, g:g + 1].broadcast_to([128, 512]),
                                 data1=vmb[p3][:], initial=Sb[tb][:, k0:k0 + 1], op0=ALU.mult, op1=ALU.add),
                     reads=["rho2", "vmb%d" % p3, prevk], writes=["Sb%d_%d" % (tb, hf + 1)])
                rk = ["Sb%d_%d" % (tb, hf + 1), prevk]
                S.op("gpsimd", I("tensor_tensor", out=wcb[p3][:], in0=Sb[tb][:, k0:k0 + 512], in1=COS[tb][:, k0:k0 + 512], op=ALU.mult),
                     reads=rk + ["COS%d" % tb], writes=["wcb%d" % p3])
                S.op("vector", I("tensor_tensor", out=wsb[p3][:], in0=Sb[tb][:, k0:k0 + 512], in1=SIN[tb][:, k0:k0 + 512], op=ALU.mult),
                     reads=rk + ["SIN%d" % tb], writes=["wsb%d" % p3])

            def stD(u):
                g, hf = u // 2, u % 2
                g8, k0, p2, p3, uu = g % 8, hf * 512, u % 2, u % 3, "usb%d" % (u % 4)
                S.op("tensor", I("matmul", pY[p2][:, :], lhsT=W_T[:, g, :], rhs=usb[u % 4][:], start=True, stop=False),
                     reads=["W_T", uu], writes=["pY%d" % p2], accum=True)
                S.op("tensor", I("matmul", pY[p2][:, :], lhsT=W_CsA[:, g, :], rhs=wcb[p3][:], start=False, stop=False),
                     reads=["W_CsA", "wcb%d" % p3], writes=["pY%d" % p2], accum=True)
                S.op("tensor", I("matmul", pY[p2][:, :], lhsT=W_CsB[:, g, :], rhs=wsb[p3][:], start=False, stop=True),
                     reads=["W_CsB", "wsb%d" % p3], writes=["pY%d" % p2], accum=True)
                S.op("scalar", I("activation", out=Gs[g8][:, k0:k0 + 512], in_=pY[p2][:, :], func=AF.Gelu_apprx_tanh),
                     reads=["pY%d" % p2], writes=["Gs%d_%d" % (g8, hf)])

            def unshuffle(cc):
                for hf in range(2):
                    for i in range(8):
                        pi_ = pur.next()
                        for g8 in range(8):
                            S.op("tensor", I("matmul", pU[pi_][:, :], lhsT=sel[:, i * 8 + g8, :], rhs=Gs[g8][:, hf * 512:(hf + 1) * 512],
                                             start=(g8 == 0), stop=(g8 == 7)),
                                 reads=["sel", "Gs%d_%d" % (g8, hf)], writes=["pU%d" % pi_], accum=True)
                        dst = gTn[:, hf * 4096:(hf + 1) * 4096].rearrange("p (k j) -> p j k", j=8)[:, i, :]
                        if i % 2:
                            S.op("scalar", I("activation", out=dst, in_=pU[pi_][:, :], func=AF.Copy), reads=["pU%d" % pi_], writes=["gTn"])
                        else:
                            S.op("vector", I("tensor_copy", out=dst, in_=pU[pi_][:, :]), reads=["pU%d" % pi_], writes=["gTn"])
                S.op("gpsimd", I("dma_start", out=gT_d[cc * 128:(cc + 1) * 128, :], in_=gTn[:]), reads=["gTn"], writes=["gT_d"], dma=True)

            for step in range(NU + 3):
                if 0 <= step - 3 < NU:
                    stD(step - 3)
                    if (step - 3) % 16 == 15:
                        unshuffle((step - 3) // 16)
                if 0 <= step - 2 < NU:
                    stC(step - 2)
                if 0 <= step - 1 < NU:
                    stB(step - 1)
                if step < NU:
                    stA(step)
            S.flush()

        with ExitStack() as ph:
            wglu = sb(ph, "wglu", [128, 4, 512], BF16)
            bglu = sb(ph, "bglu", [128, 4], F32)
            stg = sb(ph, "stgg", [128, 4, 512], F32)
            load("sync", bglu[:], T["b_glu"], "bglu")
            load("sync", stg[:], T["w_glu"].rearrange("(c p) n -> p c n", p=128), "stgg")
            S.op("vector", I("tensor_copy", out=wglu[:], in_=stg[:]), reads=["stgg"], writes=["wglu"])
            gb = [sb(ph, "gb%d" % i, [128, 4, 512], BF16) for i in range(2)]
            sg = [sb(ph, "sg%d" % i, [128, 512], BF16) for i in range(2)]
            spre = sb(ph, "spre", [128, 4, 512], BF16)
            sq = sb(ph, "sq", [128, 4, 512], BF16)
            sdt = sb(ph, "sdt", [128, 512], F32)
            rst = sb(ph, "rst", [128, 512], F32)
            sso = [sb(ph, "sso%d" % i, [128, 4, 512], BF16) for i in range(2)]
            pG = [ps(ph, "pG%d" % i, [128, 512], F32) for i in range(2)]
            pS = ps(ph, "pS", [128, 512], F32)
            gT_v = gT_d.rearrange("(c p) t -> p c t", p=128)
            cat_v = catT_d[512:1024, :].rearrange("(c p) t -> p c t", p=128)
            for blk in range(16):
                bi = blk % 2
                tsl = slice(blk * 512, (blk + 1) * 512)
                load("sync", gb[bi][:], gT_v[:, :, tsl], "gb%d" % bi)
                for co in range(4):
                    pi_ = co % 2
                    for ci in range(4):
                        S.op("tensor", I("matmul", pG[pi_][:, :], lhsT=wglu[:, ci, co * 128:(co + 1) * 128],
                                                                                        rhs=gb[bi][:, ci, :], start=(ci == 0), stop=(ci == 3)),
                             reads=["wglu", "gb%d" % bi], writes=["pG%d" % pi_], accum=True)
                    S.op("scalar", I("activation", out=sg[pi_][:], in_=pG[pi_][:, :], func=AF.Sigmoid, bias=bglu[:, co:co + 1]),
                         reads=["pG%d" % pi_, "bglu"], writes=["sg%d" % pi_])
                    S.op("vector", I("tensor_tensor", out=spre[:, co, :], in0=gb[bi][:, co, :], in1=sg[pi_][:], op=ALU.mult),
                         reads=["gb%d" % bi, "sg%d" % pi_], writes=["spre%d" % co])
                    S.op("vector", I("tensor_tensor", out=sq[:, co, :], in0=spre[:, co, :], in1=spre[:, co, :], op=ALU.mult),
                         reads=["spre%d" % co], writes=["sq%d" % co])
                for co in range(4):
                    S.op("tensor", I("matmul", pS[:, :], lhsT=ones_bf[:], rhs=sq[:, co, :], start=(co == 0), stop=(co == 3)),
                         reads=["ones_bf", "sq%d" % co], writes=["pS"], accum=True)
                S.op("scalar", I("activation", out=sdt[:], in_=pS[:, :], func=AF.Ln, scale=1.0 / 512, bias=epsb[:, 0:1]),
                     reads=["pS", "epsb"], writes=["sdt"])
                S.op("scalar", I("activation", out=rst[:], in_=sdt[:], func=AF.Exp, scale=-0.5), reads=["sdt"], writes=["rst"])
                for co in range(4):
                    S.op("gpsimd" if co % 2 else "vector",
                         I("tensor_tensor", out=sso[bi][:, co, :], in0=spre[:, co, :], in1=rst[:], op=ALU.mult),
                         reads=["spre%d" % co, "rst"], writes=["sso%d" % bi])
                S.op("gpsimd", I("dma_start", out=cat_v[:, :, tsl], in_=sso[bi][:]),
                     reads=["sso%d" % bi], writes=["catT_d"], dma=True)
            S.flush()

        mid.close()
        with ExitStack() as ph:
            wout = sb(ph, "wout", [128, 8, D], BF16)
            wg = sb(ph, "wg", [128, 8, DFF], BF16)
            wu = sb(ph, "wu", [128, 8, DFF], BF16)
            wd = sb(ph, "wd", [128, NFC, D], BF16)
            gfin = sb(ph, "gfin", [128, D], F32)
            aT = sb(ph, "aT", [128, NFC, 256], BF16)
            load("sync", gfin[:], T["g_fin"], "gfin")
            load("sync", wout[:], wout_b.rearrange("(c p) n -> p c n", p=128), "wout")
            FQ = [0, 6, 12, 17, 22]
            fq_of = lambda fc: max(q for q in range(4) if FQ[q] <= fc)
            for q in range(4):
                cs = slice(FQ[q] * 128, FQ[q + 1] * 128)
                load("scalar", wg[:, :, cs], wg_b.rearrange("(c p) n -> p c n", p=128)[:, :, cs], "wg%d" % q)
                load("scalar", wu[:, :, cs], wu_b.rearrange("(c p) n -> p c n", p=128)[:, :, cs], "wu%d" % q)
            for c0 in range(0, NFC, 11):
                load("scalar", wd[:, c0:c0 + 11, :], wd_b.rearrange("(c p) n -> p c n", p=128)[:, c0:c0 + 11, :], "wd")

            NX4 = 4
            xs = [sb(ph, "x4_%d" % i, [128, D], F32) for i in range(NX4)]
            xr = Ring(NX4)
            cat = [sb(ph, "cat%d" % i, [128, 8, 256], BF16) for i in range(2)]
            h2 = [sb(ph, "h2_%d" % i, [128, D], BF16) for i in range(2)]
            junk = sb(ph, "junk4", [128, D], BF16)
            h2T = sb(ph, "h2T", [128, 8, 256], BF16)
            sgt = [sb(ph, "sgt%d" % i, [128, 256], BF16) for i in range(2)]
            st4 = sb(ph, "st4", [128, 16], F32)
            pO = [ps(ph, "pO%d" % i, [128, 512], F32) for i in range(2)]
            por = Ring(2)
            pTp = [ps(ph, "pTp%d" % i, [128, 1024], BF16) for i in range(2)]
            pGt = [ps(ph, "pGt%d" % i, [128, 512], F32) for i in range(2)]
            pUp = [ps(ph, "pUp%d" % i, [128, 512], F32) for i in range(2)]
            x_v = T["x"].rearrange("(t p) d -> t p d", p=128)
            o_v = out_d.rearrange("(t p) d -> t p d", p=128)
            cat_v = catT_d.rearrange("(c p) t -> p c t", p=128)
            NB = 32
            xis_of = {}

            def rms_rstd(xi, col):
                S.op("scalar", I("activation", out=junk[:], in_=xs[xi][:], func=AF.Square, accum_out=st4[:, col:col + 1]),
                     reads=["x4_%d" % xi], writes=["junk4", "st4_%d" % col])
                S.op("scalar", I("activation", out=st4[:, col + 1:col + 2], in_=st4[:, col:col + 1], func=AF.Sqrt, scale=1.0 / D, bias=epsb[:, 0:1]),
                     reads=["st4_%d" % col, "epsb"], writes=["st4_%d" % (col + 1)])
                S.op("vector", I("reciprocal", out=st4[:, col + 2:col + 3], in_=st4[:, col + 1:col + 2]),
                     reads=["st4_%d" % (col + 1)], writes=["st4_%d" % (col + 2)])

            def stA1(blk):
                ci_ = blk % 2
                tsl = slice(blk * 256, (blk + 1) * 256)
                load("sync", cat[ci_][:], cat_v[:, :, tsl], "cat%d" % ci_)
                xis = []
                for i in range(2):
                    xi = xr.next()
                    xis.append(xi)
                    load("sync", xs[xi][:], x_v[blk * 2 + i], "x4_%d" % xi)
                xis_of[blk] = xis
                for i in range(2):
                    xi = xis[i]
                    for n in range(2):
                        pi_ = por.next()
                        for c in range(8):
                            S.op("tensor", I("matmul", pO[pi_][:, :], lhsT=cat[ci_][:, c, i * 128:(i + 1) * 128],
                                             rhs=wout[:, c, n * 512:(n + 1) * 512], start=(c == 0), stop=(c == 7)),
                                 reads=["cat%d" % ci_, "wout"], writes=["pO%d" % pi_], accum=True)
                        S.op("vector", I("tensor_tensor", out=xs[xi][:, n * 512:(n + 1) * 512], in0=pO[pi_][:, :],
                                         in1=xs[xi][:, n * 512:(n + 1) * 512], op=ALU.add),
                             reads=["pO%d" % pi_, "x4_%d" % xi], writes=["x4_%d" % xi])
                    col = 4 * i
                    rms_rstd(xi, col)
                    S.op("gpsimd", I("tensor_scalar", out=h2[i][:], in0=xs[xi][:], scalar1=st4[:, col + 2:col + 3], scalar2=1.0,
                                     op0=ALU.mult, op1=ALU.mult),
                         reads=["x4_%d" % xi, "st4_%d" % (col + 2)], writes=["h2_%d" % i])

            def stA2(blk):
                for i in range(2):
                    for c in range(8):
                        S.op("tensor", I("transpose", out=pTp[i][:, c * 128:(c + 1) * 128], in_=h2[i][:, c * 128:(c + 1) * 128], identity=ident_bf[:]),
                             reads=["h2_%d" % i, "ident_bf"], writes=["pTp%d" % i], accum=True)
                    if i:
                        S.op("vector", I("tensor_copy", out=h2T[:, :, i * 128:(i + 1) * 128], in_=pTp[i][:].rearrange("p (c t) -> p c t", c=8)),
                             reads=["pTp%d" % i], writes=["h2T"])
                    else:
                        S.op("scalar", I("activation", out=h2T[:, :, i * 128:(i + 1) * 128], in_=pTp[i][:].rearrange("p (c t) -> p c t", c=8), func=AF.Copy),
                             reads=["pTp%d" % i], writes=["h2T"])

            def stB(blk):
                for fc in range(NFC):
                    gi = fc % 2
                    for c in range(8):
                        S.op("tensor", I("matmul", pGt[gi][:, 0:256], lhsT=wg[:, c, fc * 128:(fc + 1) * 128], rhs=h2T[:, c, :],
                                         start=(c == 0), stop=(c == 7)),
                             reads=["wg%d" % fq_of(fc), "h2T"], writes=["pGt%d" % gi], accum=True)
                    for c in range(8):
                        S.op("tensor", I("matmul", pUp[gi][:, 0:256], lhsT=wu[:, c, fc * 128:(fc + 1) * 128], rhs=h2T[:, c, :],
                                         start=(c == 0), stop=(c == 7)),
                             reads=["wu%d" % fq_of(fc), "h2T"], writes=["pUp%d" % gi], accum=True)
                    S.op("scalar", I("activation", out=sgt[gi][:], in_=pGt[gi][:, 0:256], func=AF.Silu),
                         reads=["pGt%d" % gi], writes=["sgt%d" % gi])
                    S.op("vector", I("tensor_tensor", out=aT[:, fc, :], in0=pUp[gi][:, 0:256], in1=sgt[gi][:], op=ALU.mult),
                         reads=["pUp%d" % gi, "sgt%d" % gi], writes=["aT"])

            def stC(blk):
                xis = xis_of[blk]
                for i in range(2):
                    xi = xis[i]
                    for n in range(2):
                        pi_ = por.next()
                        for fc in range(NFC):
                            S.op("tensor", I("matmul", pO[pi_][:, :], lhsT=aT[:, fc, i * 128:(i + 1) * 128],
                                             rhs=wd[:, fc, n * 512:(n + 1) * 512], start=(fc == 0), stop=(fc == NFC - 1)),
                                 reads=["aT", "wd"], writes=["pO%d" % pi_], accum=True)
                        S.op("vector", I("tensor_tensor", out=xs[xi][:, n * 512:(n + 1) * 512], in0=pO[pi_][:, :],
                                         in1=xs[xi][:, n * 512:(n + 1) * 512], op=ALU.add),
                             reads=["pO%d" % pi_, "x4_%d" % xi], writes=["x4_%d" % xi])
                    col = 8 + 4 * i
                    rms_rstd(xi, col)
                    S.op("vector", I("scalar_tensor_tensor", out=xs[xi][:], in0=xs[xi][:], scalar=st4[:, col + 2:col + 3], in1=gfin[:],
                                     op0=ALU.mult, op1=ALU.mult),
                         reads=["x4_%d" % xi, "st4_%d" % (col + 2), "gfin"], writes=["x4_%d" % xi])
                    S.op("sync", I("dma_start", out=o_v[blk * 2 + i], in_=xs[xi][:]),
                         reads=["x4_%d" % xi], writes=["out_d"], dma=True)

            stA1(0)
            stA2(0)
            for blk in range(NB):
                stB(blk)
                if blk + 1 < NB:
                    stA1(blk + 1)
                stC(blk)
                if blk + 1 < NB:
                    stA2(blk + 1)
            S.flush()
    return nc


def _layout_params(inp):
    f = lambda a: np.ascontiguousarray(np.asarray(a, dtype=np.float32))
    p = {}
    p["w_in"] = f(inp["w_in"][0])
    p["w_glu"] = f(inp["w_glu"][0])
    p["w_out"] = f(inp["w_out"][0])
    p["w_gate"] = f(inp["w_gate"][0])
    p["w_up"] = f(inp["w_up"][0])
    p["w_down"] = f(inp["w_down"][0])
    p["g_mix"] = f(np.asarray(inp["norm_mix_g"][0]).reshape(8, 128).T)
    p["g_ffn"] = f(np.asarray(inp["norm_ffn_g"][0]).reshape(8, 128).T)
    p["g_sub"] = f(np.asarray(inp["subln_g"][0]).reshape(128, 1))
    p["g_ssm"] = f(np.asarray(inp["ssm_norm_g"][0]).reshape(4, 128).T)
    p["b_glu"] = f(np.asarray(inp["b_glu"][0]).reshape(4, 128).T)
    p["g_fin"] = f(np.broadcast_to(np.asarray(inp["norm_final_g"]).reshape(1, D), (128, D)))
    p["lamv"] = f(np.concatenate([np.asarray(inp[k][0]).reshape(64) for k in
                                  ("lambda_q1", "lambda_k1", "lambda_q2", "lambda_k2")]).reshape(1, 256))
    rb = np.asarray(inp["rel_bias"])
    p["rel_bias"] = f(rb)
    p["rb31"] = f(rb[31].reshape(4, 1))
    aT = np.asarray(inp["A_re"][0]).T
    p["areT2"] = f(np.concatenate([aT, aT], 0))
    aT = np.asarray(inp["A_im"][0]).T
    p["aimT2"] = f(np.concatenate([aT, aT], 0))
    p["logdt2"] = f(np.broadcast_to(np.asarray(inp["log_dt"][0]).reshape(1, 32), (128, 32)))
    bre = np.asarray(inp["B_re"][0]).transpose(1, 0, 2).reshape(64, 512)
    bim = np.asarray(inp["B_im"][0]).transpose(1, 0, 2).reshape(64, 512)
    p["bx2"] = f(np.concatenate([bre, bim], 0))
    p["bsw2"] = f(np.concatenate([bim, bre], 0))
    cre = np.asarray(inp["C_re"][0]).transpose(2, 0, 1).reshape(64, 512)
    cim = np.asarray(inp["C_im"][0]).transpose(2, 0, 1).reshape(64, 512)
    p["ca"] = f(np.concatenate([cre, cim], 0))
    p["cb"] = f(np.concatenate([cim, cre], 0))
    dsk = np.asarray(inp["D_skip"][0])
    p["dvec"] = f(np.tile(dsk.T, (8, 1)))
    return p


_NC_CACHE = {}


def kernel(**inputs):
    x = np.asarray(inputs["x"], dtype=np.float32)
    shared = _layout_params(inputs)
    shared.update(_consts())
    if "nc" not in _NC_CACHE:
        _NC_CACHE["nc"] = build_nc()
    nc = _NC_CACHE["nc"]
    in_maps = [dict(shared, x=np.ascontiguousarray(x[b])) for b in range(8)]
    res = run_bass_kernel_spmd(nc, in_maps, core_ids=list(range(8)))
    return np.stack([np.asarray(r["out"], dtype=np.float32) for r in res.results], axis=0)
```

```python
import math
from contextlib import ExitStack

import numpy as np
import ml_dtypes
import concourse.bass as bass
import concourse.mybir as mybir
from concourse.bass_utils import run_bass_kernel_spmd

F32 = mybir.dt.float32
BF16 = mybir.dt.bfloat16
I32 = mybir.dt.int32
AF = mybir.ActivationFunctionType
ALU = mybir.AluOpType
AX = mybir.AxisListType

S_LEN = 8192
D = 1024
DFF = 2816
NFC = DFF // 128
EPS = 1e-6
TWO_PI_S = 6.2831850051879883
INV_2PI = 1.0 / (2.0 * math.pi)

ENGS = ("sync", "scalar", "vector", "gpsimd", "tensor")
N_DMA_SEMS = 40
N_HW_SEMS = 28


class Sched:
    def __init__(self, nc, stack):
        self.nc = nc
        self.esem = {e: stack.enter_context(nc.semaphore("s_" + e)) for e in ENGS}
        self.ecnt = {e: 0 for e in ENGS}
        self.dsem = [stack.enter_context(nc.semaphore("d%d" % i)) for i in range(N_DMA_SEMS)]
        self.dcnt = [0] * N_DMA_SEMS
        self.dnext = 0
        self.dnext_sw = 0
        self.dlast = [None] * N_DMA_SEMS
        self.waited = {e: {} for e in ENGS}
        self.ops = {e: [] for e in ENGS}
        self.lastw = {}
        self.readers = {}
        self.n_ops = 0

    def _need(self, eng, tok, waits):
        if tok is None:
            return
        semkey, sem, val = tok[0], tok[1], tok[2]
        w = self.waited[eng]
        if w.get(semkey, 0) >= val:
            return
        w[semkey] = val
        waits.append((sem, val))
        for k, v in tok[5].items():
            if w.get(k, 0) < v:
                w[k] = v

    def op(self, eng, fn, reads=(), writes=(), dma=False, accum=False):
        waits = []
        for k in reads:
            self._need(eng, self.lastw.get(k), waits)
        for k in writes:
            lw = self.lastw.get(k)
            if lw is not None and not (accum and eng == "tensor" and lw[3] == "tensor" and lw[4]):
                self._need(eng, lw, waits)
            for r in self.readers.get(k, ()):
                self._need(eng, r, waits)
        if dma:
            if eng == "gpsimd":
                i = N_HW_SEMS + self.dnext_sw
                self.dnext_sw = (self.dnext_sw + 1) % (N_DMA_SEMS - N_HW_SEMS)
            else:
                i = self.dnext
                self.dnext = (self.dnext + 1) % N_HW_SEMS
            self._need(eng, self.dlast[i], waits)
            self.dcnt[i] += 16
            tok = (("d", i), self.dsem[i], self.dcnt[i], "dma", False, dict(self.waited[eng]))
            self.dlast[i] = tok
            inc = (self.dsem[i], 16)
        else:
            self.ecnt[eng] += 1
            know = dict(self.waited[eng])
            know[("e", eng)] = self.ecnt[eng] - 1
            tok = (("e", eng), self.esem[eng], self.ecnt[eng], eng, accum, know)
            inc = (self.esem[eng], 1)
        self.ops[eng].append((waits, fn, inc))
        for k in reads:
            self.readers.setdefault(k, []).append(tok)
        for k in writes:
            self.lastw[k] = tok
            self.readers[k] = []
        self.n_ops += 1
        return tok

    def wait_all(self, eng, toks):
        waits = []
        for t in toks:
            self._need(eng, t, waits)
        if waits:
            self.ops[eng].append((waits, None, None))

    def flush(self):
        nc = self.nc
        self.wait_all("sync", [(("d", i), self.dsem[i], self.dcnt[i], "dma", False, {})
                               for i in range(N_DMA_SEMS) if self.dcnt[i]])
        ops = self.ops
        with nc.Block() as block:
            def mk(e):
                def body(engh):
                    for waits, fn, inc in ops[e]:
                        fuse = None
                        if fn is not None and waits and fn[0] != "dma_start" and "accum_out" not in fn[2]:
                            fuse = waits[-1]
                            waits = waits[:-1]
                        for sem, val in waits:
                            engh.wait_ge(sem, val)
                        if fn is not None:
                            name, a, kw = fn
                            ins = getattr(engh, name)(*a, **kw)
                            if fuse is not None:
                                ins._wait_ge(fuse[0], fuse[1])
                            ins.then_inc(inc[0], inc[1])
                return body
            for e in ENGS:
                if ops[e]:
                    getattr(block, e)(mk(e))
        self.ops = {e: [] for e in ENGS}
        self.lastw = {}
        self.readers = {}


def I(name, *a, **kw):
    return (name, a, kw)


class Ring:
    def __init__(self, n):
        self.n = n
        self.i = -1

    def next(self):
        self.i = (self.i + 1) % self.n
        return self.i


def _t5_bucket_np(n):
    n = np.asarray(n)
    nf = np.maximum(n, 1).astype(np.float32)
    large = 16 + (np.log(nf / np.float32(16)) / np.float32(math.log(8.0)) * np.float32(16)).astype(np.int32)
    large = np.minimum(large, 31)
    return np.where(n < 16, n, large)


def _consts():
    c = {}
    c["ident_bf"] = np.eye(128, dtype=np.float32).astype(ml_dtypes.bfloat16)
    c["ident_f"] = np.eye(128, dtype=np.float32)
    c["jrev"] = np.eye(128, dtype=np.float32)[::-1].copy()
    oh = np.zeros((32, 383), np.float32)
    mv = np.zeros((4, 383), np.float32)
    for i in range(383):
        n = i - 127
        if n >= 0:
            oh[int(_t5_bucket_np(n)), i] = 1.0
            mv[:, i] = 1.0
    c["oh"] = oh
    c["maskvec"] = mv
    sel = np.zeros((128, 64, 128), np.float32)
    for x_ in range(8):
        for y_ in range(8):
            for cc in range(16):
                sel[x_ * 16 + cc, x_ * 8 + y_, y_ * 16 + cc] = 1.0
    c["sel"] = sel.astype(ml_dtypes.bfloat16)
    jj = np.arange(128)[:, None] // 16
    ii = np.arange(128)[None, :] // 16
    c["tmask"] = (ii >= jj).astype(np.float32)
    c["kvec"] = np.broadcast_to(np.arange(1025, dtype=np.float32)[None, :], (128, 1025)).copy()
    sg = np.ones((128, 2), np.float32)
    sg[:64, 0] = -1.0
    sg[64:, 1] = -1.0
    c["sgn"] = sg
    mlist = [-j for j in range(8)] + [7 - j for j in range(8)] + list(range(9))
    c["mvals"] = np.broadcast_to(np.asarray(mlist, np.float32)[None, :], (128, len(mlist))).copy()
    return c


NM_ = 25
_CONST_SPECS = [
    ("ident_bf", [128, 128], BF16), ("ident_f", [128, 128], F32), ("jrev", [128, 128], F32),
    ("oh", [32, 383], F32), ("maskvec", [4, 383], F32), ("sel", [128, 64, 128], BF16),
    ("tmask", [128, 128], F32), ("kvec", [128, 1025], F32), ("sgn", [128, 2], F32), ("mvals", [128, NM_], F32),
]
_PARAM_SPECS = [
    ("x", [S_LEN, D], F32),
    ("w_in", [D, 2048], F32), ("w_glu", [512, 512], F32), ("w_out", [D, D], F32),
    ("w_gate", [D, DFF], F32), ("w_up", [D, DFF], F32), ("w_down", [DFF, D], F32),
    ("g_mix", [128, 8], F32), ("g_ffn", [128, 8], F32), ("g_sub", [128, 1], F32),
    ("g_ssm", [128, 4], F32), ("b_glu", [128, 4], F32), ("g_fin", [128, D], F32),
    ("lamv", [1, 256], F32), ("rel_bias", [32, 4], F32), ("rb31", [4, 1], F32),
    ("areT2", [128, 32], F32), ("aimT2", [128, 32], F32), ("logdt2", [128, 32], F32),
    ("bx2", [128, 512], F32), ("bsw2", [128, 512], F32), ("ca", [128, 512], F32), ("cb", [128, 512], F32),
    ("dvec", [128, 32], F32),
]

M_LIST = [-j for j in range(8)] + [7 - j for j in range(8)] + list(range(9))
NM = len(M_LIST)
IDX_L, IDX_B, IDX_P = 0, 8, 16


def build_nc(debug=False):
    nc = bass.Bass("TRN2", target_bir_lowering=False)
    IK = "ExternalOutput" if debug else "Internal"
    T = {}
    for name, shape, dt_ in _PARAM_SPECS + _CONST_SPECS:
        T[name] = nc.dram_tensor(name, shape, dt_, kind="ExternalInput").ap()
    out_d = nc.dram_tensor("out", [S_LEN, D], F32, kind="ExternalOutput").ap()
    qT_d = nc.dram_tensor("qT_d", [512, S_LEN], BF16, kind=IK).ap()
    kT_d = nc.dram_tensor("kT_d", [512, S_LEN], BF16, kind=IK).ap()
    v_d = nc.dram_tensor("v_d", [S_LEN, 512], BF16, kind=IK).ap()
    uT_d = nc.dram_tensor("uT_d", [512, 8, 1024], BF16, kind=IK).ap()
    gT_d = nc.dram_tensor("gT_d", [512, S_LEN], BF16, kind=IK).ap()
    catT_d = nc.dram_tensor("catT_d", [D, S_LEN], BF16, kind=IK).ap()
    gv_d = nc.dram_tensor("gv_d", [4, 383], F32, kind=IK).ap()
    tab_d = nc.dram_tensor("tab_d", [32, 2, 128, 1025], F32, kind="Internal").ap()
    wout_b = nc.dram_tensor("wout_b", [D, D], BF16, kind="Internal").ap()
    wg_b = nc.dram_tensor("wg_b", [D, DFF], BF16, kind="Internal").ap()
    wu_b = nc.dram_tensor("wu_b", [D, DFF], BF16, kind="Internal").ap()
    wd_b = nc.dram_tensor("wd_b", [DFF, D], BF16, kind="Internal").ap()

    with ExitStack() as top:
        S = Sched(nc, top)
        uq = [0]

        def sb(st, name, shape, dt_):
            uq[0] += 1
            return st.enter_context(nc.sbuf_tensor("sb%d_%s" % (uq[0], name), shape, dt_))

        def ps(st, name, shape, dt_):
            uq[0] += 1
            return st.enter_context(nc.psum_tensor("ps%d_%s" % (uq[0], name), shape, dt_))

        ident_bf = sb(top, "ident_bf", [128, 128], BF16)
        ident_f = sb(top, "ident_f", [128, 128], F32)
        ones_bf = sb(top, "ones_bf", [128, 128], BF16)
        epsb = sb(top, "epsb", [128, 1], F32)
        mid = ExitStack()
        neglam = sb(mid, "neglam", [128, 1], F32)
        E_all = sb(mid, "E_all", [128, 8, 128], BF16)
        W_BsT = sb(mid, "W_BsT", [128, 32, 128], BF16)
        W_BsTs = sb(mid, "W_BsTs", [128, 32, 128], BF16)
        W_CsA = sb(mid, "W_CsA", [128, 32, 128], BF16)
        W_CsB = sb(mid, "W_CsB", [128, 32, 128], BF16)
        W_T = sb(mid, "W_T", [128, 32, 128], BF16)
        rho2 = sb(mid, "rho2", [128, 32], F32)
        tau8 = sb(mid, "tau8", [128, 32], F32)
        mid2 = ExitStack()
        S1t = sb(mid2, "S1t", [128, 32, 33], F32)
        C1t = sb(mid2, "C1t", [128, 32, 33], F32)
        S0t = sb(mid2, "S0t", [128, 32, 32], F32)
        C0t = sb(mid2, "C0t", [128, 32, 32], F32)

        def load(eng, dst, src, key):
            return S.op(eng, I("dma_start", out=dst, in_=src), writes=[key], dma=True)

        with ExitStack() as ph:
            load("sync", ident_bf[:], T["ident_bf"], "ident_bf")
            load("sync", ident_f[:], T["ident_f"], "ident_f")
            S.op("vector", I("memset", ones_bf[:], 1.0), writes=["ones_bf"])
            S.op("vector", I("memset", epsb[:], EPS), writes=["epsb"])
            ones_f = sb(ph, "ones_f", [1, 128], F32)
            S.op("vector", I("memset", ones_f[:], 1.0), writes=["ones_f"])

            lamv = sb(ph, "lamv", [1, 256], F32)
            lprod = sb(ph, "lprod", [1, 128], F32)
            lsum = sb(ph, "lsum", [1, 2], F32)
            lexp = sb(ph, "lexp", [1, 2], F32)
            lval = sb(ph, "lval", [1, 1], F32)
            load("sync", lamv[:], T["lamv"], "lamv")
            lv = lamv[:].rearrange("p (a b c) -> p a b c", a=2, b=2)
            S.op("vector", I("tensor_tensor", out=lprod[:].rearrange("p (a c) -> p a c", a=2),
                                                     in0=lv[:, :, 0, :], in1=lv[:, :, 1, :], op=ALU.mult),
                 reads=["lamv"], writes=["lprod"])
            S.op("vector", I("tensor_reduce", out=lsum[:], in_=lprod[:].rearrange("p (a c) -> p a c", a=2),
                                                     axis=AX.X, op=ALU.add),
                 reads=["lprod"], writes=["lsum"])
            S.op("scalar", I("activation", out=lexp[:], in_=lsum[:], func=AF.Exp), reads=["lsum"], writes=["lexp"])
            S.op("vector", I("scalar_tensor_tensor", out=lval[:], in0=lexp[:, 1:2], scalar=-0.2, in1=lexp[:, 0:1],
                                                            op0=ALU.add, op1=ALU.subtract),
                 reads=["lexp"], writes=["lval"])
            pz = ps(ph, "pz0", [128, 512], F32)
            S.op("tensor", I("matmul", pz[:, 0:1], lhsT=ones_f[:], rhs=lval[:], start=True, stop=True),
                 reads=["ones_f", "lval"], writes=["pz0"])
            S.op("vector", I("tensor_copy", out=neglam[:], in_=pz[:, 0:1]), reads=["pz0"], writes=["neglam"])

            relb = sb(ph, "relb", [32, 4], F32)
            rb31 = sb(ph, "rb31", [4, 1], F32)
            nrb31 = sb(ph, "nrb31", [4, 1], F32)
            oh = sb(ph, "oh", [32, 383], F32)
            mvec = sb(ph, "mvec", [4, 383], F32)
            gvs = sb(ph, "gvs", [4, 383], F32)
            jrev = sb(ph, "jrev", [128, 128], F32)
            erev = sb(ph, "erev", [128, 8, 128], F32)
            load("sync", relb[:], T["rel_bias"], "relb")
            load("sync", rb31[:], T["rb31"], "rb31")
            load("sync", oh[:], T["oh"], "oh")
            load("sync", mvec[:], T["maskvec"], "mvec")
            load("sync", jrev[:], T["jrev"], "jrev")
            S.op("vector", I("tensor_scalar", out=nrb31[:], in0=rb31[:], scalar1=-1.0, scalar2=None, op0=ALU.mult),
                 reads=["rb31"], writes=["nrb31"])
            pz1 = ps(ph, "pz1", [128, 512], F32)
            S.op("tensor", I("matmul", pz1[0:4, 0:383], lhsT=relb[:], rhs=oh[:], start=True, stop=True),
                 reads=["relb", "oh"], writes=["pz1"])
            S.op("scalar", I("activation", out=gvs[:], in_=pz1[0:4, 0:383], func=AF.Exp, bias=nrb31[:, 0:1]),
                 reads=["pz1", "nrb31"], writes=["gvs"])
            S.op("vector", I("tensor_tensor", out=gvs[:], in0=gvs[:], in1=mvec[:], op=ALU.mult),
                 reads=["gvs", "mvec"], writes=["gvs"])
            S.op("gpsimd", I("dma_start", out=gv_d, in_=gvs[:]), reads=["gvs"], writes=["gv_d"], dma=True)
            for hh in range(4):
                for dd in range(2):
                    src = bass.AP(gv_d.tensor, hh * 383 + dd * 128, [[1, 128], [1, 128]])
                    S.op("sync", I("dma_start", out=erev[:, hh * 2 + dd, :], in_=src),
                         reads=["gv_d"], writes=["erev"], dma=True)
            pz2 = ps(ph, "pz2", [128, 512], F32)
            pz3 = ps(ph, "pz3", [128, 512], F32)
            erf = erev[:].rearrange("p a b -> p (a b)")
            eaf = E_all[:].rearrange("p a b -> p (a b)")
            for half, pzz in ((0, pz2), (1, pz3)):
                S.op("tensor", I("matmul", pzz[:, :], lhsT=jrev[:], rhs=erf[:, half * 512:(half + 1) * 512],
                                                                      start=True, stop=True),
                     reads=["jrev", "erev"], writes=["pzz%d" % half])
                S.op("vector", I("tensor_copy", out=eaf[:, half * 512:(half + 1) * 512], in_=pzz[:, :]),
                     reads=["pzz%d" % half], writes=["E_all"])

            areT = sb(ph, "areT", [128, 32], F32)
            aimT = sb(ph, "aimT", [128, 32], F32)
            ldt = sb(ph, "ldt", [128, 32], F32)
            sgn = sb(ph, "sgn", [128, 2], F32)
            tmask = sb(ph, "tmask", [128, 128], F32)
            dvec = sb(ph, "dvec", [128, 32], F32)
            bx2 = sb(ph, "bx2", [128, 32, 16], F32)
            bsw2 = sb(ph, "bsw2", [128, 32, 16], F32)
            ca = sb(ph, "ca", [128, 32, 16], F32)
            cbm = sb(ph, "cbm", [128, 32, 16], F32)
            load("sync", areT[:], T["areT2"], "areT")
            load("sync", aimT[:], T["aimT2"], "aimT")
            load("sync", ldt[:], T["logdt2"], "ldt")
            load("sync", sgn[:], T["sgn"], "sgn")
            load("sync", tmask[:], T["tmask"], "tmask")
            load("sync", dvec[:], T["dvec"], "dvec")
            load("sync", bx2[:].rearrange("p g c -> p (g c)"), T["bx2"], "bx2")
            load("sync", bsw2[:].rearrange("p g c -> p (g c)"), T["bsw2"], "bsw2")
            load("sync", ca[:].rearrange("p g c -> p (g c)"), T["ca"], "ca")
            load("sync", cbm[:].rearrange("p g c -> p (g c)"), T["cb"], "cbm")

            dtt = sb(ph, "dtt", [128, 32], F32)
            ar = sb(ph, "ar", [128, 32], F32)
            tau = sb(ph, "tau", [128, 32], F32)
            den = sb(ph, "den", [128, 32], F32)
            tmpa = sb(ph, "tmpa", [128, 32], F32)
            rden = sb(ph, "rden", [128, 32], F32)
            V = "vector"
            S.op("scalar", I("activation", out=dtt[:], in_=ldt[:], func=AF.Exp), reads=["ldt"], writes=["dtt"])
            S.op(V, I("tensor_tensor", out=ar[:], in0=areT[:], in1=dtt[:], op=ALU.mult), reads=["areT", "dtt"], writes=["ar"])
            S.op(V, I("scalar_tensor_tensor", out=tau[:], in0=aimT[:], scalar=INV_2PI, in1=dtt[:], op0=ALU.mult, op1=ALU.mult),
                 reads=["aimT", "dtt"], writes=["tau"])
            S.op(V, I("tensor_scalar", out=tau8[:], in0=tau[:], scalar1=8.0, scalar2=None, op0=ALU.mult),
                 reads=["tau"], writes=["tau8"])
            S.op("scalar", I("activation", out=rho2[:], in_=ar[:], func=AF.Exp, scale=8.0), reads=["ar"], writes=["rho2"])
            kv0 = sb(ph, "kv0", [128, 1025], F32)
            load("sync", kv0[:], T["kvec"], "kv0")
            RIs = sb(ph, "RIs", [128, 32, 33], I32)
            t8b33 = tau8[:].unsqueeze(2).broadcast_to([128, 32, 33])
            t8b32 = tau8[:].unsqueeze(2).broadcast_to([128, 32, 32])
            k1v = kv0[:, 0:1025:32].unsqueeze(1).broadcast_to([128, 32, 33])
            k0v = kv0[:, 0:32].unsqueeze(1).broadcast_to([128, 32, 32])
            for (dst_, kv_, tb_, n_, off_, key_) in ((S1t, k1v, t8b33, 33, 0.0, "S1t"), (C1t, k1v, t8b33, 33, 0.25, "C1t"),
                                                     (S0t, k0v, t8b32, 32, 0.0, "S0t"), (C0t, k0v, t8b32, 32, 0.25, "C0t")):
                S.op(V, I("tensor_tensor", out=dst_[:], in0=tb_, in1=kv_, op=ALU.mult), reads=["tau8", "kv0"], writes=[key_])
                if off_:
                    S.op(V, I("tensor_scalar", out=dst_[:], in0=dst_[:], scalar1=off_, scalar2=None, op0=ALU.add), reads=[key_], writes=[key_])
                S.op(V, I("tensor_copy", out=RIs[:, :, 0:n_], in_=dst_[:]), reads=[key_], writes=["RIs"])
                S.op(V, I("tensor_tensor", out=dst_[:], in0=dst_[:], in1=RIs[:, :, 0:n_], op=ALU.subtract), reads=[key_, "RIs"], writes=[key_])
                S.op("scalar", I("activation", out=dst_[:], in_=dst_[:], func=AF.Sin, scale=TWO_PI_S), reads=[key_], writes=[key_])
            S.op(V, I("tensor_tensor", out=den[:], in0=areT[:], in1=areT[:], op=ALU.mult), reads=["areT"], writes=["den"])
            S.op(V, I("tensor_tensor", out=tmpa[:], in0=aimT[:], in1=aimT[:], op=ALU.mult), reads=["aimT"], writes=["tmpa"])
            S.op(V, I("tensor_tensor", out=den[:], in0=den[:], in1=tmpa[:], op=ALU.add), reads=["den", "tmpa"], writes=["den"])
            S.op(V, I("reciprocal", out=rden[:], in_=den[:]), reads=["den"], writes=["rden"])

            MAG = sb(ph, "MAG", [128, NM, 32], F32)
            AS_ = sb(ph, "AS_", [128, NM, 32], F32)
            AC_ = sb(ph, "AC_", [128, NM, 32], F32)
            RI = sb(ph, "RI", [128, NM, 32], I32)
            SINT = sb(ph, "SINT", [128, NM, 32], F32)
            COST = sb(ph, "COST", [128, NM, 32], F32)
            PRE = sb(ph, "PRE", [128, NM, 32], F32)
            PIM = sb(ph, "PIM", [128, NM, 32], F32)
            mvt = sb(ph, "mvt", [128, NM], F32)
            load("sync", mvt[:], T["mvals"], "mvt")
            mb_ = mvt[:].unsqueeze(2).broadcast_to([128, NM, 32])
            S.op(V, I("tensor_tensor", out=AS_[:], in0=tau[:].unsqueeze(1).broadcast_to([128, NM, 32]), in1=mb_, op=ALU.mult),
                 reads=["tau", "mvt"], writes=["AS_"])
            S.op(V, I("tensor_scalar", out=AC_[:], in0=AS_[:], scalar1=0.25, scalar2=None, op0=ALU.add), reads=["AS_"], writes=["AC_"])
            S.op(V, I("tensor_tensor", out=MAG[:], in0=ar[:].unsqueeze(1).broadcast_to([128, NM, 32]), in1=mb_, op=ALU.mult),
                 reads=["ar", "mvt"], writes=["MAG"])
            S.op("scalar", I("activation", out=MAG[:], in_=MAG[:], func=AF.Exp), reads=["MAG"], writes=["MAG"])
            fl = lambda t: t[:].rearrange("p a b -> p (a b)")
            for A_, OUT, key in ((AS_, SINT, "SINT"), (AC_, COST, "COST")):
                akey = "AS_" if A_ is AS_ else "AC_"
                S.op(V, I("tensor_copy", out=fl(RI), in_=fl(A_)), reads=[akey], writes=["RI"])
                S.op(V, I("tensor_tensor", out=fl(A_), in0=fl(A_), in1=fl(RI), op=ALU.subtract),
                     reads=[akey, "RI"], writes=[akey])
                S.op("scalar", I("activation", out=fl(OUT), in_=fl(A_), func=AF.Sin, scale=TWO_PI_S),
                     reads=[akey], writes=[key])
            S.op(V, I("tensor_tensor", out=fl(PRE), in0=fl(MAG), in1=fl(COST), op=ALU.mult), reads=["MAG", "COST"], writes=["PRE"])
            S.op(V, I("tensor_tensor", out=fl(PIM), in0=fl(MAG), in1=fl(SINT), op=ALU.mult), reads=["MAG", "SINT"], writes=["PIM"])
            i1 = IDX_P + 1
            t0 = sb(ph, "c_t0", [128, 32], F32)
            t1 = sb(ph, "c_t1", [128, 32], F32)
            t2 = sb(ph, "c_t2", [128, 32], F32)
            cre = sb(ph, "cre", [128, 32], F32)
            cim = sb(ph, "cim", [128, 32], F32)
            S.op(V, I("tensor_scalar", out=t0[:], in0=PRE[:, i1, :], scalar1=-1.0, scalar2=None, op0=ALU.add), reads=["PRE"], writes=["c_t0"])
            S.op(V, I("tensor_tensor", out=t1[:], in0=t0[:], in1=areT[:], op=ALU.mult), reads=["c_t0", "areT"], writes=["c_t1"])
            S.op(V, I("tensor_tensor", out=t2[:], in0=PIM[:, i1, :], in1=aimT[:], op=ALU.mult), reads=["PIM", "aimT"], writes=["c_t2"])
            S.op(V, I("tensor_tensor", out=t1[:], in0=t1[:], in1=t2[:], op=ALU.add), reads=["c_t1", "c_t2"], writes=["c_t1"])
            S.op(V, I("tensor_tensor", out=cre[:], in0=t1[:], in1=rden[:], op=ALU.mult), reads=["c_t1", "rden"], writes=["cre"])
            S.op(V, I("tensor_tensor", out=t1[:], in0=PIM[:, i1, :], in1=areT[:], op=ALU.mult), reads=["PIM", "areT", "cre"], writes=["c_t1"])
            S.op(V, I("tensor_tensor", out=t2[:], in0=t0[:], in1=aimT[:], op=ALU.mult), reads=["c_t0", "aimT"], writes=["c_t2"])
            S.op(V, I("tensor_tensor", out=t1[:], in0=t1[:], in1=t2[:], op=ALU.subtract), reads=["c_t1", "c_t2"], writes=["c_t1"])
            S.op(V, I("tensor_tensor", out=cim[:], in0=t1[:], in1=rden[:], op=ALU.mult), reads=["c_t1", "rden"], writes=["cim"])
            QRE = sb(ph, "QRE", [128, 16, 32], F32)
            QIM = sb(ph, "QIM", [128, 16, 32], F32)
            QT1 = sb(ph, "QT1", [128, 16, 32], F32)
            cre_b = cre[:].unsqueeze(1).broadcast_to([128, 16, 32])
            cim_b = cim[:].unsqueeze(1).broadcast_to([128, 16, 32])
            S.op(V, I("tensor_tensor", out=QRE[:], in0=PRE[:, 0:16, :], in1=cre_b, op=ALU.mult), reads=["PRE", "cre"], writes=["QRE"])
            S.op(V, I("tensor_tensor", out=QT1[:], in0=PIM[:, 0:16, :], in1=cim_b, op=ALU.mult), reads=["PIM", "cim"], writes=["QT1"])
            S.op(V, I("tensor_tensor", out=QRE[:], in0=QRE[:], in1=QT1[:], op=ALU.subtract), reads=["QRE", "QT1"], writes=["QRE"])
            S.op(V, I("tensor_tensor", out=QIM[:], in0=PRE[:, 0:16, :], in1=cim_b, op=ALU.mult), reads=["PRE", "cim"], writes=["QIM"])
            S.op(V, I("tensor_tensor", out=QT1[:], in0=PIM[:, 0:16, :], in1=cre_b, op=ALU.mult), reads=["PIM", "cre", "QRE"], writes=["QT1"])
            S.op(V, I("tensor_tensor", out=QIM[:], in0=QIM[:], in1=QT1[:], op=ALU.add), reads=["QIM", "QT1"], writes=["QIM"])
            sT = sgn[:, 0:1]
            sB = sgn[:, 1:2]
            QIMsT = sb(ph, "QIMsT", [128, 16, 32], F32)
            QREsB = sb(ph, "QREsB", [128, 16, 32], F32)
            PREsB = sb(ph, "PREsB", [128, NM, 32], F32)
            PIMsT = sb(ph, "PIMsT", [128, NM, 32], F32)
            PREn = sb(ph, "PREn", [128, NM, 32], F32)
            PIMn = sb(ph, "PIMn", [128, NM, 32], F32)
            for (o_, i_, sc_, k_o, k_i) in ((QIMsT, QIM, sT, "QIMsT", "QIM"), (QREsB, QRE, sB, "QREsB", "QRE"),
                                            (PREsB, PRE, sB, "PREsB", "PRE"), (PIMsT, PIM, sT, "PIMsT", "PIM"),
                                            (PREn, PRE, -1.0, "PREn", "PRE"), (PIMn, PIM, -1.0, "PIMn", "PIM")):
                S.op(V, I("tensor_scalar", out=fl(o_), in0=fl(i_), scalar1=sc_, scalar2=None, op0=ALU.mult),
                     reads=[k_i, "sgn"], writes=[k_o])

            BIGA = sb(ph, "BIGA", [128, 32, 8, 16], F32)
            BIGB = sb(ph, "BIGB", [128, 32, 8, 16], F32)
            BIGT = sb(ph, "BIGT", [128, 32, 8, 16], F32)

            def big(out_t, okey, A, akey, a0, X, xkey, Bq, bkey, Y, ykey, eng):
                a_b = A[:, a0:a0 + 8, :].rearrange("p j g -> p g j").unsqueeze(3).broadcast_to([128, 32, 8, 16])
                b_b = Bq[:, a0:a0 + 8, :].rearrange("p j g -> p g j").unsqueeze(3).broadcast_to([128, 32, 8, 16])
                x_b = X[:].unsqueeze(2).broadcast_to([128, 32, 8, 16])
                y_b = Y[:].unsqueeze(2).broadcast_to([128, 32, 8, 16])
                for g0 in range(0, 32, 8):
                    gs = slice(g0, g0 + 8)
                    qk = "%s_q%d" % (okey, g0)
                    S.op(eng, I("tensor_tensor", out=out_t[:, gs], in0=a_b[:, gs], in1=x_b[:, gs], op=ALU.mult),
                         reads=[akey, xkey], writes=[qk, okey])
                    S.op(eng, I("tensor_tensor", out=BIGT[:, gs], in0=b_b[:, gs], in1=y_b[:, gs], op=ALU.mult),
                         reads=[bkey, ykey], writes=["BIGT_q%d" % g0])
                    S.op(eng, I("tensor_tensor", out=out_t[:, gs], in0=out_t[:, gs], in1=BIGT[:, gs], op=ALU.add),
                         reads=[qk, "BIGT_q%d" % g0], writes=[qk, okey])

            pT = [ps(ph, "pT%d" % i, [128, 512], F32) for i in range(4)]
            ptr = Ring(4)

            big(BIGA, "BIGA", PREsB, "PREsB", IDX_P + 1, ca, "ca", PIMn, "PIMn", cbm, "cbm", V)
            S.op("scalar", I("activation", out=W_CsA[:].rearrange("p g m -> p (g m)"), in_=BIGA[:].rearrange("p g j c -> p (g j c)"),
                                                  func=AF.Copy), reads=["BIGA"], writes=["W_CsA"])
            big(BIGB, "BIGB", PIMsT, "PIMsT", IDX_P + 1, ca, "ca", PREn, "PREn", cbm, "cbm", V)
            S.op("scalar", I("activation", out=W_CsB[:].rearrange("p g m -> p (g m)"), in_=BIGB[:].rearrange("p g j c -> p (g j c)"),
                                                  func=AF.Copy), reads=["BIGB"], writes=["W_CsB"])
            big(BIGA, "BIGA", QRE, "QRE", IDX_B, bx2, "bx2", QIMsT, "QIMsT", bsw2, "bsw2", V)
            for (src_t, skey, dst) in ((BIGA, "BIGA", W_BsT),):
                for g in range(32):
                    pi_ = ptr.next()
                    S.op("tensor", I("transpose", out=pT[pi_][:, 0:128],
                                                                                   in_=src_t[:, g].rearrange("p j c -> p (j c)"),
                                                                                   identity=ident_f[:]),
                         reads=[skey, "ident_f"], writes=["pT%d" % pi_])
                    if g % 2:
                        S.op("scalar", I("activation", out=dst[:, g, :], in_=pT[pi_][:, 0:128], func=AF.Copy),
                             reads=["pT%d" % pi_], writes=["W_BsT"])
                    else:
                        S.op("vector", I("tensor_copy", out=dst[:, g, :], in_=pT[pi_][:, 0:128]),
                             reads=["pT%d" % pi_], writes=["W_BsT"])
            big(BIGB, "BIGB", QREsB, "QREsB", IDX_B, bsw2, "bsw2", QIM, "QIM", bx2, "bx2", V)
            for g in range(32):
                pi_ = ptr.next()
                S.op("tensor", I("transpose", out=pT[pi_][:, 0:128], in_=BIGB[:, g].rearrange("p j c -> p (j c)"),
                                                                  identity=ident_f[:]),
                     reads=["BIGB", "ident_f"], writes=["pT%d" % pi_])
                if g % 2:
                    S.op("scalar", I("activation", out=W_BsTs[:, g, :], in_=pT[pi_][:, 0:128], func=AF.Copy),
                         reads=["pT%d" % pi_], writes=["W_BsTs"])
                else:
                    S.op("vector", I("tensor_copy", out=W_BsTs[:, g, :], in_=pT[pi_][:, 0:128]),
                         reads=["pT%d" % pi_], writes=["W_BsTs"])
            big(BIGA, "BIGA", QRE, "QRE", IDX_L, bx2, "bx2", QIMsT, "QIMsT", bsw2, "bsw2", V)
            big(BIGB, "BIGB", PREsB, "PREsB", IDX_P, ca, "ca", PIMn, "PIMn", cbm, "cbm", V)
            ttmp = sb(ph, "ttmp", [128, 2, 128], F32)
            tr2 = Ring(2)
            for g in range(32):
                pi_ = ptr.next()
                ti = tr2.next()
                S.op("tensor", I("matmul", pT[pi_][:, 0:128], lhsT=BIGA[:, g].rearrange("p j c -> p (j c)"),
                                                               rhs=BIGB[:, g].rearrange("p j c -> p (j c)"), start=True, stop=True),
                     reads=["BIGA", "BIGB"], writes=["pT%d" % pi_])
                S.op(V, I("tensor_tensor", out=ttmp[:, ti, :], in0=pT[pi_][:, 0:128], in1=tmask[:], op=ALU.mult),
                     reads=["pT%d" % pi_, "tmask"], writes=["ttmp%d" % ti])
                S.op(V, I("scalar_tensor_tensor", out=W_T[:, g, :], in0=ident_f[:], scalar=dvec[:, g:g + 1],
                                                                     in1=ttmp[:, ti, :], op0=ALU.mult, op1=ALU.add),
                     reads=["ident_f", "dvec", "ttmp%d" % ti], writes=["W_T"])
            S.flush()

        with ExitStack() as ph:
            win = sb(ph, "win", [128, 8, 2048], BF16)
            gmix = sb(ph, "gmix", [128, 8], F32)
            stg = [sb(ph, "stg%d" % i, [128, 2048], F32) for i in range(2)]
            load("sync", gmix[:], T["g_mix"], "gmix")
            w_in_v = T["w_in"].rearrange("(c p) n -> c p n", p=128)
            for c in range(8):
                si = c % 2
                load("sync", stg[si][:], w_in_v[c], "stg%d" % si)
                S.op("gpsimd" if c % 2 else "vector",
                     I("tensor_scalar", out=win[:, c, :], in0=stg[si][:], scalar1=gmix[:, c:c + 1], scalar2=1.0,
                                                           op0=ALU.mult, op1=ALU.mult),
                     reads=["stg%d" % si, "gmix"], writes=["win"])
            NXS = 6
            xs = [sb(ph, "xs%d" % i, [128, D], F32) for i in range(NXS)]
            xring = Ring(NXS)
            hb = [sb(ph, "hb%d" % i, [128, D], BF16) for i in range(8)]
            hring = Ring(8)
            junk = sb(ph, "junk", [128, D], BF16)
            ssq = sb(ph, "ssq", [128, 8], F32)
            sring = Ring(8)
            sdv = sb(ph, "sdv", [128, 8], F32)
            rsd = sb(ph, "rsd", [128, 8], F32)
            hT = [sb(ph, "hT%d" % i, [128, 8, 512], BF16) for i in range(2)]
            ost = [sb(ph, "ost%d" % i, [128, 512], BF16) for i in range(6)]
            oring = Ring(6)
            ptp = [ps(ph, "ptp%d" % i, [128, 1024], BF16) for i in range(2)]
            tpr = Ring(2)
            pmm = [ps(ph, "pmm%d" % i, [128, 512], F32) for i in range(4)]
            mring = Ring(4)
            x_v = T["x"].rearrange("(t p) d -> t p d", p=128)
            ev = [0]

            def evac(dst, src, rkeys, wkeys):
                ev[0] += 1
                if ev[0] % 2:
                    S.op("vector", I("tensor_copy", out=dst, in_=src), reads=rkeys, writes=wkeys)
                    return "vector"
                S.op("scalar", I("activation", out=dst, in_=src, func=AF.Copy), reads=rkeys, writes=wkeys)
                return "scalar"


            hb_of = {}

            def prepA(blk):
                his = []
                for i in range(4):
                    tix = blk * 4 + i
                    xi = xring.next()
                    load("sync", xs[xi][:], x_v[tix], "xs%d" % xi)
                    si = sring.next()
                    S.op("scalar", I("activation", out=junk[:], in_=xs[xi][:], func=AF.Square, accum_out=ssq[:, si:si + 1]),
                         reads=["xs%d" % xi], writes=["junk", "ssq%d" % si])
                    S.op("scalar", I("activation", out=sdv[:, si:si + 1], in_=ssq[:, si:si + 1], func=AF.Sqrt, scale=1.0 / D, bias=epsb[:, 0:1]),
                         reads=["ssq%d" % si, "epsb"], writes=["sdv%d" % si])
                    S.op("vector", I("reciprocal", out=rsd[:, si:si + 1], in_=sdv[:, si:si + 1]),
                         reads=["sdv%d" % si], writes=["rsd%d" % si])
                    hi = hring.next()
                    his.append(hi)
                    S.op("gpsimd", I("tensor_scalar", out=hb[hi][:], in0=xs[xi][:], scalar1=rsd[:, si:si + 1], scalar2=1.0,
                                     op0=ALU.mult, op1=ALU.mult),
                         reads=["xs%d" % xi, "rsd%d" % si], writes=["hb%d" % hi])
                hb_of[blk] = his

            def prepB(blk):
                hTi = blk % 2
                for i in range(4):
                    hi = hb_of[blk][i]
                    ti = tpr.next()
                    for c in range(8):
                        S.op("tensor", I("transpose", out=ptp[ti][:, c * 128:(c + 1) * 128], in_=hb[hi][:, c * 128:(c + 1) * 128],
                                         identity=ident_bf[:]),
                             reads=["hb%d" % hi, "ident_bf"], writes=["ptp%d" % ti], accum=True)
                    evac(hT[hTi][:, :, i * 128:(i + 1) * 128], ptp[ti][:].rearrange("p (c t) -> p c t", c=8),
                         ["ptp%d" % ti], ["hT%d" % hTi])

            prepA(0)
            prepB(0)
            prepA(1)
            for blk in range(16):
                hTi = blk % 2
                if blk + 2 < 16:
                    prepA(blk + 2)
                tsl = slice(blk * 512, (blk + 1) * 512)
                for oc in range(12):
                    col0 = oc * 128 if oc < 8 else 1536 + (oc - 8) * 128
                    mi = mring.next()
                    for c in range(8):
                        S.op("tensor", I("matmul", pmm[mi][:, :], lhsT=win[:, c, col0:col0 + 128],
                                                                                 rhs=hT[hTi][:, c, :], start=(c == 0), stop=(c == 7)),
                             reads=["win", "hT%d" % hTi], writes=["pmm%d" % mi], accum=True)
                    oi = oring.next()
                    if oc < 8:
                        se = evac(ost[oi][:], pmm[mi][:, :], ["pmm%d" % mi], ["ost%d" % oi])
                    else:
                        se = evac(ost[oi][:].rearrange("p (j k) -> p j k", j=8), pmm[mi][:, :].rearrange("p (k j) -> p j k", j=8),
                                  ["pmm%d" % mi], ["ost%d" % oi])
                    if oc < 4:
                        dst = qT_d[oc * 128:(oc + 1) * 128, tsl]
                        dk = "qT_d"
                    elif oc < 8:
                        dst = kT_d[(oc - 4) * 128:(oc - 3) * 128, tsl]
                        dk = "kT_d"
                    else:
                        dst = uT_d[(oc - 8) * 128:(oc - 7) * 128, :, blk * 64:(blk + 1) * 64]
                        dk = "uT_d"
                    src_ = ost[oi][:] if oc < 8 else ost[oi][:].rearrange("p (j k) -> p j k", j=8)
                    S.op("sync" if se == "vector" else se, I("dma_start", out=dst, in_=src_), reads=["ost%d" % oi], writes=[dk], dma=True)
                for i in range(4):
                    mi = mring.next()
                    for c in range(8):
                        S.op("tensor", I("matmul", pmm[mi][:, :], lhsT=hT[hTi][:, c, i * 128:(i + 1) * 128],
                                                                           rhs=win[:, c, 1024:1536], start=(c == 0), stop=(c == 7)),
                             reads=["win", "hT%d" % hTi], writes=["pmm%d" % mi], accum=True)
                    oi = oring.next()
                    se = evac(ost[oi][:], pmm[mi][:, :], ["pmm%d" % mi], ["ost%d" % oi])
                    r0 = blk * 512 + i * 128
                    S.op("sync" if se == "vector" else se, I("dma_start", out=v_d[r0:r0 + 128, :], in_=ost[oi][:]),
                         reads=["ost%d" % oi], writes=["v_d"], dma=True)
                if blk + 1 < 16:
                    prepB(blk + 1)
            S.flush()

        with ExitStack() as ph:
            KT = [sb(ph, "KT%d" % i, [128, S_LEN], BF16) for i in range(2)]
            QZ = [sb(ph, "QZ%d" % i, [128, S_LEN], BF16) for i in range(2)]
            S.op("vector", I("memset", QZ[0][64:128, :], 0.0), writes=["QZ"])
            S.op("gpsimd", I("memset", QZ[1][0:64, :], 0.0), writes=["QZ"])
            VA = [sb(ph, "VA%d" % i, [128, 64, 130], BF16) for i in range(2)]
            NPT = 3
            PT = [[sb(ph, "PT%d_%d" % (c, i), [128, 512], BF16) for i in range(NPT)] for c in range(2)]
            ptr_ = Ring(NPT)
            SP = [[ps(ph, "SP%d_%d" % (c, i), [128, 512], F32) for i in range(2)] for c in range(2)]
            OA = ps(ph, "OA", [128, 512], F32)
            OB = ps(ph, "OB", [128, 512], F32)
            OC = ps(ph, "OC", [128, 512], F32)
            PTR = ps(ph, "PTR", [128, 1024], BF16)
            rc = sb(ph, "rc", [128, 8], F32)
            o2 = sb(ph, "o2", [128, 128], F32)
            oo4 = sb(ph, "oo4", [128, 4, 128], F32)
            ojunk = sb(ph, "ojunk", [128, 128], F32)
            ass = sb(ph, "ass", [128, 8], F32)
            attb2 = [sb(ph, "attb%d" % i, [128, 128], BF16) for i in range(2)]
            aTs = [sb(ph, "aTs%d" % i, [128, 512], BF16) for i in range(2)]
            for i in range(2):
                S.op("vector", I("memset", VA[i][:, :, 128:130], 1.0), writes=["VA%d" % i])

            def acc_ap(c, r):
                if r < 3:
                    return (OA if c == 0 else OB)[:, r * 129:(r + 1) * 129], ("OA" if c == 0 else "OB")
                return OC[:, c * 129:(c + 1) * 129], "OC"

            v_v = v_d.rearrange("(t p) d -> p t d", p=128)
            gffn = sb(ph, "gffn", [128, 8], F32)
            gsub = sb(ph, "gsub", [128, 1], F32)
            gssm = sb(ph, "gssm", [128, 4], F32)
            gout = sb(ph, "gout", [128, 8], F32)
            load("sync", gffn[:], T["g_ffn"], "gffn")
            load("sync", gsub[:], T["g_sub"], "gsub")
            load("sync", gssm[:], T["g_ssm"], "gssm")
            for c in range(4):
                S.op("vector", I("tensor_scalar", out=gout[:, c:c + 1], in0=gsub[:], scalar1=0.8, scalar2=None, op0=ALU.mult),
                     reads=["gsub"], writes=["gout"])
            S.op("vector", I("tensor_copy", out=gout[:, 4:8], in_=gssm[:]), reads=["gssm"], writes=["gout"])
            cst = [sb(ph, "cst%d" % i, [128, DFF], F32) for i in range(1)]
            cob = [sb(ph, "cob%d" % i, [128, DFF], BF16) for i in range(1)]
            osin = sb(ph, "osin", [128, 1025], F32)
            ocos = sb(ph, "ocos", [128, 1025], F32)
            tmpA = sb(ph, "tmpA", [128, 32, 32], F32)

            def gen_tab(g):
                s1 = S1t[:, g, 0:32].unsqueeze(2).broadcast_to([128, 32, 32])
                c1 = C1t[:, g, 0:32].unsqueeze(2).broadcast_to([128, 32, 32])
                s0 = S0t[:, g, :].unsqueeze(1).broadcast_to([128, 32, 32])
                c0 = C0t[:, g, :].unsqueeze(1).broadcast_to([128, 32, 32])
                osv = osin[:, 0:1024].rearrange("p (a b) -> p a b", b=32)
                ocv = ocos[:, 0:1024].rearrange("p (a b) -> p a b", b=32)
                P_ = "gpsimd"
                rk = ["S1t", "C1t", "S0t", "C0t"]
                S.op(P_, I("tensor_tensor", out=osv, in0=s1, in1=c0, op=ALU.mult), reads=rk, writes=["osin"])
                S.op(P_, I("tensor_tensor", out=tmpA[:], in0=c1, in1=s0, op=ALU.mult), reads=rk, writes=["tmpA"])
                S.op(P_, I("tensor_tensor", out=osv, in0=osv, in1=tmpA[:], op=ALU.add), reads=["osin", "tmpA"], writes=["osin"])
                S.op(P_, I("tensor_copy", out=osin[:, 1024:1025], in_=S1t[:, g, 32:33]), reads=rk, writes=["osin"])
                S.op(P_, I("dma_start", out=tab_d[g, 0], in_=osin[:]), reads=["osin"], writes=["tab_d"], dma=True)
                S.op(P_, I("tensor_tensor", out=ocv, in0=c1, in1=c0, op=ALU.mult), reads=rk, writes=["ocos"])
                S.op(P_, I("tensor_tensor", out=tmpA[:], in0=s1, in1=s0, op=ALU.mult), reads=rk, writes=["tmpA"])
                S.op(P_, I("tensor_tensor", out=ocv, in0=ocv, in1=tmpA[:], op=ALU.subtract), reads=["ocos", "tmpA"], writes=["ocos"])
                S.op(P_, I("tensor_copy", out=ocos[:, 1024:1025], in_=C1t[:, g, 32:33]), reads=rk, writes=["ocos"])
                S.op(P_, I("dma_start", out=tab_d[g, 1], in_=ocos[:]), reads=["ocos"], writes=["tab_d"], dma=True)

            cjobs = []
            for (src, dst, ncols, gain, nch) in ((T["w_out"], wout_b, D, gout, 8), (T["w_gate"], wg_b, DFF, gffn, 8),
                                                 (T["w_up"], wu_b, DFF, gffn, 8), (T["w_down"], wd_b, D, None, NFC)):
                sv = src.rearrange("(c p) n -> c p n", p=128)
                dv = dst.rearrange("(c p) n -> c p n", p=128)
                for c in range(nch):
                    cjobs.append((sv[c], dv[c], ncols, gain, c))
            cj = [0]

            def conv_job():
                if cj[0] >= len(cjobs):
                    return
                src, dst, ncols, gain, c = cjobs[cj[0]]
                k = 0
                cj[0] += 1
                load("sync", cst[k][:, 0:ncols], src, "cst%d" % k)
                if gain is None:
                    S.op("gpsimd", I("tensor_copy", out=cob[k][:, 0:ncols], in_=cst[k][:, 0:ncols]), reads=["cst%d" % k], writes=["cob%d" % k])
                else:
                    S.op("gpsimd", I("tensor_scalar", out=cob[k][:, 0:ncols], in0=cst[k][:, 0:ncols], scalar1=gain[:, c:c + 1], scalar2=1.0,
                                     op0=ALU.mult, op1=ALU.mult), reads=["cst%d" % k, "gout", "gffn"], writes=["cob%d" % k])
                S.op("gpsimd", I("dma_start", out=dst, in_=cob[k][:, 0:ncols]), reads=["cob%d" % k], writes=["wconv_d"], dma=True)

            def head_loads(h):
                bi = h % 2
                load("sync", KT[bi][:], kT_d[h * 128:(h + 1) * 128, :], "KT%d" % bi)
                for t0 in range(0, 64, 16):
                    S.op("sync", I("dma_start", out=VA[bi][:, t0:t0 + 16, 0:128],
                                                                          in_=v_v[:, t0:t0 + 16, h * 128:(h + 1) * 128]),
                         reads=["v_d"], writes=["VA%d" % bi], dma=True)

            head_loads(0)
            for h in range(4):
                bi = h % 2
                if h + 1 < 4:
                    head_loads(h + 1)
                S.op("sync", I("dma_start", out=QZ[0][0:64, :], in_=qT_d[h * 128:h * 128 + 64, :]), writes=["QZ"], dma=True)
                S.op("sync", I("dma_start", out=QZ[1][64:128, :], in_=qT_d[h * 128 + 64:(h + 1) * 128, :]), writes=["QZ"], dma=True)
                kq = ["KT%d" % bi, "QZ"]
                iters = [(jb, kt) for jb in range(16) for kt in range(4 * jb + 4)]

                def emit_qk(jb, kt, slot):
                    m = kt - 4 * jb
                    c0 = 128 * max(m, 0)
                    for c in range(2):
                        S.op("tensor", I("matmul", SP[c][slot][:, c0:512], lhsT=KT[bi][:, kt * 128:(kt + 1) * 128],
                            rhs=QZ[c][:, jb * 512 + c0:(jb + 1) * 512], start=True, stop=True),
                             reads=kq, writes=["SP%d_%d" % (c, slot)])

                emit_qk(iters[0][0], iters[0][1], 0)
                started = set()
                for it, (jb, kt) in enumerate(iters):
                    slot = it % 2
                    if it % 40 == 20:
                        conv_job()
                    gi_ = h * len(iters) + it
                    if gi_ % 68 == 10:
                        gen_tab(gi_ // 68)
                    if it + 1 < len(iters):
                        emit_qk(iters[it + 1][0], iters[it + 1][1], (it + 1) % 2)
                    m = kt - 4 * jb
                    c0 = 128 * max(m, 0)
                    pi_ = ptr_.next()
                    for c in range(2):
                        S.op("scalar", I("activation", out=PT[c][pi_][:, c0:512], in_=SP[c][slot][:, c0:512], func=AF.Exp, scale=0.125),
                             reads=["SP%d_%d" % (c, slot)], writes=["PT%d_%d" % (c, pi_)])
                    for r in range(4):
                        dl = 4 * jb + r - kt
                        if dl in (0, 1):
                            for c in range(2):
                                S.op("vector", I("tensor_tensor", out=PT[c][pi_][:, r * 128:(r + 1) * 128], in0=PT[c][pi_][:, r * 128:(r + 1) * 128],
                                                 in1=E_all[:, h * 2 + dl, :], op=ALU.mult),
                                     reads=["PT%d_%d" % (c, pi_), "E_all"], writes=["PT%d_%d" % (c, pi_)])
                    if kt == 0:
                        started = set()
                    for c in range(2):
                        for r in range(max(m, 0), 4):
                            ap_, key = acc_ap(c, r)
                            first = key not in started
                            started.add(key)
                            S.op("tensor", I("matmul", ap_, lhsT=PT[c][pi_][:, r * 128:(r + 1) * 128], rhs=VA[bi][:, kt, 0:129],
                                start=first, stop=(kt == 4 * jb + r), skip_group_check=True),
                                 reads=["PT%d_%d" % (c, pi_), "VA%d" % bi], writes=[key], accum=True)
                    if kt == 4 * jb + 3:
                        asi = jb % 2
                        for r in range(4):
                            a1, k1 = acc_ap(0, r)
                            a2, k2 = acc_ap(1, r)
                            S.op("vector", I("reciprocal", out=rc[:, 0:1], in_=a1[:, 128:129]), reads=[k1], writes=["rc0"])
                            S.op("vector", I("reciprocal", out=rc[:, 1:2], in_=a2[:, 128:129]), reads=[k2], writes=["rc1"])
                            S.op("vector", I("tensor_tensor", out=rc[:, 2:3], in0=rc[:, 1:2], in1=neglam[:], op=ALU.mult),
                                 reads=["rc1", "neglam"], writes=["rc2"])
                            S.op("vector", I("tensor_scalar", out=o2[:], in0=a2[:, 0:128], scalar1=rc[:, 2:3], scalar2=None, op0=ALU.mult),
                                 reads=[k2, "rc2"], writes=["o2"])
                            S.op("vector", I("scalar_tensor_tensor", out=oo4[:, r, :], in0=a1[:, 0:128], scalar=rc[:, 0:1], in1=o2[:],
                                             op0=ALU.mult, op1=ALU.add),
                                 reads=[k1, "rc0", "o2"], writes=["oo%d" % r])
                            S.op("vector", I("scalar_tensor_tensor", out=ojunk[:], in0=oo4[:, r, :], scalar=1.0, in1=oo4[:, r, :],
                                             op0=ALU.mult, op1=ALU.mult, accum_out=ass[:, r:r + 1]),
                                 reads=["oo%d" % r], writes=["ojunk", "ass_s%d" % r])
                        S.op("scalar", I("activation", out=ass[:, 4:8], in_=ass[:, 0:4], func=AF.Ln, scale=1.0 / 128, bias=epsb[:, 0:1]),
                             reads=["ass_s0", "ass_s1", "ass_s2", "ass_s3", "epsb"], writes=["ass_l"])
                        S.op("scalar", I("activation", out=rc[:, 4:8], in_=ass[:, 4:8], func=AF.Exp, scale=-0.5), reads=["ass_l"], writes=["rc_r"])
                        for r in range(4):
                            S.op("vector", I("tensor_scalar", out=attb2[r % 2][:], in0=oo4[:, r, :], scalar1=rc[:, 4 + r:5 + r], scalar2=None, op0=ALU.mult),
                                 reads=["oo%d" % r, "rc_r"], writes=["attb%d" % (r % 2)])
                            S.op("tensor", I("transpose", out=PTR[:, r * 128:(r + 1) * 128], in_=attb2[r % 2][:], identity=ident_bf[:]),
                                 reads=["attb%d" % (r % 2), "ident_bf"], writes=["PTR"], accum=True)
                        S.op("vector", I("tensor_copy", out=aTs[asi][:], in_=PTR[:, 0:512]), reads=["PTR"], writes=["aTs%d" % asi])
                        S.op("gpsimd", I("dma_start", out=catT_d[h * 128:(h + 1) * 128, jb * 512:(jb + 1) * 512],
                                                                                  in_=aTs[asi][:]),
                             reads=["aTs%d" % asi], writes=["catT_d"], dma=True)
            while cj[0] < len(cjobs):
                conv_job()
            S.flush()

        mid2.close()
        with ExitStack() as ph:
            sel = sb(ph, "sel", [128, 64, 128], BF16)
            load("sync", sel[:].rearrange("p a b -> p (a b)"), T["sel"].rearrange("p a b -> p (a b)"), "sel")
            uT = sb(ph, "uT", [128, 8, 1024], BF16)
            Gs = [sb(ph, "Gs%d" % i, [128, 1024], BF16) for i in range(8)]
            gTn = sb(ph, "gTn", [128, S_LEN], BF16)
            NTB = 3
            COS = [sb(ph, "COS%d" % i, [128, 1025], F32) for i in range(NTB)]
            SIN = [sb(ph, "SIN%d" % i, [128, 1025], F32) for i in range(NTB)]
            Sb = [sb(ph, "Sb%d" % i, [128, 1025], F32) for i in range(NTB)]
            t1b = [sb(ph, "t1b%d" % i, [128, 512], F32) for i in range(2)]
            t2b = [sb(ph, "t2b%d" % i, [128, 512], F32) for i in range(2)]
            vmb = [sb(ph, "vmb%d" % i, [128, 512], F32) for i in range(3)]
            wcb = [sb(ph, "wcb%d" % i, [128, 512], BF16) for i in range(3)]
            wsb = [sb(ph, "wsb%d" % i, [128, 512], BF16) for i in range(3)]
            usb = [sb(ph, "usb%d" % i, [128, 512], BF16) for i in range(4)]
            pU = [ps(ph, "pU%d" % i, [128, 512], F32) for i in range(2)]
            pV = [ps(ph, "pV%d" % i, [128, 512], F32) for i in range(2)]
            pVs = [ps(ph, "pVs%d" % i, [128, 512], F32) for i in range(2)]
            pY = [ps(ph, "pY%d" % i, [128, 512], F32) for i in range(2)]
            for i in range(NTB):
                S.op("vector", I("memset", Sb[i][:, 0:1], 0.0), writes=["Sb%d_0" % i])
            NU = 64
            pur = Ring(2)

            def tables(g):
                tb = g % NTB
                load("sync", SIN[tb][:], tab_d[g, 0], "SIN%d" % tb)
                load("sync", COS[tb][:], tab_d[g, 1], "COS%d" % tb)

            def stA(u):
                g, hf = u // 2, u % 2
                cc, g8 = g // 8, g % 8
                if u % 16 == 0:
                    load("sync", uT[:], uT_d[cc * 128:(cc + 1) * 128, :, :], "uT")
                if hf == 0:
                    tables(g)
                pi_ = pur.next()
                for j in range(8):
                    S.op("tensor", I("matmul", pU[pi_][:, :], lhsT=sel[:, g8 * 8 + j, :],
                                     rhs=uT[:, j, hf * 512:(hf + 1) * 512],
                                     start=(j == 0), stop=(j == 7)),
                         reads=["sel", "uT"], writes=["pU%d" % pi_], accum=True)
                S.op("scalar", I("activation", out=usb[u % 4][:], in_=pU[pi_][:, :], func=AF.Copy), reads=["pU%d" % pi_], writes=["usb%d" % (u % 4)])

            def stB(u):
                g, hf = u // 2, u % 2
                tb, k0, p2, uu = g % NTB, hf * 512, u % 2, "usb%d" % (u % 4)
                S.op("tensor", I("matmul", pV[p2][:, :], lhsT=W_BsT[:, g, :], rhs=usb[u % 4][:], start=True, stop=True),
                     reads=["W_BsT", uu], writes=["pV%d" % p2])
                S.op("tensor", I("matmul", pVs[p2][:, :], lhsT=W_BsTs[:, g, :], rhs=usb[u % 4][:], start=True, stop=True),
                     reads=["W_BsTs", uu], writes=["pVs%d" % p2])
                S.op("vector", I("tensor_tensor", out=t1b[p2][:], in0=pV[p2][:, :], in1=COS[tb][:, 1 + k0:513 + k0], op=ALU.mult),
                     reads=["pV%d" % p2, "COS%d" % tb], writes=["t1b%d" % p2])
                S.op("vector", I("tensor_tensor", out=t2b[p2][:], in0=pVs[p2][:, :], in1=SIN[tb][:, 1 + k0:513 + k0], op=ALU.mult),
                     reads=["pVs%d" % p2, "SIN%d" % tb], writes=["t2b%d" % p2])
                S.op("gpsimd", I("tensor_tensor", out=vmb[u % 3][:], in0=t1b[p2][:], in1=t2b[p2][:], op=ALU.add),
                     reads=["t1b%d" % p2, "t2b%d" % p2], writes=["vmb%d" % (u % 3)])

            def stC(u):
                g, hf = u // 2, u % 2
                tb, k0, p3 = g % NTB, hf * 512, u % 3
                prevk = "Sb%d_%d" % (tb, hf)
                S.op("vector", I("tensor_tensor_scan", out=Sb[tb][:, 1 + k0:513 + k0], data0=rho2[:, g:g + 1].broadcast_to([128, 512]),
                                 data1=vmb[p3][:], initial=Sb[tb][:, k0:k0 + 1], op0=ALU.mult, op1=ALU.add),
                     reads=["rho2", "vmb%d" % p3, prevk], writes=["Sb%d_%d" % (tb, hf + 1)])
                rk = ["Sb%d_%d" % (tb, hf + 1), prevk]
                S.op("gpsimd", I("tensor_tensor", out=wcb[p3][:], in0=Sb[tb][:, k0:k0 + 512], in1=COS[tb][:, k0:k0 + 512], op=ALU.mult),
                     reads=rk + ["COS%d" % tb], writes=["wcb%d" % p3])
                S.op("vector", I("tensor_tensor", out=wsb[p3][:], in0=Sb[tb][:, k0:k0 + 512], in1=SIN[tb][:, k0:k0 + 512], op=ALU.mult),
                     reads=rk + ["SIN%d" % tb], writes=["wsb%d" % p3])

            def stD(u):
                g, hf = u // 2, u % 2
                g8, k0, p2, p3, uu = g % 8, hf * 512, u % 2, u % 3, "usb%d" % (u % 4)
                S.op("tensor", I("matmul", pY[p2][:, :], lhsT=W_T[:, g, :], rhs=usb[u % 4][:], start=True, stop=False),
                     reads=["W_T", uu], writes=["pY%d" % p2], accum=True)
                S.op("tensor", I("matmul", pY[p2][:, :], lhsT=W_CsA[:, g, :], rhs=wcb[p3][:], start=False, stop=False),
                     reads=["W_CsA", "wcb%d" % p3], writes=["pY%d" % p2], accum=True)
                S.op("tensor", I("matmul", pY[p2][:, :], lhsT=W_CsB[:, g, :], rhs=wsb[p3][:], start=False, stop=True),
                     reads=["W_CsB", "wsb%d" % p3], writes=["pY%d" % p2], accum=True)
                S.op("scalar", I("activation", out=Gs[g8][:, k0:k0 + 512], in_=pY[p2][:, :], func=AF.Gelu_apprx_tanh),
                     reads=["pY%d" % p2], writes=["Gs%d_%d" % (g8, hf)])

            def unshuffle(cc):
                for hf in range(2):
                    for i in range(8):
                        pi_ = pur.next()
                        for g8 in range(8):
                            S.op("tensor", I("matmul", pU[pi_][:, :], lhsT=sel[:, i * 8 + g8, :], rhs=Gs[g8][:, hf * 512:(hf + 1) * 512],
                                             start=(g8 == 0), stop=(g8 == 7)),
                                 reads=["sel", "Gs%d_%d" % (g8, hf)], writes=["pU%d" % pi_], accum=True)
                        dst = gTn[:, hf * 4096:(hf + 1) * 4096].rearrange("p (k j) -> p j k", j=8)[:, i, :]
                        if i % 2:
                            S.op("scalar", I("activation", out=dst, in_=pU[pi_][:, :], func=AF.Copy), reads=["pU%d" % pi_], writes=["gTn"])
                        else:
                            S.op("vector", I("tensor_copy", out=dst, in_=pU[pi_][:, :]), reads=["pU%d" % pi_], writes=["gTn"])
                S.op("gpsimd", I("dma_start", out=gT_d[cc * 128:(cc + 1) * 128, :], in_=gTn[:]), reads=["gTn"], writes=["gT_d"], dma=True)

            for step in range(NU + 3):
                if 0 <= step - 3 < NU:
                    stD(step - 3)
                    if (step - 3) % 16 == 15:
                        unshuffle((step - 3) // 16)
                if 0 <= step - 2 < NU:
                    stC(step - 2)
                if 0 <= step - 1 < NU:
                    stB(step - 1)
                if step < NU:
                    stA(step)
            S.flush()

        with ExitStack() as ph:
            wglu = sb(ph, "wglu", [128, 4, 512], BF16)
            bglu = sb(ph, "bglu", [128, 4], F32)
            stg = sb(ph, "stgg", [128, 4, 512], F32)
            load("sync", bglu[:], T["b_glu"], "bglu")
            load("sync", stg[:], T["w_glu"].rearrange("(c p) n -> p c n", p=128), "stgg")
            S.op("vector", I("tensor_copy", out=wglu[:], in_=stg[:]), reads=["stgg"], writes=["wglu"])
            gb = [sb(ph, "gb%d" % i, [128, 4, 512], BF16) for i in range(2)]
            sg = [sb(ph, "sg%d" % i, [128, 512], BF16) for i in range(2)]
            spre = sb(ph, "spre", [128, 4, 512], BF16)
            sq = sb(ph, "sq", [128, 4, 512], BF16)
            sdt = sb(ph, "sdt", [128, 512], F32)
            rst = sb(ph, "rst", [128, 512], F32)
            sso = [sb(ph, "sso%d" % i, [128, 4, 512], BF16) for i in range(2)]
            pG = [ps(ph, "pG%d" % i, [128, 512], F32) for i in range(2)]
            pS = ps(ph, "pS", [128, 512], F32)
            gT_v = gT_d.rearrange("(c p) t -> p c t", p=128)
            cat_v = catT_d[512:1024, :].rearrange("(c p) t -> p c t", p=128)
            for blk in range(16):
                bi = blk % 2
                tsl = slice(blk * 512, (blk + 1) * 512)
                load("sync", gb[bi][:], gT_v[:, :, tsl], "gb%d" % bi)
                for co in range(4):
                    pi_ = co % 2
                    for ci in range(4):
                        S.op("tensor", I("matmul", pG[pi_][:, :], lhsT=wglu[:, ci, co * 128:(co + 1) * 128],
                                                                                        rhs=gb[bi][:, ci, :], start=(ci == 0), stop=(ci == 3)),
                             reads=["wglu", "gb%d" % bi], writes=["pG%d" % pi_], accum=True)
                    S.op("scalar", I("activation", out=sg[pi_][:], in_=pG[pi_][:, :], func=AF.Sigmoid, bias=bglu[:, co:co + 1]),
                         reads=["pG%d" % pi_, "bglu"], writes=["sg%d" % pi_])
                    S.op("vector", I("tensor_tensor", out=spre[:, co, :], in0=gb[bi][:, co, :], in1=sg[pi_][:], op=ALU.mult),
                         reads=["gb%d" % bi, "sg%d" % pi_], writes=["spre%d" % co])
                    S.op("vector", I("tensor_tensor", out=sq[:, co, :], in0=spre[:, co, :], in1=spre[:, co, :], op=ALU.mult),
                         reads=["spre%d" % co], writes=["sq%d" % co])
                for co in range(4):
                    S.op("tensor", I("matmul", pS[:, :], lhsT=ones_bf[:], rhs=sq[:, co, :], start=(co == 0), stop=(co == 3)),
                         reads=["ones_bf", "sq%d" % co], writes=["pS"], accum=True)
                S.op("scalar", I("activation", out=sdt[:], in_=pS[:, :], func=AF.Ln, scale=1.0 / 512, bias=epsb[:, 0:1]),
                     reads=["pS", "epsb"], writes=["sdt"])
                S.op("scalar", I("activation", out=rst[:], in_=sdt[:], func=AF.Exp, scale=-0.5), reads=["sdt"], writes=["rst"])
                for co in range(4):
                    S.op("gpsimd" if co % 2 else "vector",
                         I("tensor_tensor", out=sso[bi][:, co, :], in0=spre[:, co, :], in1=rst[:], op=ALU.mult),
                         reads=["spre%d" % co, "rst"], writes=["sso%d" % bi])
                S.op("gpsimd", I("dma_start", out=cat_v[:, :, tsl], in_=sso[bi][:]),
                     reads=["sso%d" % bi], writes=["catT_d"], dma=True)
            S.flush()

        mid.close()
        with ExitStack() as ph:
            wout = sb(ph, "wout", [128, 8, D], BF16)
            wg = sb(ph, "wg", [128, 8, DFF], BF16)
            wu = sb(ph, "wu", [128, 8, DFF], BF16)
            wd = sb(ph, "wd", [128, NFC, D], BF16)
            gfin = sb(ph, "gfin", [128, D], F32)
            aT = sb(ph, "aT", [128, NFC, 256], BF16)
            load("sync", gfin[:], T["g_fin"], "gfin")
            load("sync", wout[:], wout_b.rearrange("(c p) n -> p c n", p=128), "wout")
            FQ = [0, 6, 12, 17, 22]
            fq_of = lambda fc: max(q for q in range(4) if FQ[q] <= fc)
            for q in range(4):
                cs = slice(FQ[q] * 128, FQ[q + 1] * 128)
                load("scalar", wg[:, :, cs], wg_b.rearrange("(c p) n -> p c n", p=128)[:, :, cs], "wg%d" % q)
                load("scalar", wu[:, :, cs], wu_b.rearrange("(c p) n -> p c n", p=128)[:, :, cs], "wu%d" % q)
            for c0 in range(0, NFC, 11):
                load("scalar", wd[:, c0:c0 + 11, :], wd_b.rearrange("(c p) n -> p c n", p=128)[:, c0:c0 + 11, :], "wd")

            NX4 = 4
            xs = [sb(ph, "x4_%d" % i, [128, D], F32) for i in range(NX4)]
            xr = Ring(NX4)
            cat = [sb(ph, "cat%d" % i, [128, 8, 256], BF16) for i in range(2)]
            h2 = [sb(ph, "h2_%d" % i, [128, D], BF16) for i in range(2)]
            junk = sb(ph, "junk4", [128, D], BF16)
            h2T = sb(ph, "h2T", [128, 8, 256], BF16)
            sgt = [sb(ph, "sgt%d" % i, [128, 256], BF16) for i in range(2)]
            st4 = sb(ph, "st4", [128, 16], F32)
            pO = [ps(ph, "pO%d" % i, [128, 512], F32) for i in range(2)]
            por = Ring(2)
            pTp = [ps(ph, "pTp%d" % i, [128, 1024], BF16) for i in range(2)]
            pGt = [ps(ph, "pGt%d" % i, [128, 512], F32) for i in range(2)]
            pUp = [ps(ph, "pUp%d" % i, [128, 512], F32) for i in range(2)]
            x_v = T["x"].rearrange("(t p) d -> t p d", p=128)
            o_v = out_d.rearrange("(t p) d -> t p d", p=128)
            cat_v = catT_d.rearrange("(c p) t -> p c t", p=128)
            NB = 32
            xis_of = {}

            def rms_rstd(xi, col):
                S.op("scalar", I("activation", out=junk[:], in_=xs[xi][:], func=AF.Square, accum_out=st4[:, col:col + 1]),
                     reads=["x4_%d" % xi], writes=["junk4", "st4_%d" % col])
                S.op("scalar", I("activation", out=st4[:, col + 1:col + 2], in_=st4[:, col:col + 1], func=AF.Sqrt, scale=1.0 / D, bias=epsb[:, 0:1]),
                     reads=["st4_%d" % col, "epsb"], writes=["st4_%d" % (col + 1)])
                S.op("vector", I("reciprocal", out=st4[:, col + 2:col + 3], in_=st4[:, col + 1:col + 2]),
                     reads=["st4_%d" % (col + 1)], writes=["st4_%d" % (col + 2)])

            def stA1(blk):
                ci_ = blk % 2
                tsl = slice(blk * 256, (blk + 1) * 256)
                load("sync", cat[ci_][:], cat_v[:, :, tsl], "cat%d" % ci_)
                xis = []
                for i in range(2):
                    xi = xr.next()
                    xis.append(xi)
                    load("sync", xs[xi][:], x_v[blk * 2 + i], "x4_%d" % xi)
                xis_of[blk] = xis
                for i in range(2):
                    xi = xis[i]
                    for n in range(2):
                        pi_ = por.next()
                        for c in range(8):
                            S.op("tensor", I("matmul", pO[pi_][:, :], lhsT=cat[ci_][:, c, i * 128:(i + 1) * 128],
                                             rhs=wout[:, c, n * 512:(n + 1) * 512], start=(c == 0), stop=(c == 7)),
                                 reads=["cat%d" % ci_, "wout"], writes=["pO%d" % pi_], accum=True)
                        S.op("vector", I("tensor_tensor", out=xs[xi][:, n * 512:(n + 1) * 512], in0=pO[pi_][:, :],
                                         in1=xs[xi][:, n * 512:(n + 1) * 512], op=ALU.add),
                             reads=["pO%d" % pi_, "x4_%d" % xi], writes=["x4_%d" % xi])
                    col = 4 * i
                    rms_rstd(xi, col)
                    S.op("gpsimd", I("tensor_scalar", out=h2[i][:], in0=xs[xi][:], scalar1=st4[:, col + 2:col + 3], scalar2=1.0,
                                     op0=ALU.mult, op1=ALU.mult),
                         reads=["x4_%d" % xi, "st4_%d" % (col + 2)], writes=["h2_%d" % i])

            def stA2(blk):
                for i in range(2):
                    for c in range(8):
                        S.op("tensor", I("transpose", out=pTp[i][:, c * 128:(c + 1) * 128], in_=h2[i][:, c * 128:(c + 1) * 128], identity=ident_bf[:]),
                             reads=["h2_%d" % i, "ident_bf"], writes=["pTp%d" % i], accum=True)
                    if i:
                        S.op("vector", I("tensor_copy", out=h2T[:, :, i * 128:(i + 1) * 128], in_=pTp[i][:].rearrange("p (c t) -> p c t", c=8)),
                             reads=["pTp%d" % i], writes=["h2T"])
                    else:
                        S.op("scalar", I("activation", out=h2T[:, :, i * 128:(i + 1) * 128], in_=pTp[i][:].rearrange("p (c t) -> p c t", c=8), func=AF.Copy),
                             reads=["pTp%d" % i], writes=["h2T"])

            def stB(blk):
                for fc in range(NFC):
                    gi = fc % 2
                    for c in range(8):
                        S.op("tensor", I("matmul", pGt[gi][:, 0:256], lhsT=wg[:, c, fc * 128:(fc + 1) * 128], rhs=h2T[:, c, :],
                                         start=(c == 0), stop=(c == 7)),
                             reads=["wg%d" % fq_of(fc), "h2T"], writes=["pGt%d" % gi], accum=True)
                    for c in range(8):
                        S.op("tensor", I("matmul", pUp[gi][:, 0:256], lhsT=wu[:, c, fc * 128:(fc + 1) * 128], rhs=h2T[:, c, :],
                                         start=(c == 0), stop=(c == 7)),
                             reads=["wu%d" % fq_of(fc), "h2T"], writes=["pUp%d" % gi], accum=True)
                    S.op("scalar", I("activation", out=sgt[gi][:], in_=pGt[gi][:, 0:256], func=AF.Silu),
                         reads=["pGt%d" % gi], writes=["sgt%d" % gi])
                    S.op("vector", I("tensor_tensor", out=aT[:, fc, :], in0=pUp[gi][:, 0:256], in1=sgt[gi][:], op=ALU.mult),
                         reads=["pUp%d" % gi, "sgt%d" % gi], writes=["aT"])

            def stC(blk):
                xis = xis_of[blk]
                for i in range(2):
                    xi = xis[i]
                    for n in range(2):
                        pi_ = por.next()
                        for fc in range(NFC):
                            S.op("tensor", I("matmul", pO[pi_][:, :], lhsT=aT[:, fc, i * 128:(i + 1) * 128],
                                             rhs=wd[:, fc, n * 512:(n + 1) * 512], start=(fc == 0), stop=(fc == NFC - 1)),
                                 reads=["aT", "wd"], writes=["pO%d" % pi_], accum=True)
                        S.op("vector", I("tensor_tensor", out=xs[xi][:, n * 512:(n + 1) * 512], in0=pO[pi_][:, :],
                                         in1=xs[xi][:, n * 512:(n + 1) * 512], op=ALU.add),
                             reads=["pO%d" % pi_, "x4_%d" % xi], writes=["x4_%d" % xi])
                    col = 8 + 4 * i
                    rms_rstd(xi, col)
                    S.op("vector", I("scalar_tensor_tensor", out=xs[xi][:], in0=xs[xi][:], scalar=st4[:, col + 2:col + 3], in1=gfin[:],
                                     op0=ALU.mult, op1=ALU.mult),
                         reads=["x4_%d" % xi, "st4_%d" % (col + 2), "gfin"], writes=["x4_%d" % xi])
                    S.op("sync", I("dma_start", out=o_v[blk * 2 + i], in_=xs[xi][:]),
                         reads=["x4_%d" % xi], writes=["out_d"], dma=True)

            stA1(0)
            stA2(0)
            for blk in range(NB):
                stB(blk)
                if blk + 1 < NB:
                    stA1(blk + 1)
                stC(blk)
                if blk + 1 < NB:
                    stA2(blk + 1)
            S.flush()
    return nc


def _layout_params(inp):
    f = lambda a: np.ascontiguousarray(np.asarray(a, dtype=np.float32))
    p = {}
    p["w_in"] = f(inp["w_in"][0])
    p["w_glu"] = f(inp["w_glu"][0])
    p["w_out"] = f(inp["w_out"][0])
    p["w_gate"] = f(inp["w_gate"][0])
    p["w_up"] = f(inp["w_up"][0])
    p["w_down"] = f(inp["w_down"][0])
    p["g_mix"] = f(np.asarray(inp["norm_mix_g"][0]).reshape(8, 128).T)
    p["g_ffn"] = f(np.asarray(inp["norm_ffn_g"][0]).reshape(8, 128).T)
    p["g_sub"] = f(np.asarray(inp["subln_g"][0]).reshape(128, 1))
    p["g_ssm"] = f(np.asarray(inp["ssm_norm_g"][0]).reshape(4, 128).T)
    p["b_glu"] = f(np.asarray(inp["b_glu"][0]).reshape(4, 128).T)
    p["g_fin"] = f(np.broadcast_to(np.asarray(inp["norm_final_g"]).reshape(1, D), (128, D)))
    p["lamv"] = f(np.concatenate([np.asarray(inp[k][0]).reshape(64) for k in
                                  ("lambda_q1", "lambda_k1", "lambda_q2", "lambda_k2")]).reshape(1, 256))
    rb = np.asarray(inp["rel_bias"])
    p["rel_bias"] = f(rb)
    p["rb31"] = f(rb[31].reshape(4, 1))
    aT = np.asarray(inp["A_re"][0]).T
    p["areT2"] = f(np.concatenate([aT, aT], 0))
    aT = np.asarray(inp["A_im"][0]).T
    p["aimT2"] = f(np.concatenate([aT, aT], 0))
    p["logdt2"] = f(np.broadcast_to(np.asarray(inp["log_dt"][0]).reshape(1, 32), (128, 32)))
    bre = np.asarray(inp["B_re"][0]).transpose(1, 0, 2).reshape(64, 512)
    bim = np.asarray(inp["B_im"][0]).transpose(1, 0, 2).reshape(64, 512)
    p["bx2"] = f(np.concatenate([bre, bim], 0))
    p["bsw2"] = f(np.concatenate([bim, bre], 0))
    cre = np.asarray(inp["C_re"][0]).transpose(2, 0, 1).reshape(64, 512)
    cim = np.asarray(inp["C_im"][0]).transpose(2, 0, 1).reshape(64, 512)
    p["ca"] = f(np.concatenate([cre, cim], 0))
    p["cb"] = f(np.concatenate([cim, cre], 0))
    dsk = np.asarray(inp["D_skip"][0])
    p["dvec"] = f(np.tile(dsk.T, (8, 1)))
    return p


_NC_CACHE = {}


def kernel(**inputs):
    x = np.asarray(inputs["x"], dtype=np.float32)
    shared = _layout_params(inputs)
    shared.update(_consts())
    if "nc" not in _NC_CACHE:
        _NC_CACHE["nc"] = build_nc()
    nc = _NC_CACHE["nc"]
    in_maps = [dict(shared, x=np.ascontiguousarray(x[b])) for b in range(8)]
    res = run_bass_kernel_spmd(nc, in_maps, core_ids=list(range(8)))
    return np.stack([np.asarray(r["out"], dtype=np.float32) for r in res.results], axis=0)
```

```python
import math
from contextlib import ExitStack

import numpy as np
import ml_dtypes
import concourse.bass as bass
import concourse.mybir as mybir
from concourse.bass_utils import run_bass_kernel_spmd

F32 = mybir.dt.float32
BF16 = mybir.dt.bfloat16
I32 = mybir.dt.int32
AF = mybir.ActivationFunctionType
ALU = mybir.AluOpType
AX = mybir.AxisListType

S_LEN = 8192
D = 1024
DFF = 2816
NFC = DFF // 128
EPS = 1e-6
TWO_PI_S = 6.2831850051879883
INV_2PI = 1.0 / (2.0 * math.pi)

ENGS = ("sync", "scalar", "vector", "gpsimd", "tensor")
N_DMA_SEMS = 40
N_HW_SEMS = 28


class Sched:
    def __init__(self, nc, stack):
        self.nc = nc
        self.esem = {e: stack.enter_context(nc.semaphore("s_" + e)) for e in ENGS}
        self.ecnt = {e: 0 for e in ENGS}
        self.dsem = [stack.enter_context(nc.semaphore("d%d" % i)) for i in range(N_DMA_SEMS)]
        self.dcnt = [0] * N_DMA_SEMS
        self.dnext = 0
        self.dnext_sw = 0
        self.dlast = [None] * N_DMA_SEMS
        self.waited = {e: {} for e in ENGS}
        self.ops = {e: [] for e in ENGS}
        self.lastw = {}
        self.readers = {}
        self.n_ops = 0

    def _need(self, eng, tok, waits):
        if tok is None:
            return
        semkey, sem, val = tok[0], tok[1], tok[2]
        w = self.waited[eng]
        if w.get(semkey, 0) >= val:
            return
        w[semkey] = val
        waits.append((sem, val))
        for k, v in tok[5].items():
            if w.get(k, 0) < v:
                w[k] = v

    def op(self, eng, fn, reads=(), writes=(), dma=False, accum=False):
        waits = []
        for k in reads:
            self._need(eng, self.lastw.get(k), waits)
        for k in writes:
            lw = self.lastw.get(k)
            if lw is not None and not (accum and eng == "tensor" and lw[3] == "tensor" and lw[4]):
                self._need(eng, lw, waits)
            for r in self.readers.get(k, ()):
                self._need(eng, r, waits)
        if dma:
            if eng == "gpsimd":
                i = N_HW_SEMS + self.dnext_sw
                self.dnext_sw = (self.dnext_sw + 1) % (N_DMA_SEMS - N_HW_SEMS)
            else:
                i = self.dnext
                self.dnext = (self.dnext + 1) % N_HW_SEMS
            self._need(eng, self.dlast[i], waits)
            self.dcnt[i] += 16
            tok = (("d", i), self.dsem[i], self.dcnt[i], "dma", False, dict(self.waited[eng]))
            self.dlast[i] = tok
            inc = (self.dsem[i], 16)
        else:
            self.ecnt[eng] += 1
            know = dict(self.waited[eng])
            know[("e", eng)] = self.ecnt[eng] - 1
            tok = (("e", eng), self.esem[eng], self.ecnt[eng], eng, accum, know)
            inc = (self.esem[eng], 1)
        self.ops[eng].append((waits, fn, inc))
        for k in reads:
            self.readers.setdefault(k, []).append(tok)
        for k in writes:
            self.lastw[k] = tok
            self.readers[k] = []
        self.n_ops += 1
        return tok

    def wait_all(self, eng, toks):
        waits = []
        for t in toks:
            self._need(eng, t, waits)
        if waits:
            self.ops[eng].append((waits, None, None))

    def flush(self):
        nc = self.nc
        self.wait_all("sync", [(("d", i), self.dsem[i], self.dcnt[i], "dma", False, {})
                               for i in range(N_DMA_SEMS) if self.dcnt[i]])
        ops = self.ops
        with nc.Block() as block:
            def mk(e):
                def body(engh):
                    for waits, fn, inc in ops[e]:
                        fuse = None
                        if fn is not None and waits and fn[0] != "dma_start" and "accum_out" not in fn[2]:
                            fuse = waits[-1]
                            waits = waits[:-1]
                        for sem, val in waits:
                            engh.wait_ge(sem, val)
                        if fn is not None:
                            name, a, kw = fn
                            ins = getattr(engh, name)(*a, **kw)
                            if fuse is not None:
                                ins._wait_ge(fuse[0], fuse[1])
                            ins.then_inc(inc[0], inc[1])
                return body
            for e in ENGS:
                if ops[e]:
                    getattr(block, e)(mk(e))
        self.ops = {e: [] for e in ENGS}
        self.lastw = {}
        self.readers = {}


def I(name, *a, **kw):
    return (name, a, kw)


class Ring:
    def __init__(self, n):
        self.n = n
        self.i = -1

    def next(self):
        self.i = (self.i + 1) % self.n
        return self.i


def _t5_bucket_np(n):
    n = np.asarray(n)
    nf = np.maximum(n, 1).astype(np.float32)
    large = 16 + (np.log(nf / np.float32(16)) / np.float32(math.log(8.0)) * np.float32(16)).astype(np.int32)
    large = np.minimum(large, 31)
    return np.where(n < 16, n, large)


def _consts():
    c = {}
    c["ident_bf"] = np.eye(128, dtype=np.float32).astype(ml_dtypes.bfloat16)
    c["ident_f"] = np.eye(128, dtype=np.float32)
    c["jrev"] = np.eye(128, dtype=np.float32)[::-1].copy()
    oh = np.zeros((32, 383), np.float32)
    mv = np.zeros((4, 383), np.float32)
    for i in range(383):
        n = i - 127
        if n >= 0:
            oh[int(_t5_bucket_np(n)), i] = 1.0
            mv[:, i] = 1.0
    c["oh"] = oh
    c["maskvec"] = mv
    sel = np.zeros((128, 64, 128), np.float32)
    for x_ in range(8):
        for y_ in range(8):
            for cc in range(16):
                sel[x_ * 16 + cc, x_ * 8 + y_, y_ * 16 + cc] = 1.0
    c["sel"] = sel.astype(ml_dtypes.bfloat16)
    jj = np.arange(128)[:, None] // 16
    ii = np.arange(128)[None, :] // 16
    c["tmask"] = (ii >= jj).astype(np.float32)
    c["kvec"] = np.broadcast_to(np.arange(1025, dtype=np.float32)[None, :], (128, 1025)).copy()
    sg = np.ones((128, 2), np.float32)
    sg[:64, 0] = -1.0
    sg[64:, 1] = -1.0
    c["sgn"] = sg
    mlist = [-j for j in range(8)] + [7 - j for j in range(8)] + list(range(9))
    c["mvals"] = np.broadcast_to(np.asarray(mlist, np.float32)[None, :], (128, len(mlist))).copy()
    return c


NM_ = 25
_CONST_SPECS = [
    ("ident_bf", [128, 128], BF16), ("ident_f", [128, 128], F32), ("jrev", [128, 128], F32),
    ("oh", [32, 383], F32), ("maskvec", [4, 383], F32), ("sel", [128, 64, 128], BF16),
    ("tmask", [128, 128], F32), ("kvec", [128, 1025], F32), ("sgn", [128, 2], F32), ("mvals", [128, NM_], F32),
]
_PARAM_SPECS = [
    ("x", [S_LEN, D], F32),
    ("w_in", [D, 2048], F32), ("w_glu", [512, 512], F32), ("w_out", [D, D], F32),
    ("w_gate", [D, DFF], F32), ("w_up", [D, DFF], F32), ("w_down", [DFF, D], F32),
    ("g_mix", [128, 8], F32), ("g_ffn", [128, 8], F32), ("g_sub", [128, 1], F32),
    ("g_ssm", [128, 4], F32), ("b_glu", [128, 4], F32), ("g_fin", [128, D], F32),
    ("lamv", [1, 256], F32), ("rel_bias", [32, 4], F32), ("rb31", [4, 1], F32),
    ("areT2", [128, 32], F32), ("aimT2", [128, 32], F32), ("logdt2", [128, 32], F32),
    ("bx2", [128, 512], F32), ("bsw2", [128, 512], F32), ("ca", [128, 512], F32), ("cb", [128, 512], F32),
    ("dvec", [128, 32], F32),
]

M_LIST = [-j for j in range(8)] + [7 - j for j in range(8)] + list(range(9))
NM = len(M_LIST)
IDX_L, IDX_B, IDX_P = 0, 8, 16


def build_nc(debug=False):
    nc = bass.Bass("TRN2", target_bir_lowering=False)
    IK = "ExternalOutput" if debug else "Internal"
    T = {}
    for name, shape, dt_ in _PARAM_SPECS + _CONST_SPECS:
        T[name] = nc.dram_tensor(name, shape, dt_, kind="ExternalInput").ap()
    out_d = nc.dram_tensor("out", [S_LEN, D], F32, kind="ExternalOutput").ap()
    qT_d = nc.dram_tensor("qT_d", [512, S_LEN], BF16, kind=IK).ap()
    kT_d = nc.dram_tensor("kT_d", [512, S_LEN], BF16, kind=IK).ap()
    v_d = nc.dram_tensor("v_d", [S_LEN, 512], BF16, kind=IK).ap()
    uT_d = nc.dram_tensor("uT_d", [512, 8, 1024], BF16, kind=IK).ap()
    gT_d = nc.dram_tensor("gT_d", [512, S_LEN], BF16, kind=IK).ap()
    catT_d = nc.dram_tensor("catT_d", [D, S_LEN], BF16, kind=IK).ap()
    gv_d = nc.dram_tensor("gv_d", [4, 383], F32, kind=IK).ap()
    tab_d = nc.dram_tensor("tab_d", [32, 2, 128, 1025], F32, kind="Internal").ap()
    wout_b = nc.dram_tensor("wout_b", [D, D], BF16, kind="Internal").ap()
    wg_b = nc.dram_tensor("wg_b", [D, DFF], BF16, kind="Internal").ap()
    wu_b = nc.dram_tensor("wu_b", [D, DFF], BF16, kind="Internal").ap()
    wd_b = nc.dram_tensor("wd_b", [DFF, D], BF16, kind="Internal").ap()

    with ExitStack() as top:
        S = Sched(nc, top)
        uq = [0]

        def sb(st, name, shape, dt_):
            uq[0] += 1
            return st.enter_context(nc.sbuf_tensor("sb%d_%s" % (uq[0], name), shape, dt_))

        def ps(st, name, shape, dt_):
            uq[0] += 1
            return st.enter_context(nc.psum_tensor("ps%d_%s" % (uq[0], name), shape, dt_))

        ident_bf = sb(top, "ident_bf", [128, 128], BF16)
        ident_f = sb(top, "ident_f", [128, 128], F32)
        ones_bf = sb(top, "ones_bf", [128, 128], BF16)
        epsb = sb(top, "epsb", [128, 1], F32)
        mid = ExitStack()
        neglam = sb(mid, "neglam", [128, 1], F32)
        E_all = sb(mid, "E_all", [128, 8, 128], BF16)
        W_BsT = sb(mid, "W_BsT", [128, 32, 128], BF16)
        W_BsTs = sb(mid, "W_BsTs", [128, 32, 128], BF16)
        W_CsA = sb(mid, "W_CsA", [128, 32, 128], BF16)
        W_CsB = sb(mid, "W_CsB", [128, 32, 128], BF16)
        W_T = sb(mid, "W_T", [128, 32, 128], BF16)
        rho2 = sb(mid, "rho2", [128, 32], F32)
        tau8 = sb(mid, "tau8", [128, 32], F32)
        mid2 = ExitStack()
        S1t = sb(mid2, "S1t", [128, 32, 33], F32)
        C1t = sb(mid2, "C1t", [128, 32, 33], F32)
        S0t = sb(mid2, "S0t", [128, 32, 32], F32)
        C0t = sb(mid2, "C0t", [128, 32, 32], F32)

        def load(eng, dst, src, key):
            return S.op(eng, I("dma_start", out=dst, in_=src), writes=[key], dma=True)

        with ExitStack() as ph:
            load("sync", ident_bf[:], T["ident_bf"], "ident_bf")
            load("sync", ident_f[:], T["ident_f"], "ident_f")
            S.op("vector", I("memset", ones_bf[:], 1.0), writes=["ones_bf"])
            S.op("vector", I("memset", epsb[:], EPS), writes=["epsb"])
            ones_f = sb(ph, "ones_f", [1, 128], F32)
            S.op("vector", I("memset", ones_f[:], 1.0), writes=["ones_f"])

            lamv = sb(ph, "lamv", [1, 256], F32)
            lprod = sb(ph, "lprod", [1, 128], F32)
            lsum = sb(ph, "lsum", [1, 2], F32)
            lexp = sb(ph, "lexp", [1, 2], F32)
            lval = sb(ph, "lval", [1, 1], F32)
            load("sync", lamv[:], T["lamv"], "lamv")
            lv = lamv[:].rearrange("p (a b c) -> p a b c", a=2, b=2)
            S.op("vector", I("tensor_tensor", out=lprod[:].rearrange("p (a c) -> p a c", a=2),
                                                     in0=lv[:, :, 0, :], in1=lv[:, :, 1, :], op=ALU.mult),
                 reads=["lamv"], writes=["lprod"])
            S.op("vector", I("tensor_reduce", out=lsum[:], in_=lprod[:].rearrange("p (a c) -> p a c", a=2),
                                                     axis=AX.X, op=ALU.add),
                 reads=["lprod"], writes=["lsum"])
            S.op("scalar", I("activation", out=lexp[:], in_=lsum[:], func=AF.Exp), reads=["lsum"], writes=["lexp"])
            S.op("vector", I("scalar_tensor_tensor", out=lval[:], in0=lexp[:, 1:2], scalar=-0.2, in1=lexp[:, 0:1],
                                                            op0=ALU.add, op1=ALU.subtract),
                 reads=["lexp"], writes=["lval"])
            pz = ps(ph, "pz0", [128, 512], F32)
            S.op("tensor", I("matmul", pz[:, 0:1], lhsT=ones_f[:], rhs=lval[:], start=True, stop=True),
                 reads=["ones_f", "lval"], writes=["pz0"])
            S.op("vector", I("tensor_copy", out=neglam[:], in_=pz[:, 0:1]), reads=["pz0"], writes=["neglam"])

            relb = sb(ph, "relb", [32, 4], F32)
            rb31 = sb(ph, "rb31", [4, 1], F32)
            nrb31 = sb(ph, "nrb31", [4, 1], F32)
            oh = sb(ph, "oh", [32, 383], F32)
            mvec = sb(ph, "mvec", [4, 383], F32)
            gvs = sb(ph, "gvs", [4, 383], F32)
            jrev = sb(ph, "jrev", [128, 128], F32)
            erev = sb(ph, "erev", [128, 8, 128], F32)
            load("sync", relb[:], T["rel_bias"], "relb")
            load("sync", rb31[:], T["rb31"], "rb31")
            load("sync", oh[:], T["oh"], "oh")
            load("sync", mvec[:], T["maskvec"], "mvec")
            load("sync", jrev[:], T["jrev"], "jrev")
            S.op("vector", I("tensor_scalar", out=nrb31[:], in0=rb31[:], scalar1=-1.0, scalar2=None, op0=ALU.mult),
                 reads=["rb31"], writes=["nrb31"])
            pz1 = ps(ph, "pz1", [128, 512], F32)
            S.op("tensor", I("matmul", pz1[0:4, 0:383], lhsT=relb[:], rhs=oh[:], start=True, stop=True),
                 reads=["relb", "oh"], writes=["pz1"])
            S.op("scalar", I("activation", out=gvs[:], in_=pz1[0:4, 0:383], func=AF.Exp, bias=nrb31[:, 0:1]),
                 reads=["pz1", "nrb31"], writes=["gvs"])
            S.op("vector", I("tensor_tensor", out=gvs[:], in0=gvs[:], in1=mvec[:], op=ALU.mult),
                 reads=["gvs", "mvec"], writes=["gvs"])
            S.op("gpsimd", I("dma_start", out=gv_d, in_=gvs[:]), reads=["gvs"], writes=["gv_d"], dma=True)
            for hh in range(4):
                for dd in range(2):
                    src = bass.AP(gv_d.tensor, hh * 383 + dd * 128, [[1, 128], [1, 128]])
                    S.op("sync", I("dma_start", out=erev[:, hh * 2 + dd, :], in_=src),
                         reads=["gv_d"], writes=["erev"], dma=True)
            pz2 = ps(ph, "pz2", [128, 512], F32)
            pz3 = ps(ph, "pz3", [128, 512], F32)
            erf = erev[:].rearrange("p a b -> p (a b)")
            eaf = E_all[:].rearrange("p a b -> p (a b)")
            for half, pzz in ((0, pz2), (1, pz3)):
                S.op("tensor", I("matmul", pzz[:, :], lhsT=jrev[:], rhs=erf[:, half * 512:(half + 1) * 512],
                                                                      start=True, stop=True),
                     reads=["jrev", "erev"], writes=["pzz%d" % half])
                S.op("vector", I("tensor_copy", out=eaf[:, half * 512:(half + 1) * 512], in_=pzz[:, :]),
                     reads=["pzz%d" % half], writes=["E_all"])

            areT = sb(ph, "areT", [128, 32], F32)
            aimT = sb(ph, "aimT", [128, 32], F32)
            ldt = sb(ph, "ldt", [128, 32], F32)
            sgn = sb(ph, "sgn", [128, 2], F32)
            tmask = sb(ph, "tmask", [128, 128], F32)
            dvec = sb(ph, "dvec", [128, 32], F32)
            bx2 = sb(ph, "bx2", [128, 32, 16], F32)
            bsw2 = sb(ph, "bsw2", [128, 32, 16], F32)
            ca = sb(ph, "ca", [128, 32, 16], F32)
            cbm = sb(ph, "cbm", [128, 32, 16], F32)
            load("sync", areT[:], T["areT2"], "areT")
            load("sync", aimT[:], T["aimT2"], "aimT")
            load("sync", ldt[:], T["logdt2"], "ldt")
            load("sync", sgn[:], T["sgn"], "sgn")
            load("sync", tmask[:], T["tmask"], "tmask")
            load("sync", dvec[:], T["dvec"], "dvec")
            load("sync", bx2[:].rearrange("p g c -> p (g c)"), T["bx2"], "bx2")
            load("sync", bsw2[:].rearrange("p g c -> p (g c)"), T["bsw2"], "bsw2")
            load("sync", ca[:].rearrange("p g c -> p (g c)"), T["ca"], "ca")
            load("sync", cbm[:].rearrange("p g c -> p (g c)"), T["cb"], "cbm")

            dtt = sb(ph, "dtt", [128, 32], F32)
            ar = sb(ph, "ar", [128, 32], F32)
            tau = sb(ph, "tau", [128, 32], F32)
            den = sb(ph, "den", [128, 32], F32)
            tmpa = sb(ph, "tmpa", [128, 32], F32)
            rden = sb(ph, "rden", [128, 32], F32)
            V = "vector"
            S.op("scalar", I("activation", out=dtt[:], in_=ldt[:], func=AF.Exp), reads=["ldt"], writes=["dtt"])
            S.op(V, I("tensor_tensor", out=ar[:], in0=areT[:], in1=dtt[:], op=ALU.mult), reads=["areT", "dtt"], writes=["ar"])
            S.op(V, I("scalar_tensor_tensor", out=tau[:], in0=aimT[:], scalar=INV_2PI, in1=dtt[:], op0=ALU.mult, op1=ALU.mult),
                 reads=["aimT", "dtt"], writes=["tau"])
            S.op(V, I("tensor_scalar", out=tau8[:], in0=tau[:], scalar1=8.0, scalar2=None, op0=ALU.mult),
                 reads=["tau"], writes=["tau8"])
            S.op("scalar", I("activation", out=rho2[:], in_=ar[:], func=AF.Exp, scale=8.0), reads=["ar"], writes=["rho2"])
            kv0 = sb(ph, "kv0", [128, 1025], F32)
            load("sync", kv0[:], T["kvec"], "kv0")
            RIs = sb(ph, "RIs", [128, 32, 33], I32)
            t8b33 = tau8[:].unsqueeze(2).broadcast_to([128, 32, 33])
            t8b32 = tau8[:].unsqueeze(2).broadcast_to([128, 32, 32])
            k1v = kv0[:, 0:1025:32].unsqueeze(1).broadcast_to([128, 32, 33])
            k0v = kv0[:, 0:32].unsqueeze(1).broadcast_to([128, 32, 32])
            for (dst_, kv_, tb_, n_, off_, key_) in ((S1t, k1v, t8b33, 33, 0.0, "S1t"), (C1t, k1v, t8b33, 33, 0.25, "C1t"),
                                                     (S0t, k0v, t8b32, 32, 0.0, "S0t"), (C0t, k0v, t8b32, 32, 0.25, "C0t")):
                S.op(V, I("tensor_tensor", out=dst_[:], in0=tb_, in1=kv_, op=ALU.mult), reads=["tau8", "kv0"], writes=[key_])
                if off_:
                    S.op(V, I("tensor_scalar", out=dst_[:], in0=dst_[:], scalar1=off_, scalar2=None, op0=ALU.add), reads=[key_], writes=[key_])
                S.op(V, I("tensor_copy", out=RIs[:, :, 0:n_], in_=dst_[:]), reads=[key_], writes=["RIs"])
                S.op(V, I("tensor_tensor", out=dst_[:], in0=dst_[:], in1=RIs[:, :, 0:n_], op=ALU.subtract), reads=[key_, "RIs"], writes=[key_])
                S.op("scalar", I("activation", out=dst_[:], in_=dst_[:], func=AF.Sin, scale=TWO_PI_S), reads=[key_], writes=[key_])
            S.op(V, I("tensor_tensor", out=den[:], in0=areT[:], in1=areT[:], op=ALU.mult), reads=["areT"], writes=["den"])
            S.op(V, I("tensor_tensor", out=tmpa[:], in0=aimT[:], in1=aimT[:], op=ALU.mult), reads=["aimT"], writes=["tmpa"])
            S.op(V, I("tensor_tensor", out=den[:], in0=den[:], in1=tmpa[:], op=ALU.add), reads=["den", "tmpa"], writes=["den"])
            S.op(V, I("reciprocal", out=rden[:], in_=den[:]), reads=["den"], writes=["rden"])

            MAG = sb(ph, "MAG", [128, NM, 32], F32)
            AS_ = sb(ph, "AS_", [128, NM, 32], F32)
            AC_ = sb(ph, "AC_", [128, NM, 32], F32)
            RI = sb(ph, "RI", [128, NM, 32], I32)
            SINT = sb(ph, "SINT", [128, NM, 32], F32)
            COST = sb(ph, "COST", [128, NM, 32], F32)
            PRE = sb(ph, "PRE", [128, NM, 32], F32)
            PIM = sb(ph, "PIM", [128, NM, 32], F32)
            mvt = sb(ph, "mvt", [128, NM], F32)
            load("sync", mvt[:], T["mvals"], "mvt")
            mb_ = mvt[:].unsqueeze(2).broadcast_to([128, NM, 32])
            S.op(V, I("tensor_tensor", out=AS_[:], in0=tau[:].unsqueeze(1).broadcast_to([128, NM, 32]), in1=mb_, op=ALU.mult),
                 reads=["tau", "mvt"], writes=["AS_"])
            S.op(V, I("tensor_scalar", out=AC_[:], in0=AS_[:], scalar1=0.25, scalar2=None, op0=ALU.add), reads=["AS_"], writes=["AC_"])
            S.op(V, I("tensor_tensor", out=MAG[:], in0=ar[:].unsqueeze(1).broadcast_to([128, NM, 32]), in1=mb_, op=ALU.mult),
                 reads=["ar", "mvt"], writes=["MAG"])
            S.op("scalar", I("activation", out=MAG[:], in_=MAG[:], func=AF.Exp), reads=["MAG"], writes=["MAG"])
            fl = lambda t: t[:].rearrange("p a b -> p (a b)")
            for A_, OUT, key in ((AS_, SINT, "SINT"), (AC_, COST, "COST")):
                akey = "AS_" if A_ is AS_ else "AC_"
                S.op(V, I("tensor_copy", out=fl(RI), in_=fl(A_)), reads=[akey], writes=["RI"])
                S.op(V, I("tensor_tensor", out=fl(A_), in0=fl(A_), in1=fl(RI), op=ALU.subtract),
                     reads=[akey, "RI"], writes=[akey])
                S.op("scalar", I("activation", out=fl(OUT), in_=fl(A_), func=AF.Sin, scale=TWO_PI_S),
                     reads=[akey], writes=[key])
            S.op(V, I("tensor_tensor", out=fl(PRE), in0=fl(MAG), in1=fl(COST), op=ALU.mult), reads=["MAG", "COST"], writes=["PRE"])
            S.op(V, I("tensor_tensor", out=fl(PIM), in0=fl(MAG), in1=fl(SINT), op=ALU.mult), reads=["MAG", "SINT"], writes=["PIM"])
            i1 = IDX_P + 1
            t0 = sb(ph, "c_t0", [128, 32], F32)
            t1 = sb(ph, "c_t1", [128, 32], F32)
            t2 = sb(ph, "c_t2", [128, 32], F32)
            cre = sb(ph, "cre", [128, 32], F32)
            cim = sb(ph, "cim", [128, 32], F32)
            S.op(V, I("tensor_scalar", out=t0[:], in0=PRE[:, i1, :], scalar1=-1.0, scalar2=None, op0=ALU.add), reads=["PRE"], writes=["c_t0"])
            S.op(V, I("tensor_tensor", out=t1[:], in0=t0[:], in1=areT[:], op=ALU.mult), reads=["c_t0", "areT"], writes=["c_t1"])
            S.op(V, I("tensor_tensor", out=t2[:], in0=PIM[:, i1, :], in1=aimT[:], op=ALU.mult), reads=["PIM", "aimT"], writes=["c_t2"])
            S.op(V, I("tensor_tensor", out=t1[:], in0=t1[:], in1=t2[:], op=ALU.add), reads=["c_t1", "c_t2"], writes=["c_t1"])
            S.op(V, I("tensor_tensor", out=cre[:], in0=t1[:], in1=rden[:], op=ALU.mult), reads=["c_t1", "rden"], writes=["cre"])
            S.op(V, I("tensor_tensor", out=t1[:], in0=PIM[:, i1, :], in1=areT[:], op=ALU.mult), reads=["PIM", "areT", "cre"], writes=["c_t1"])
            S.op(V, I("tensor_tensor", out=t2[:], in0=t0[:], in1=aimT[:], op=ALU.mult), reads=["c_t0", "aimT"], writes=["c_t2"])
            S.op(V, I("tensor_tensor", out=t1[:], in0=t1[:], in1=t2[:], op=ALU.subtract), reads=["c_t1", "c_t2"], writes=["c_t1"])
            S.op(V, I("tensor_tensor", out=cim[:], in0=t1[:], in1=rden[:], op=ALU.mult), reads=["c_t1", "rden"], writes=["cim"])
            QRE = sb(ph, "QRE", [128, 16, 32], F32)
            QIM = sb(ph, "QIM", [128, 16, 32], F32)
            QT1 = sb(ph, "QT1", [128, 16, 32], F32)
            cre_b = cre[:].unsqueeze(1).broadcast_to([128, 16, 32])
            cim_b = cim[:].unsqueeze(1).broadcast_to([128, 16, 32])
            S.op(V, I("tensor_tensor", out=QRE[:], in0=PRE[:, 0:16, :], in1=cre_b, op=ALU.mult), reads=["PRE", "cre"], writes=["QRE"])
            S.op(V, I("tensor_tensor", out=QT1[:], in0=PIM[:, 0:16, :], in1=cim_b, op=ALU.mult), reads=["PIM", "cim"], writes=["QT1"])
            S.op(V, I("tensor_tensor", out=QRE[:], in0=QRE[:], in1=QT1[:], op=ALU.subtract), reads=["QRE", "QT1"], writes=["QRE"])
            S.op(V, I("tensor_tensor", out=QIM[:], in0=PRE[:, 0:16, :], in1=cim_b, op=ALU.mult), reads=["PRE", "cim"], writes=["QIM"])
            S.op(V, I("tensor_tensor", out=QT1[:], in0=PIM[:, 0:16, :], in1=cre_b, op=ALU.mult), reads=["PIM", "cre", "QRE"], writes=["QT1"])
            S.op(V, I("tensor_tensor", out=QIM[:], in0=QIM[:], in1=QT1[:], op=ALU.add), reads=["QIM", "QT1"], writes=["QIM"])
            sT = sgn[:, 0:1]
            sB = sgn[:, 1:2]
            QIMsT = sb(ph, "QIMsT", [128, 16, 32], F32)
            QREsB = sb(ph, "QREsB", [128, 16, 32], F32)
            PREsB = sb(ph, "PREsB", [128, NM, 32], F32)
            PIMsT = sb(ph, "PIMsT", [128, NM, 32], F32)
            PREn = sb(ph, "PREn", [128, NM, 32], F32)
            PIMn = sb(ph, "PIMn", [128, NM, 32], F32)
            for (o_, i_, sc_, k_o, k_i) in ((QIMsT, QIM, sT, "QIMsT", "QIM"), (QREsB, QRE, sB, "QREsB", "QRE"),
                                            (PREsB, PRE, sB, "PREsB", "PRE"), (PIMsT, PIM, sT, "PIMsT", "PIM"),
                                            (PREn, PRE, -1.0, "PREn", "PRE"), (PIMn, PIM, -1.0, "PIMn", "PIM")):
                S.op(V, I("tensor_scalar", out=fl(o_), in0=fl(i_), scalar1=sc_, scalar2=None, op0=ALU.mult),
                     reads=[k_i, "sgn"], writes=[k_o])

            BIGA = sb(ph, "BIGA", [128, 32, 8, 16], F32)
            BIGB = sb(ph, "BIGB", [128, 32, 8, 16], F32)
            BIGT = sb(ph, "BIGT", [128, 32, 8, 16], F32)

            def big(out_t, okey, A, akey, a0, X, xkey, Bq, bkey, Y, ykey, eng):
                a_b = A[:, a0:a0 + 8, :].rearrange("p j g -> p g j").unsqueeze(3).broadcast_to([128, 32, 8, 16])
                b_b = Bq[:, a0:a0 + 8, :].rearrange("p j g -> p g j").unsqueeze(3).broadcast_to([128, 32, 8, 16])
                x_b = X[:].unsqueeze(2).broadcast_to([128, 32, 8, 16])
                y_b = Y[:].unsqueeze(2).broadcast_to([128, 32, 8, 16])
                for g0 in range(0, 32, 8):
                    gs = slice(g0, g0 + 8)
                    qk = "%s_q%d" % (okey, g0)
                    S.op(eng, I("tensor_tensor", out=out_t[:, gs], in0=a_b[:, gs], in1=x_b[:, gs], op=ALU.mult),
                         reads=[akey, xkey], writes=[qk, okey])
                    S.op(eng, I("tensor_tensor", out=BIGT[:, gs], in0=b_b[:, gs], in1=y_b[:, gs], op=ALU.mult),
                         reads=[bkey, ykey], writes=["BIGT_q%d" % g0])
                    S.op(eng, I("tensor_tensor", out=out_t[:, gs], in0=out_t[:, gs], in1=BIGT[:, gs], op=ALU.add),
                         reads=[qk, "BIGT_q%d" % g0], writes=[qk, okey])

            pT = [ps(ph, "pT%d" % i, [128, 512], F32) for i in range(4)]
            ptr = Ring(4)

            big(BIGA, "BIGA", PREsB, "PREsB", IDX_P + 1, ca, "ca", PIMn, "PIMn", cbm, "cbm", V)
            S.op("scalar", I("activation", out=W_CsA[:].rearrange("p g m -> p (g m)"), in_=BIGA[:].rearrange("p g j c -> p (g j c)"),
                                                  func=AF.Copy), reads=["BIGA"], writes=["W_CsA"])
            big(BIGB, "BIGB", PIMsT, "PIMsT", IDX_P + 1, ca, "ca", PREn, "PREn", cbm, "cbm", V)
            S.op("scalar", I("activation", out=W_CsB[:].rearrange("p g m -> p (g m)"), in_=BIGB[:].rearrange("p g j c -> p (g j c)"),
                                                  func=AF.Copy), reads=["BIGB"], writes=["W_CsB"])
            big(BIGA, "BIGA", QRE, "QRE", IDX_B, bx2, "bx2", QIMsT, "QIMsT", bsw2, "bsw2", V)
            for (src_t, skey, dst) in ((BIGA, "BIGA", W_BsT),):
                for g in range(32):
                    pi_ = ptr.next()
                    S.op("tensor", I("transpose", out=pT[pi_][:, 0:128],
                                                                                   in_=src_t[:, g].rearrange("p j c -> p (j c)"),
                                                                                   identity=ident_f[:]),
                         reads=[skey, "ident_f"], writes=["pT%d" % pi_])
                    if g % 2:
                        S.op("scalar", I("activation", out=dst[:, g, :], in_=pT[pi_][:, 0:128], func=AF.Copy),
                             reads=["pT%d" % pi_], writes=["W_BsT"])
                    else:
                        S.op("vector", I("tensor_copy", out=dst[:, g, :], in_=pT[pi_][:, 0:128]),
                             reads=["pT%d" % pi_], writes=["W_BsT"])
            big(BIGB, "BIGB", QREsB, "QREsB", IDX_B, bsw2, "bsw2", QIM, "QIM", bx2, "bx2", V)
            for g in range(32):
                pi_ = ptr.next()
                S.op("tensor", I("transpose", out=pT[pi_][:, 0:128], in_=BIGB[:, g].rearrange("p j c -> p (j c)"),
                                                                  identity=ident_f[:]),
                     reads=["BIGB", "ident_f"], writes=["pT%d" % pi_])
                if g % 2:
                    S.op("scalar", I("activation", out=W_BsTs[:, g, :], in_=pT[pi_][:, 0:128], func=AF.Copy),
                         reads=["pT%d" % pi_], writes=["W_BsTs"])
                else:
                    S.op("vector", I("tensor_copy", out=W_BsTs[:, g, :], in_=pT[pi_][:, 0:128]),
                         reads=["pT%d" % pi_], writes=["W_BsTs"])
            big(BIGA, "BIGA", QRE, "QRE", IDX_L, bx2, "bx2", QIMsT, "QIMsT", bsw2, "bsw2", V)
            big(BIGB, "BIGB", PREsB, "PREsB", IDX_P, ca, "ca", PIMn, "PIMn", cbm, "cbm", V)
            ttmp = sb(ph, "ttmp", [128, 2, 128], F32)
            tr2 = Ring(2)
            for g in range(32):
                pi_ = ptr.next()
                ti = tr2.next()
                S.op("tensor", I("matmul", pT[pi_][:, 0:128], lhsT=BIGA[:, g].rearrange("p j c -> p (j c)"),
                                                               rhs=BIGB[:, g].rearrange("p j c -> p (j c)"), start=True, stop=True),
                     reads=["BIGA", "BIGB"], writes=["pT%d" % pi_])
                S.op(V, I("tensor_tensor", out=ttmp[:, ti, :], in0=pT[pi_][:, 0:128], in1=tmask[:], op=ALU.mult),
                     reads=["pT%d" % pi_, "tmask"], writes=["ttmp%d" % ti])
                S.op(V, I("scalar_tensor_tensor", out=W_T[:, g, :], in0=ident_f[:], scalar=dvec[:, g:g + 1],
                                                                     in1=ttmp[:, ti, :], op0=ALU.mult, op1=ALU.add),
                     reads=["ident_f", "dvec", "ttmp%d" % ti], writes=["W_T"])
            S.flush()

        with ExitStack() as ph:
            win = sb(ph, "win", [128, 8, 2048], BF16)
            gmix = sb(ph, "gmix", [128, 8], F32)
            stg = [sb(ph, "stg%d" % i, [128, 2048], F32) for i in range(2)]
            load("sync", gmix[:], T["g_mix"], "gmix")
            w_in_v = T["w_in"].rearrange("(c p) n -> c p n", p=128)
            for c in range(8):
                si = c % 2
                load("sync", stg[si][:], w_in_v[c], "stg%d" % si)
                S.op("gpsimd" if c % 2 else "vector",
                     I("tensor_scalar", out=win[:, c, :], in0=stg[si][:], scalar1=gmix[:, c:c + 1], scalar2=1.0,
                                                           op0=ALU.mult, op1=ALU.mult),
                     reads=["stg%d" % si, "gmix"], writes=["win"])
            NXS = 6
            xs = [sb(ph, "xs%d" % i, [128, D], F32) for i in range(NXS)]
            xring = Ring(NXS)
            hb = [sb(ph, "hb%d" % i, [128, D], BF16) for i in range(8)]
            hring = Ring(8)
            junk = sb(ph, "junk", [128, D], BF16)
            ssq = sb(ph, "ssq", [128, 8], F32)
            sring = Ring(8)
            sdv = sb(ph, "sdv", [128, 8], F32)
            rsd = sb(ph, "rsd", [128, 8], F32)
            hT = [sb(ph, "hT%d" % i, [128, 8, 512], BF16) for i in range(2)]
            ost = [sb(ph, "ost%d" % i, [128, 512], BF16) for i in range(6)]
            oring = Ring(6)
            ptp = [ps(ph, "ptp%d" % i, [128, 1024], BF16) for i in range(2)]
            tpr = Ring(2)
            pmm = [ps(ph, "pmm%d" % i, [128, 512], F32) for i in range(4)]
            mring = Ring(4)
            x_v = T["x"].rearrange("(t p) d -> t p d", p=128)
            ev = [0]

            def evac(dst, src, rkeys, wkeys):
                ev[0] += 1
                if ev[0] % 2:
                    S.op("vector", I("tensor_copy", out=dst, in_=src), reads=rkeys, writes=wkeys)
                    return "vector"
                S.op("scalar", I("activation", out=dst, in_=src, func=AF.Copy), reads=rkeys, writes=wkeys)
                return "scalar"


            hb_of = {}

            def prepA(blk):
                his = []
                for i in range(4):
                    tix = blk * 4 + i
                    xi = xring.next()
                    load("sync", xs[xi][:], x_v[tix], "xs%d" % xi)
                    si = sring.next()
                    S.op("scalar", I("activation", out=junk[:], in_=xs[xi][:], func=AF.Square, accum_out=ssq[:, si:si + 1]),
                         reads=["xs%d" % xi], writes=["junk", "ssq%d" % si])
                    S.op("scalar", I("activation", out=sdv[:, si:si + 1], in_=ssq[:, si:si + 1], func=AF.Sqrt, scale=1.0 / D, bias=epsb[:, 0:1]),
                         reads=["ssq%d" % si, "epsb"], writes=["sdv%d" % si])
                    S.op("vector", I("reciprocal", out=rsd[:, si:si + 1], in_=sdv[:, si:si + 1]),
                         reads=["sdv%d" % si], writes=["rsd%d" % si])
                    hi = hring.next()
                    his.append(hi)
                    S.op("gpsimd", I("tensor_scalar", out=hb[hi][:], in0=xs[xi][:], scalar1=rsd[:, si:si + 1], scalar2=1.0,
                                     op0=ALU.mult, op1=ALU.mult),
                         reads=["xs%d" % xi, "rsd%d" % si], writes=["hb%d" % hi])
                hb_of[blk] = his

            def prepB(blk):
                hTi = blk % 2
                for i in range(4):
                    hi = hb_of[blk][i]
                    ti = tpr.next()
                    for c in range(8):
                        S.op("tensor", I("transpose", out=ptp[ti][:, c * 128:(c + 1) * 128], in_=hb[hi][:, c * 128:(c + 1) * 128],
                                         identity=ident_bf[:]),
                             reads=["hb%d" % hi, "ident_bf"], writes=["ptp%d" % ti], accum=True)
                    evac(hT[hTi][:, :, i * 128:(i + 1) * 128], ptp[ti][:].rearrange("p (c t) -> p c t", c=8),
                         ["ptp%d" % ti], ["hT%d" % hTi])

            prepA(0)
            prepB(0)
            prepA(1)
            for blk in range(16):
                hTi = blk % 2
                if blk + 2 < 16:
                    prepA(blk + 2)
                tsl = slice(blk * 512, (blk + 1) * 512)
                for oc in range(12):
                    col0 = oc * 128 if oc < 8 else 1536 + (oc - 8) * 128
                    mi = mring.next()
                    for c in range(8):
                        S.op("tensor", I("matmul", pmm[mi][:, :], lhsT=win[:, c, col0:col0 + 128],
                                                                                 rhs=hT[hTi][:, c, :], start=(c == 0), stop=(c == 7)),
                             reads=["win", "hT%d" % hTi], writes=["pmm%d" % mi], accum=True)
                    oi = oring.next()
                    if oc < 8:
                        se = evac(ost[oi][:], pmm[mi][:, :], ["pmm%d" % mi], ["ost%d" % oi])
                    else:
                        se = evac(ost[oi][:].rearrange("p (j k) -> p j k", j=8), pmm[mi][:, :].rearrange("p (k j) -> p j k", j=8),
                                  ["pmm%d" % mi], ["ost%d" % oi])
                    if oc < 4:
                        dst = qT_d[oc * 128:(oc + 1) * 128, tsl]
                        dk = "qT_d"
                    elif oc < 8:
                        dst = kT_d[(oc - 4) * 128:(oc - 3) * 128, tsl]
                        dk = "kT_d"
                    else:
                        dst = uT_d[(oc - 8) * 128:(oc - 7) * 128, :, blk * 64:(blk + 1) * 64]
                        dk = "uT_d"
                    src_ = ost[oi][:] if oc < 8 else ost[oi][:].rearrange("p (j k) -> p j k", j=8)
                    S.op("sync" if se == "vector" else se, I("dma_start", out=dst, in_=src_), reads=["ost%d" % oi], writes=[dk], dma=True)
                for i in range(4):
                    mi = mring.next()
                    for c in range(8):
                        S.op("tensor", I("matmul", pmm[mi][:, :], lhsT=hT[hTi][:, c, i * 128:(i + 1) * 128],
                                                                           rhs=win[:, c, 1024:1536], start=(c == 0), stop=(c == 7)),
                             reads=["win", "hT%d" % hTi], writes=["pmm%d" % mi], accum=True)
                    oi = oring.next()
                    se = evac(ost[oi][:], pmm[mi][:, :], ["pmm%d" % mi], ["ost%d" % oi])
                    r0 = blk * 512 + i * 128
                    S.op("sync" if se == "vector" else se, I("dma_start", out=v_d[r0:r0 + 128, :], in_=ost[oi][:]),
                         reads=["ost%d" % oi], writes=["v_d"], dma=True)
                if blk + 1 < 16:
                    prepB(blk + 1)
            S.flush()

        with ExitStack() as ph:
            KT = [sb(ph, "KT%d" % i, [128, S_LEN], BF16) for i in range(2)]
            QT = [sb(ph, "QT%d" % i, [128, S_LEN], BF16) for i in range(2)]
            VA = [sb(ph, "VA%d" % i, [128, 64, 130], BF16) for i in range(2)]
            NPT = 4
            PT = [[sb(ph, "PT%d_%d" % (c, i), [128, 512], BF16) for i in range(NPT)] for c in range(2)]
            ptr_ = Ring(NPT)
            SP = [[ps(ph, "SP%d_%d" % (c, i), [128, 512], F32) for i in range(2)] for c in range(2)]
            OA = ps(ph, "OA", [128, 512], F32)
            OB = ps(ph, "OB", [128, 512], F32)
            OC = ps(ph, "OC", [128, 512], F32)
            PTR = ps(ph, "PTR", [128, 1024], BF16)
            rc = sb(ph, "rc", [128, 8], F32)
            o2 = sb(ph, "o2", [128, 128], F32)
            oo4 = sb(ph, "oo4", [128, 4, 128], F32)
            ojunk = sb(ph, "ojunk", [128, 128], F32)
            ass = sb(ph, "ass", [128, 8], F32)
            attb2 = [sb(ph, "attb%d" % i, [128, 128], BF16) for i in range(2)]
            aTs = [sb(ph, "aTs%d" % i, [128, 512], BF16) for i in range(2)]
            for i in range(2):
                S.op("vector", I("memset", VA[i][:, :, 128:130], 1.0), writes=["VA%d" % i])

            def acc_ap(c, r):
                if r < 3:
                    return (OA if c == 0 else OB)[:, r * 129:(r + 1) * 129], ("OA" if c == 0 else "OB")
                return OC[:, c * 129:(c + 1) * 129], "OC"

            v_v = v_d.rearrange("(t p) d -> p t d", p=128)
            gffn = sb(ph, "gffn", [128, 8], F32)
            gsub = sb(ph, "gsub", [128, 1], F32)
            gssm = sb(ph, "gssm", [128, 4], F32)
            gout = sb(ph, "gout", [128, 8], F32)
            load("sync", gffn[:], T["g_ffn"], "gffn")
            load("sync", gsub[:], T["g_sub"], "gsub")
            load("sync", gssm[:], T["g_ssm"], "gssm")
            for c in range(4):
                S.op("vector", I("tensor_scalar", out=gout[:, c:c + 1], in0=gsub[:], scalar1=0.8, scalar2=None, op0=ALU.mult),
                     reads=["gsub"], writes=["gout"])
            S.op("vector", I("tensor_copy", out=gout[:, 4:8], in_=gssm[:]), reads=["gssm"], writes=["gout"])
            cst = [sb(ph, "cst%d" % i, [128, DFF], F32) for i in range(1)]
            cob = [sb(ph, "cob%d" % i, [128, DFF], BF16) for i in range(1)]
            osin = sb(ph, "osin", [128, 1025], F32)
            ocos = sb(ph, "ocos", [128, 1025], F32)
            tmpA = sb(ph, "tmpA", [128, 32, 32], F32)

            def gen_tab(g):
                s1 = S1t[:, g, 0:32].unsqueeze(2).broadcast_to([128, 32, 32])
                c1 = C1t[:, g, 0:32].unsqueeze(2).broadcast_to([128, 32, 32])
                s0 = S0t[:, g, :].unsqueeze(1).broadcast_to([128, 32, 32])
                c0 = C0t[:, g, :].unsqueeze(1).broadcast_to([128, 32, 32])
                osv = osin[:, 0:1024].rearrange("p (a b) -> p a b", b=32)
                ocv = ocos[:, 0:1024].rearrange("p (a b) -> p a b", b=32)
                P_ = "gpsimd"
                rk = ["S1t", "C1t", "S0t", "C0t"]
                S.op(P_, I("tensor_tensor", out=osv, in0=s1, in1=c0, op=ALU.mult), reads=rk, writes=["osin"])
                S.op(P_, I("tensor_tensor", out=tmpA[:], in0=c1, in1=s0, op=ALU.mult), reads=rk, writes=["tmpA"])
                S.op(P_, I("tensor_tensor", out=osv, in0=osv, in1=tmpA[:], op=ALU.add), reads=["osin", "tmpA"], writes=["osin"])
                S.op(P_, I("tensor_copy", out=osin[:, 1024:1025], in_=S1t[:, g, 32:33]), reads=rk, writes=["osin"])
                S.op(P_, I("dma_start", out=tab_d[g, 0], in_=osin[:]), reads=["osin"], writes=["tab_d"], dma=True)
                S.op(P_, I("tensor_tensor", out=ocv, in0=c1, in1=c0, op=ALU.mult), reads=rk, writes=["ocos"])
                S.op(P_, I("tensor_tensor", out=tmpA[:], in0=s1, in1=s0, op=ALU.mult), reads=rk, writes=["tmpA"])
                S.op(P_, I("tensor_tensor", out=ocv, in0=ocv, in1=tmpA[:], op=ALU.subtract), reads=["ocos", "tmpA"], writes=["ocos"])
                S.op(P_, I("tensor_copy", out=ocos[:, 1024:1025], in_=C1t[:, g, 32:33]), reads=rk, writes=["ocos"])
                S.op(P_, I("dma_start", out=tab_d[g, 1], in_=ocos[:]), reads=["ocos"], writes=["tab_d"], dma=True)

            cjobs = []
            for (src, dst, ncols, gain, nch) in ((T["w_out"], wout_b, D, gout, 8), (T["w_gate"], wg_b, DFF, gffn, 8),
                                                 (T["w_up"], wu_b, DFF, gffn, 8), (T["w_down"], wd_b, D, None, NFC)):
                sv = src.rearrange("(c p) n -> c p n", p=128)
                dv = dst.rearrange("(c p) n -> c p n", p=128)
                for c in range(nch):
                    cjobs.append((sv[c], dv[c], ncols, gain, c))
            cj = [0]

            def conv_job():
                if cj[0] >= len(cjobs):
                    return
                src, dst, ncols, gain, c = cjobs[cj[0]]
                k = 0
                cj[0] += 1
                load("sync", cst[k][:, 0:ncols], src, "cst%d" % k)
                if gain is None:
                    S.op("gpsimd", I("tensor_copy", out=cob[k][:, 0:ncols], in_=cst[k][:, 0:ncols]), reads=["cst%d" % k], writes=["cob%d" % k])
                else:
                    S.op("gpsimd", I("tensor_scalar", out=cob[k][:, 0:ncols], in0=cst[k][:, 0:ncols], scalar1=gain[:, c:c + 1], scalar2=1.0,
                                     op0=ALU.mult, op1=ALU.mult), reads=["cst%d" % k, "gout", "gffn"], writes=["cob%d" % k])
                S.op("gpsimd", I("dma_start", out=dst, in_=cob[k][:, 0:ncols]), reads=["cob%d" % k], writes=["wconv_d"], dma=True)

            def head_loads(h):
                bi = h % 2
                load("sync", KT[bi][:], kT_d[h * 128:(h + 1) * 128, :], "KT%d" % bi)
                load("sync", QT[bi][:], qT_d[h * 128:(h + 1) * 128, :], "QT%d" % bi)
                for t0 in range(0, 64, 16):
                    S.op("sync", I("dma_start", out=VA[bi][:, t0:t0 + 16, 0:128],
                                                                          in_=v_v[:, t0:t0 + 16, h * 128:(h + 1) * 128]),
                         reads=["v_d"], writes=["VA%d" % bi], dma=True)

            head_loads(0)
            for h in range(4):
                bi = h % 2
                if h + 1 < 4:
                    head_loads(h + 1)
                kq = ["KT%d" % bi, "QT%d" % bi]
                iters = [(jb, kt) for jb in range(16) for kt in range(4 * jb + 4)]

                def emit_qk(jb, kt, slot):
                    m = kt - 4 * jb
                    c0 = 128 * max(m, 0)
                    for c in range(2):
                        S.op("tensor", I("matmul", SP[c][slot][:, c0:512], lhsT=KT[bi][c * 64:(c + 1) * 64, kt * 128:(kt + 1) * 128],
                            rhs=QT[bi][c * 64:(c + 1) * 64, jb * 512 + c0:(jb + 1) * 512], start=True, stop=True),
                             reads=kq, writes=["SP%d_%d" % (c, slot)])

                emit_qk(iters[0][0], iters[0][1], 0)
                started = set()
                for it, (jb, kt) in enumerate(iters):
                    slot = it % 2
                    if it % 40 == 20:
                        conv_job()
                    gi_ = h * len(iters) + it
                    if gi_ % 68 == 10:
                        gen_tab(gi_ // 68)
                    if it + 1 < len(iters):
                        emit_qk(iters[it + 1][0], iters[it + 1][1], (it + 1) % 2)
                    m = kt - 4 * jb
                    c0 = 128 * max(m, 0)
                    pi_ = ptr_.next()
                    for c in range(2):
                        S.op("scalar", I("activation", out=PT[c][pi_][:, c0:512], in_=SP[c][slot][:, c0:512], func=AF.Exp, scale=0.125),
                             reads=["SP%d_%d" % (c, slot)], writes=["PT%d_%d" % (c, pi_)])
                    for r in range(4):
                        dl = 4 * jb + r - kt
                        if dl in (0, 1):
                            for c in range(2):
                                S.op("vector", I("tensor_tensor", out=PT[c][pi_][:, r * 128:(r + 1) * 128], in0=PT[c][pi_][:, r * 128:(r + 1) * 128],
                                                 in1=E_all[:, h * 2 + dl, :], op=ALU.mult),
                                     reads=["PT%d_%d" % (c, pi_), "E_all"], writes=["PT%d_%d" % (c, pi_)])
                    if kt == 0:
                        started = set()
                    for c in range(2):
                        for r in range(max(m, 0), 4):
                            ap_, key = acc_ap(c, r)
                            first = key not in started
                            started.add(key)
                            S.op("tensor", I("matmul", ap_, lhsT=PT[c][pi_][:, r * 128:(r + 1) * 128], rhs=VA[bi][:, kt, 0:129],
                                start=first, stop=(kt == 4 * jb + r), skip_group_check=True),
                                 reads=["PT%d_%d" % (c, pi_), "VA%d" % bi], writes=[key], accum=True)
                    if kt == 4 * jb + 3:
                        asi = jb % 2
                        for r in range(4):
                            a1, k1 = acc_ap(0, r)
                            a2, k2 = acc_ap(1, r)
                            S.op("vector", I("reciprocal", out=rc[:, 0:1], in_=a1[:, 128:129]), reads=[k1], writes=["rc0"])
                            S.op("vector", I("reciprocal", out=rc[:, 1:2], in_=a2[:, 128:129]), reads=[k2], writes=["rc1"])
                            S.op("vector", I("tensor_tensor", out=rc[:, 2:3], in0=rc[:, 1:2], in1=neglam[:], op=ALU.mult),
                                 reads=["rc1", "neglam"], writes=["rc2"])
                            S.op("vector", I("tensor_scalar", out=o2[:], in0=a2[:, 0:128], scalar1=rc[:, 2:3], scalar2=None, op0=ALU.mult),
                                 reads=[k2, "rc2"], writes=["o2"])
                            S.op("vector", I("scalar_tensor_tensor", out=oo4[:, r, :], in0=a1[:, 0:128], scalar=rc[:, 0:1], in1=o2[:],
                                             op0=ALU.mult, op1=ALU.add),
                                 reads=[k1, "rc0", "o2"], writes=["oo%d" % r])
                            S.op("vector", I("scalar_tensor_tensor", out=ojunk[:], in0=oo4[:, r, :], scalar=1.0, in1=oo4[:, r, :],
                                             op0=ALU.mult, op1=ALU.mult, accum_out=ass[:, r:r + 1]),
                                 reads=["oo%d" % r], writes=["ojunk", "ass_s%d" % r])
                        S.op("scalar", I("activation", out=ass[:, 4:8], in_=ass[:, 0:4], func=AF.Ln, scale=1.0 / 128, bias=epsb[:, 0:1]),
                             reads=["ass_s0", "ass_s1", "ass_s2", "ass_s3", "epsb"], writes=["ass_l"])
                        S.op("scalar", I("activation", out=rc[:, 4:8], in_=ass[:, 4:8], func=AF.Exp, scale=-0.5), reads=["ass_l"], writes=["rc_r"])
                        for r in range(4):
                            S.op("vector", I("tensor_scalar", out=attb2[r % 2][:], in0=oo4[:, r, :], scalar1=rc[:, 4 + r:5 + r], scalar2=None, op0=ALU.mult),
                                 reads=["oo%d" % r, "rc_r"], writes=["attb%d" % (r % 2)])
                            S.op("tensor", I("transpose", out=PTR[:, r * 128:(r + 1) * 128], in_=attb2[r % 2][:], identity=ident_bf[:]),
                                 reads=["attb%d" % (r % 2), "ident_bf"], writes=["PTR"], accum=True)
                        S.op("vector", I("tensor_copy", out=aTs[asi][:], in_=PTR[:, 0:512]), reads=["PTR"], writes=["aTs%d" % asi])
                        S.op("gpsimd", I("dma_start", out=catT_d[h * 128:(h + 1) * 128, jb * 512:(jb + 1) * 512],
                                                                                  in_=aTs[asi][:]),
                             reads=["aTs%d" % asi], writes=["catT_d"], dma=True)
            while cj[0] < len(cjobs):
                conv_job()
            S.flush()

        mid2.close()
        with ExitStack() as ph:
            sel = sb(ph, "sel", [128, 64, 128], BF16)
            load("sync", sel[:].rearrange("p a b -> p (a b)"), T["sel"].rearrange("p a b -> p (a b)"), "sel")
            uT = sb(ph, "uT", [128, 8, 1024], BF16)
            Gs = [sb(ph, "Gs%d" % i, [128, 1024], BF16) for i in range(8)]
            gTn = sb(ph, "gTn", [128, S_LEN], BF16)
            NTB = 3
            COS = [sb(ph, "COS%d" % i, [128, 1025], F32) for i in range(NTB)]
            SIN = [sb(ph, "SIN%d" % i, [128, 1025], F32) for i in range(NTB)]
            Sb = [sb(ph, "Sb%d" % i, [128, 1025], F32) for i in range(NTB)]
            t1b = [sb(ph, "t1b%d" % i, [128, 512], F32) for i in range(2)]
            t2b = [sb(ph, "t2b%d" % i, [128, 512], F32) for i in range(2)]
            vmb = [sb(ph, "vmb%d" % i, [128, 512], F32) for i in range(3)]
            wcb = [sb(ph, "wcb%d" % i, [128, 512], BF16) for i in range(3)]
            wsb = [sb(ph, "wsb%d" % i, [128, 512], BF16) for i in range(3)]
            usb = [sb(ph, "usb%d" % i, [128, 512], BF16) for i in range(4)]
            pU = [ps(ph, "pU%d" % i, [128, 512], F32) for i in range(2)]
            pV = [ps(ph, "pV%d" % i, [128, 512], F32) for i in range(2)]
            pVs = [ps(ph, "pVs%d" % i, [128, 512], F32) for i in range(2)]
            pY = [ps(ph, "pY%d" % i, [128, 512], F32) for i in range(2)]
            for i in range(NTB):
                S.op("vector", I("memset", Sb[i][:, 0:1], 0.0), writes=["Sb%d_0" % i])
            NU = 64
            pur = Ring(2)

            def tables(g):
                tb = g % NTB
                load("sync", SIN[tb][:], tab_d[g, 0], "SIN%d" % tb)
                load("sync", COS[tb][:], tab_d[g, 1], "COS%d" % tb)

            def stA(u):
                g, hf = u // 2, u % 2
                cc, g8 = g // 8, g % 8
                if u % 16 == 0:
                    load("sync", uT[:], uT_d[cc * 128:(cc + 1) * 128, :, :], "uT")
                if hf == 0:
                    tables(g)
                pi_ = pur.next()
                for j in range(8):
                    S.op("tensor", I("matmul", pU[pi_][:, :], lhsT=sel[:, g8 * 8 + j, :],
                                     rhs=uT[:, j, hf * 512:(hf + 1) * 512],
                                     start=(j == 0), stop=(j == 7)),
                         reads=["sel", "uT"], writes=["pU%d" % pi_], accum=True)
                S.op("scalar", I("activation", out=usb[u % 4][:], in_=pU[pi_][:, :], func=AF.Copy), reads=["pU%d" % pi_], writes=["usb%d" % (u % 4)])

            def stB(u):
                g, hf = u // 2, u % 2
                tb, k0, p2, uu = g % NTB, hf * 512, u % 2, "usb%d" % (u % 4)
                S.op("tensor", I("matmul", pV[p2][:, :], lhsT=W_BsT[:, g, :], rhs=usb[u % 4][:], start=True, stop=True),
                     reads=["W_BsT", uu], writes=["pV%d" % p2])
                S.op("tensor", I("matmul", pVs[p2][:, :], lhsT=W_BsTs[:, g, :], rhs=usb[u % 4][:], start=True, stop=True),
                     reads=["W_BsTs", uu], writes=["pVs%d" % p2])
                S.op("vector", I("tensor_tensor", out=t1b[p2][:], in0=pV[p2][:, :], in1=COS[tb][:, 1 + k0:513 + k0], op=ALU.mult),
                     reads=["pV%d" % p2, "COS%d" % tb], writes=["t1b%d" % p2])
                S.op("vector", I("tensor_tensor", out=t2b[p2][:], in0=pVs[p2][:, :], in1=SIN[tb][:, 1 + k0:513 + k0], op=ALU.mult),
                     reads=["pVs%d" % p2, "SIN%d" % tb], writes=["t2b%d" % p2])
                S.op("gpsimd", I("tensor_tensor", out=vmb[u % 3][:], in0=t1b[p2][:], in1=t2b[p2][:], op=ALU.add),
                     reads=["t1b%d" % p2, "t2b%d" % p2], writes=["vmb%d" % (u % 3)])

            def stC(u):
                g, hf = u // 2, u % 2
                tb, k0, p3 = g % NTB, hf * 512, u % 3
                prevk = "Sb%d_%d" % (tb, hf)
                S.op("vector", I("tensor_tensor_scan", out=Sb[tb][:, 1 + k0:513 + k0], data0=rho2[:, g:g + 1].broadcast_to([128, 512]),
                                 data1=vmb[p3][:], initial=Sb[tb][:, k0:k0 + 1], op0=ALU.mult, op1=ALU.add),
                     reads=["rho2", "vmb%d" % p3, prevk], writes=["Sb%d_%d" % (tb, hf + 1)])
                rk = ["Sb%d_%d" % (tb, hf + 1), prevk]
                S.op("gpsimd", I("tensor_tensor", out=wcb[p3][:], in0=Sb[tb][:, k0:k0 + 512], in1=COS[tb][:, k0:k0 + 512], op=ALU.mult),
                     reads=rk + ["COS%d" % tb], writes=["wcb%d" % p3])
                S.op("vector", I("tensor_tensor", out=wsb[p3][:], in0=Sb[tb][:, k0:k0 + 512], in1=SIN[tb][:, k0:k0 + 512], op=ALU.mult),
                     reads=rk + ["SIN%d" % tb], writes=["wsb%d" % p3])

            def stD(u):
                g, hf = u // 2, u % 2
                g8, k0, p2, p3, uu = g % 8, hf * 512, u % 2, u % 3, "usb%d" % (u % 4)
                S.op("tensor", I("matmul", pY[p2][:, :], lhsT=W_T[:, g, :], rhs=usb[u % 4][:], start=True, stop=False),
                     reads=["W_T", uu], writes=["pY%d" % p2], accum=True)
                S.op("tensor", I("matmul", pY[p2][:, :], lhsT=W_CsA[:, g, :], rhs=wcb[p3][:], start=False, stop=False),
                     reads=["W_CsA", "wcb%d" % p3], writes=["pY%d" % p2], accum=True)
                S.op("tensor", I("matmul", pY[p2][:, :], lhsT=W_CsB[:, g, :], rhs=wsb[p3][:], start=False, stop=True),
                     reads=["W_CsB", "wsb%d" % p3], writes=["pY%d" % p2], accum=True)
                S.op("scalar", I("activation", out=Gs[g8][:, k0:k0 + 512], in_=pY[p2][:, :], func=AF.Gelu_apprx_tanh),
                     reads=["pY%d" % p2], writes=["Gs%d_%d" % (g8, hf)])

            def unshuffle(cc):
                for hf in range(2):
                    for i in range(8):
                        pi_ = pur.next()
                        for g8 in range(8):
                            S.op("tensor", I("matmul", pU[pi_][:, :], lhsT=sel[:, i * 8 + g8, :], rhs=Gs[g8][:, hf * 512:(hf + 1) * 512],
                                             start=(g8 == 0), stop=(g8 == 7)),
                                 reads=["sel", "Gs%d_%d" % (g8, hf)], writes=["pU%d" % pi_], accum=True)
                        dst = gTn[:, hf * 4096:(hf + 1) * 4096].rearrange("p (k j) -> p j k", j=8)[:, i, :]
                        if i % 2:
                            S.op("scalar", I("activation", out=dst, in_=pU[pi_][:, :], func=AF.Copy), reads=["pU%d" % pi_], writes=["gTn"])
                        else:
                            S.op("vector", I("tensor_copy", out=dst, in_=pU[pi_][:, :]), reads=["pU%d" % pi_], writes=["gTn"])
                S.op("gpsimd", I("dma_start", out=gT_d[cc * 128:(cc + 1) * 128, :], in_=gTn[:]), reads=["gTn"], writes=["gT_d"], dma=True)

            for step in range(NU + 3):
                if 0 <= step - 3 < NU:
                    stD(step - 3)
                    if (step - 3) % 16 == 15:
                        unshuffle((step - 3) // 16)
                if 0 <= step - 2 < NU:
                    stC(step - 2)
                if 0 <= step - 1 < NU:
                    stB(step - 1)
                if step < NU:
                    stA(step)
            S.flush()

        with ExitStack() as ph:
            wglu = sb(ph, "wglu", [128, 4, 512], BF16)
            bglu = sb(ph, "bglu", [128, 4], F32)
            stg = sb(ph, "stgg", [128, 4, 512], F32)
            load("sync", bglu[:], T["b_glu"], "bglu")
            load("sync", stg[:], T["w_glu"].rearrange("(c p) n -> p c n", p=128), "stgg")
            S.op("vector", I("tensor_copy", out=wglu[:], in_=stg[:]), reads=["stgg"], writes=["wglu"])
            gb = [sb(ph, "gb%d" % i, [128, 4, 512], BF16) for i in range(2)]
            sg = [sb(ph, "sg%d" % i, [128, 512], BF16) for i in range(2)]
            spre = sb(ph, "spre", [128, 4, 512], BF16)
            sq = sb(ph, "sq", [128, 4, 512], BF16)
            sdt = sb(ph, "sdt", [128, 512], F32)
            rst = sb(ph, "rst", [128, 512], F32)
            sso = [sb(ph, "sso%d" % i, [128, 4, 512], BF16) for i in range(2)]
            pG = [ps(ph, "pG%d" % i, [128, 512], F32) for i in range(2)]
            pS = ps(ph, "pS", [128, 512], F32)
            gT_v = gT_d.rearrange("(c p) t -> p c t", p=128)
            cat_v = catT_d[512:1024, :].rearrange("(c p) t -> p c t", p=128)
            for blk in range(16):
                bi = blk % 2
                tsl = slice(blk * 512, (blk + 1) * 512)
                load("sync", gb[bi][:], gT_v[:, :, tsl], "gb%d" % bi)
                for co in range(4):
                    pi_ = co % 2
                    for ci in range(4):
                        S.op("tensor", I("matmul", pG[pi_][:, :], lhsT=wglu[:, ci, co * 128:(co + 1) * 128],
                                                                                        rhs=gb[bi][:, ci, :], start=(ci == 0), stop=(ci == 3)),
                             reads=["wglu", "gb%d" % bi], writes=["pG%d" % pi_], accum=True)
                    S.op("scalar", I("activation", out=sg[pi_][:], in_=pG[pi_][:, :], func=AF.Sigmoid, bias=bglu[:, co:co + 1]),
                         reads=["pG%d" % pi_, "bglu"], writes=["sg%d" % pi_])
                    S.op("vector", I("tensor_tensor", out=spre[:, co, :], in0=gb[bi][:, co, :], in1=sg[pi_][:], op=ALU.mult),
                         reads=["gb%d" % bi, "sg%d" % pi_], writes=["spre%d" % co])
                    S.op("vector", I("tensor_tensor", out=sq[:, co, :], in0=spre[:, co, :], in1=spre[:, co, :], op=ALU.mult),
                         reads=["spre%d" % co], writes=["sq%d" % co])
                for co in range(4):
                    S.op("tensor", I("matmul", pS[:, :], lhsT=ones_bf[:], rhs=sq[:, co, :], start=(co == 0), stop=(co == 3)),
                         reads=["ones_bf", "sq%d" % co], writes=["pS"], accum=True)
                S.op("scalar", I("activation", out=sdt[:], in_=pS[:, :], func=AF.Ln, scale=1.0 / 512, bias=epsb[:, 0:1]),
                     reads=["pS", "epsb"], writes=["sdt"])
                S.op("scalar", I("activation", out=rst[:], in_=sdt[:], func=AF.Exp, scale=-0.5), reads=["sdt"], writes=["rst"])
                for co in range(4):
                    S.op("gpsimd" if co % 2 else "vector",
                         I("tensor_tensor", out=sso[bi][:, co, :], in0=spre[:, co, :], in1=rst[:], op=ALU.mult),
                         reads=["spre%d" % co, "rst"], writes=["sso%d" % bi])
                S.op("gpsimd", I("dma_start", out=cat_v[:, :, tsl], in_=sso[bi][:]),
                     reads=["sso%d" % bi], writes=["catT_d"], dma=True)
            S.flush()

        mid.close()
        with ExitStack() as ph:
            wout = sb(ph, "wout", [128, 8, D], BF16)
            wg = sb(ph, "wg", [128, 8, DFF], BF16)
            wu = sb(ph, "wu", [128, 8, DFF], BF16)
            wd = sb(ph, "wd", [128, NFC, D], BF16)
            gfin = sb(ph, "gfin", [128, D], F32)
            aT = sb(ph, "aT", [128, NFC, 256], BF16)
            load("sync", gfin[:], T["g_fin"], "gfin")
            load("sync", wout[:], wout_b.rearrange("(c p) n -> p c n", p=128), "wout")
            FQ = [0, 6, 12, 17, 22]
            fq_of = lambda fc: max(q for q in range(4) if FQ[q] <= fc)
            for q in range(4):
                cs = slice(FQ[q] * 128, FQ[q + 1] * 128)
                load("scalar", wg[:, :, cs], wg_b.rearrange("(c p) n -> p c n", p=128)[:, :, cs], "wg%d" % q)
                load("scalar", wu[:, :, cs], wu_b.rearrange("(c p) n -> p c n", p=128)[:, :, cs], "wu%d" % q)
            for c0 in range(0, NFC, 11):
                load("scalar", wd[:, c0:c0 + 11, :], wd_b.rearrange("(c p) n -> p c n", p=128)[:, c0:c0 + 11, :], "wd")

            NX4 = 4
            xs = [sb(ph, "x4_%d" % i, [128, D], F32) for i in range(NX4)]
            xr = Ring(NX4)
            cat = [sb(ph, "cat%d" % i, [128, 8, 256], BF16) for i in range(2)]
            h2 = [sb(ph, "h2_%d" % i, [128, D], BF16) for i in range(2)]
            junk = sb(ph, "junk4", [128, D], BF16)
            h2T = sb(ph, "h2T", [128, 8, 256], BF16)
            sgt = [sb(ph, "sgt%d" % i, [128, 256], BF16) for i in range(2)]
            st4 = sb(ph, "st4", [128, 16], F32)
            pO = [ps(ph, "pO%d" % i, [128, 512], F32) for i in range(2)]
            por = Ring(2)
            pTp = [ps(ph, "pTp%d" % i, [128, 1024], BF16) for i in range(2)]
            pGt = [ps(ph, "pGt%d" % i, [128, 512], F32) for i in range(2)]
            pUp = [ps(ph, "pUp%d" % i, [128, 512], F32) for i in range(2)]
            x_v = T["x"].rearrange("(t p) d -> t p d", p=128)
            o_v = out_d.rearrange("(t p) d -> t p d", p=128)
            cat_v = catT_d.rearrange("(c p) t -> p c t", p=128)
            NB = 32
            xis_of = {}

            def rms_rstd(xi, col):
                S.op("scalar", I("activation", out=junk[:], in_=xs[xi][:], func=AF.Square, accum_out=st4[:, col:col + 1]),
                     reads=["x4_%d" % xi], writes=["junk4", "st4_%d" % col])
                S.op("scalar", I("activation", out=st4[:, col + 1:col + 2], in_=st4[:, col:col + 1], func=AF.Sqrt, scale=1.0 / D, bias=epsb[:, 0:1]),
                     reads=["st4_%d" % col, "epsb"], writes=["st4_%d" % (col + 1)])
                S.op("vector", I("reciprocal", out=st4[:, col + 2:col + 3], in_=st4[:, col + 1:col + 2]),
                     reads=["st4_%d" % (col + 1)], writes=["st4_%d" % (col + 2)])

            def stA1(blk):
                ci_ = blk % 2
                tsl = slice(blk * 256, (blk + 1) * 256)
                load("sync", cat[ci_][:], cat_v[:, :, tsl], "cat%d" % ci_)
                xis = []
                for i in range(2):
                    xi = xr.next()
                    xis.append(xi)
                    load("sync", xs[xi][:], x_v[blk * 2 + i], "x4_%d" % xi)
                xis_of[blk] = xis
                for i in range(2):
                    xi = xis[i]
                    for n in range(2):
                        pi_ = por.next()
                        for c in range(8):
                            S.op("tensor", I("matmul", pO[pi_][:, :], lhsT=cat[ci_][:, c, i * 128:(i + 1) * 128],
                                             rhs=wout[:, c, n * 512:(n + 1) * 512], start=(c == 0), stop=(c == 7)),
                                 reads=["cat%d" % ci_, "wout"], writes=["pO%d" % pi_], accum=True)
                        S.op("vector", I("tensor_tensor", out=xs[xi][:, n * 512:(n + 1) * 512], in0=pO[pi_][:, :],
                                         in1=xs[xi][:, n * 512:(n + 1) * 512], op=ALU.add),
                             reads=["pO%d" % pi_, "x4_%d" % xi], writes=["x4_%d" % xi])
                    col = 4 * i
                    rms_rstd(xi, col)
                    S.op("gpsimd", I("tensor_scalar", out=h2[i][:], in0=xs[xi][:], scalar1=st4[:, col + 2:col + 3], scalar2=1.0,
                                     op0=ALU.mult, op1=ALU.mult),
                         reads=["x4_%d" % xi, "st4_%d" % (col + 2)], writes=["h2_%d" % i])

            def stA2(blk):
                for i in range(2):
                    for c in range(8):
                        S.op("tensor", I("transpose", out=pTp[i][:, c * 128:(c + 1) * 128], in_=h2[i][:, c * 128:(c + 1) * 128], identity=ident_bf[:]),
                             reads=["h2_%d" % i, "ident_bf"], writes=["pTp%d" % i], accum=True)
                    if i:
                        S.op("vector", I("tensor_copy", out=h2T[:, :, i * 128:(i + 1) * 128], in_=pTp[i][:].rearrange("p (c t) -> p c t", c=8)),
                             reads=["pTp%d" % i], writes=["h2T"])
                    else:
                        S.op("scalar", I("activation", out=h2T[:, :, i * 128:(i + 1) * 128], in_=pTp[i][:].rearrange("p (c t) -> p c t", c=8), func=AF.Copy),
                             reads=["pTp%d" % i], writes=["h2T"])

            def stB(blk):
                for fc in range(NFC):
                    gi = fc % 2
                    for c in range(8):
                        S.op("tensor", I("matmul", pGt[gi][:, 0:256], lhsT=wg[:, c, fc * 128:(fc + 1) * 128], rhs=h2T[:, c, :],
                                         start=(c == 0), stop=(c == 7)),
                             reads=["wg%d" % fq_of(fc), "h2T"], writes=["pGt%d" % gi], accum=True)
                    for c in range(8):
                        S.op("tensor", I("matmul", pUp[gi][:, 0:256], lhsT=wu[:, c, fc * 128:(fc + 1) * 128], rhs=h2T[:, c, :],
                                         start=(c == 0), stop=(c == 7)),
                             reads=["wu%d" % fq_of(fc), "h2T"], writes=["pUp%d" % gi], accum=True)
                    S.op("scalar", I("activation", out=sgt[gi][:], in_=pGt[gi][:, 0:256], func=AF.Silu),
                         reads=["pGt%d" % gi], writes=["sgt%d" % gi])
                    S.op("vector", I("tensor_tensor", out=aT[:, fc, :], in0=pUp[gi][:, 0:256], in1=sgt[gi][:], op=ALU.mult),
                         reads=["pUp%d" % gi, "sgt%d" % gi], writes=["aT"])

            def stC(blk):
                xis = xis_of[blk]
                for i in range(2):
                    xi = xis[i]
                    for n in range(2):
                        pi_ = por.next()
                        for fc in range(NFC):
                            S.op("tensor", I("matmul", pO[pi_][:, :], lhsT=aT[:, fc, i * 128:(i + 1) * 128],
                                             rhs=wd[:, fc, n * 512:(n + 1) * 512], start=(fc == 0), stop=(fc == NFC - 1)),
                                 reads=["aT", "wd"], writes=["pO%d" % pi_], accum=True)
                        S.op("vector", I("tensor_tensor", out=xs[xi][:, n * 512:(n + 1) * 512], in0=pO[pi_][:, :],
                                         in1=xs[xi][:, n * 512:(n + 1) * 512], op=ALU.add),
                             reads=["pO%d" % pi_, "x4_%d" % xi], writes=["x4_%d" % xi])
                    col = 8 + 4 * i
                    rms_rstd(xi, col)
                    S.op("vector", I("scalar_tensor_tensor", out=xs[xi][:], in0=xs[xi][:], scalar=st4[:, col + 2:col + 3], in1=gfin[:],
                                     op0=ALU.mult, op1=ALU.mult),
                         reads=["x4_%d" % xi, "st4_%d" % (col + 2), "gfin"], writes=["x4_%d" % xi])
                    S.op("sync", I("dma_start", out=o_v[blk * 2 + i], in_=xs[xi][:]),
                         reads=["x4_%d" % xi], writes=["out_d"], dma=True)

            stA1(0)
            stA2(0)
            for blk in range(NB):
                stB(blk)
                if blk + 1 < NB:
                    stA1(blk + 1)
                stC(blk)
                if blk + 1 < NB:
                    stA2(blk + 1)
            S.flush()
    return nc


def _layout_params(inp):
    f = lambda a: np.ascontiguousarray(np.asarray(a, dtype=np.float32))
    p = {}
    p["w_in"] = f(inp["w_in"][0])
    p["w_glu"] = f(inp["w_glu"][0])
    p["w_out"] = f(inp["w_out"][0])
    p["w_gate"] = f(inp["w_gate"][0])
    p["w_up"] = f(inp["w_up"][0])
    p["w_down"] = f(inp["w_down"][0])
    p["g_mix"] = f(np.asarray(inp["norm_mix_g"][0]).reshape(8, 128).T)
    p["g_ffn"] = f(np.asarray(inp["norm_ffn_g"][0]).reshape(8, 128).T)
    p["g_sub"] = f(np.asarray(inp["subln_g"][0]).reshape(128, 1))
    p["g_ssm"] = f(np.asarray(inp["ssm_norm_g"][0]).reshape(4, 128).T)
    p["b_glu"] = f(np.asarray(inp["b_glu"][0]).reshape(4, 128).T)
    p["g_fin"] = f(np.broadcast_to(np.asarray(inp["norm_final_g"]).reshape(1, D), (128, D)))
    p["lamv"] = f(np.concatenate([np.asarray(inp[k][0]).reshape(64) for k in
                                  ("lambda_q1", "lambda_k1", "lambda_q2", "lambda_k2")]).reshape(1, 256))
    rb = np.asarray(inp["rel_bias"])
    p["rel_bias"] = f(rb)
    p["rb31"] = f(rb[31].reshape(4, 1))
    aT = np.asarray(inp["A_re"][0]).T
    p["areT2"] = f(np.concatenate([aT, aT], 0))
    aT = np.asarray(inp["A_im"][0]).T
    p["aimT2"] = f(np.concatenate([aT, aT], 0))
    p["logdt2"] = f(np.broadcast_to(np.asarray(inp["log_dt"][0]).reshape(1, 32), (128, 32)))
    bre = np.asarray(inp["B_re"][0]).transpose(1, 0, 2).reshape(64, 512)
    bim = np.asarray(inp["B_im"][0]).transpose(1, 0, 2).reshape(64, 512)
    p["bx2"] = f(np.concatenate([bre, bim], 0))
    p["bsw2"] = f(np.concatenate([bim, bre], 0))
    cre = np.asarray(inp["C_re"][0]).transpose(2, 0, 1).reshape(64, 512)
    cim = np.asarray(inp["C_im"][0]).transpose(2, 0, 1).reshape(64, 512)
    p["ca"] = f(np.concatenate([cre, cim], 0))
    p["cb"] = f(np.concatenate([cim, cre], 0))
    dsk = np.asarray(inp["D_skip"][0])
    p["dvec"] = f(np.tile(dsk.T, (8, 1)))
    return p


_NC_CACHE = {}


def kernel(**inputs):
    x = np.asarray(inputs["x"], dtype=np.float32)
    shared = _layout_params(inputs)
    shared.update(_consts())
    if "nc" not in _NC_CACHE:
        _NC_CACHE["nc"] = build_nc()
    nc = _NC_CACHE["nc"]
    in_maps = [dict(shared, x=np.ascontiguousarray(x[b])) for b in range(8)]
    res = run_bass_kernel_spmd(nc, in_maps, core_ids=list(range(8)))
    return np.stack([np.asarray(r["out"], dtype=np.float32) for r in res.results], axis=0)
```
